# Optimizing a Trainium2 kernel written in Bass

```python
import math
import jax, jax.numpy as jnp
from jax import lax
import numpy as np

D_MODEL = 1024
BATCH = 4
SEQ = 8192
DEPTH = 2

GRID_W = 64
PLE_DIM = 256
N_EVEN = (DEPTH + 1) // 2
N_ODD = DEPTH // 2
EPS = 1e-6

RWKV_HEADS = 8
RWKV_HEAD_DIM = 64
RWKV_WIDTH = RWKV_HEADS * RWKV_HEAD_DIM
DECAY_RANK = 64
ICLR_RANK = 64
GATE_RANK = 128
RWKV_SHIFT_COLS = 3 * RWKV_WIDTH + DECAY_RANK + ICLR_RANK
RWKV_COLS = RWKV_SHIFT_COLS + GATE_RANK
RWKV_GN_EPS = 64e-5

HYENA_WIDTH = D_MODEL - RWKV_WIDTH
HYENA_COLS = 3 * HYENA_WIDTH
SHORT_CONV = 3
FILTER_EMB = 17
FILTER_HIDDEN = 64
DECAY_MIN = math.log(1e-2) / 1.5
DECAY_MAX = math.log(1e-2) / 0.3

IN_COLS = RWKV_COLS + HYENA_COLS

NA_HEADS = 16
NA_HEAD_DIM = D_MODEL // NA_HEADS
NA_WIN_ROWS_MAX = 8
NA_WIN_COLS = 16

D_FF = 2816
FFN_CONV = 3

kernel_name = "hybrid_rwkv7_hyena_natten_encoder"


def rmsnorm(x, g):
    xf = x.astype(jnp.float32)
    y = xf * lax.rsqrt(jnp.mean(xf * xf, axis=-1, keepdims=True) + EPS)
    return (y * g.astype(jnp.float32)).astype(x.dtype)


def dwconv_centred(x, w, b):
    K, C = w.shape
    y = lax.conv_general_dilated(x, w.reshape(K, 1, C).astype(x.dtype), window_strides=(1,),
                                 padding=[(K // 2, K // 2)], dimension_numbers=('NWC', 'WIO', 'NWC'),
                                 feature_group_count=C)
    return y + b


def rwkv7_bidir(z, mu, w0, w_up, a0, a_up, g_up, k_k, k_a, r_k, ln_w, ln_b):
    B, L, _ = z.shape
    H, N, C = RWKV_HEADS, RWKV_HEAD_DIM, RWKV_WIDTH
    zf = z.astype(jnp.float32)
    zs = zf[..., :RWKV_SHIFT_COLS]
    prev = jnp.pad(zs, ((0, 0), (1, 0), (0, 0)))[:, :L]
    nxt = jnp.pad(zs, ((0, 0), (0, 1), (0, 0)))[:, 1:]
    zd = jnp.stack([zs + (prev - zs) * mu[0], zs + (nxt - zs) * mu[1]])
    r = zd[..., :C]
    k = zd[..., C:2 * C]
    v = zd[..., 2 * C:3 * C]
    wd = zd[..., 3 * C:3 * C + DECAY_RANK]
    ad = zd[..., 3 * C + DECAY_RANK:]
    gd = zf[..., RWKV_SHIFT_COLS:]
    log_w = -jax.nn.softplus(-(w0[:, None, None] + jnp.einsum('dblr,drc->dblc', jnp.tanh(wd), w_up))) - 0.5
    w = jnp.exp(-jnp.exp(log_w))
    a = jax.nn.sigmoid(a0[:, None, None] + jnp.einsum('dblr,drc->dblc', ad, a_up))
    g = jax.nn.sigmoid(gd) @ g_up

    def heads(t):
        return t.reshape(2, B, L, H, N)

    kk = heads(k * k_k)
    kk = kk / jnp.maximum(jnp.sqrt(jnp.sum(kk * kk, axis=-1, keepdims=True)), 1e-12)
    k = heads(k * (1.0 + (a - 1.0) * k_a))
    r, w, v, a = heads(r), heads(w), heads(v), heads(a)

    def time_major(t):
        t = jnp.stack([t[0], jnp.flip(t[1], axis=1)])
        return jnp.moveaxis(t, 2, 0)

    def step(S, inp):
        r_t, w_t, k_t, v_t, kk_t, a_t = inp
        sa = jnp.einsum('dbhvk,dbhk->dbhv', S, kk_t)
        S = (S * w_t[..., None, :] - sa[..., :, None] * (kk_t * a_t)[..., None, :]
             + v_t[..., :, None] * k_t[..., None, :])
        return S, jnp.einsum('dbhvk,dbhk->dbhv', S, r_t)

    S0 = jnp.zeros((2, B, H, N, N), jnp.float32)
    seqs = (time_major(r), time_major(w), time_major(k), time_major(v), time_major(kk), time_major(a))
    _, ys = lax.scan(step, S0, seqs)
    ys = jnp.moveaxis(ys, 0, 2)
    y = ys[0] + jnp.flip(ys[1], axis=1)
    mean = jnp.mean(y, axis=-1, keepdims=True)
    var = jnp.mean(jnp.square(y - mean), axis=-1, keepdims=True)
    y = ((y - mean) * lax.rsqrt(var + RWKV_GN_EPS)).reshape(B, L, C) * ln_w + ln_b
    bonus = jnp.sum(jnp.sum(r * k * r_k, axis=-1, keepdims=True) * v, axis=0).reshape(B, L, C)
    return ((y + bonus) * g).astype(z.dtype)


def hyena_filters(L, w1, b1, w2, b2, w3, b3, w4, freq):
    f32 = jnp.float32
    t = jnp.linspace(0.0, 1.0, L, dtype=f32)[:, None]
    bands = (FILTER_EMB - 1) // 2
    ang = (2.0 * math.pi / L) * jnp.arange(L, dtype=f32)[:, None] * jnp.linspace(1e-4, bands - 1, bands, dtype=f32)[None]
    feats = jnp.concatenate([t, jnp.cos(ang), -jnp.sin(ang)], axis=-1)
    hdn = jnp.sin(freq * (feats @ w1 + b1))
    hdn = jnp.sin(freq * (hdn @ w2 + b2))
    hdn = jnp.sin(freq * (hdn @ w3 + b3))
    h = (hdn @ w4).reshape(L, 2, HYENA_WIDTH)
    deltas = jnp.abs(jnp.linspace(DECAY_MIN, DECAY_MAX, HYENA_WIDTH, dtype=f32))
    h = h * jnp.exp(-t * deltas)[:, None, :]
    return h * lax.rsqrt(jnp.sum(h * h, axis=(0, 1), keepdims=True) + EPS)


def hyena_bidir(z, short_w, short_b, f_w1, f_b1, f_w2, f_b2, f_w3, f_b3, f_w4, f_freq, bias):
    B, L, _ = z.shape
    C = HYENA_WIDTH
    u = dwconv_centred(z, short_w, short_b)
    x0, x1, v = u[..., :C], u[..., C:2 * C], u[..., 2 * C:]
    h = hyena_filters(L, f_w1.astype(jnp.float32), f_b1.astype(jnp.float32), f_w2.astype(jnp.float32),
                      f_b2.astype(jnp.float32), f_w3.astype(jnp.float32), f_b3.astype(jnp.float32),
                      f_w4.astype(jnp.float32), f_freq.astype(jnp.float32))
    kern = jnp.concatenate([h[:, 0], jnp.zeros((1, C), jnp.float32), jnp.flip(h[1:, 1], axis=0)], axis=0)
    s = (v * x1).astype(jnp.float32)
    y = jnp.fft.irfft(jnp.fft.rfft(s, n=2 * L, axis=1) * jnp.fft.rfft(kern, n=2 * L, axis=0)[None],
                      n=2 * L, axis=1)[:, :L]
    y = y + s * bias.astype(jnp.float32)
    return (y * x0.astype(jnp.float32)).astype(z.dtype)


def neighbourhood_attention(z, q_g, k_g, rpb):
    B, L, _ = z.shape
    rows = L // GRID_W
    kh, kw = min(NA_WIN_ROWS_MAX, rows), NA_WIN_COLS
    H, dh, D = NA_HEADS, NA_HEAD_DIM, D_MODEL

    def split(t):
        return jnp.transpose(t.reshape(B, rows, GRID_W, H, dh), (0, 3, 1, 2, 4))

    q = split(rmsnorm(z[..., :D].reshape(B, L, H, dh), q_g) * (dh ** -0.5))
    k = split(rmsnorm(z[..., D:2 * D].reshape(B, L, H, dh), k_g))
    v = split(z[..., 2 * D:])
    col = np.arange(GRID_W)
    cstart = np.clip(col - kw // 2, 0, GRID_W - kw)
    col_idx = cstart[:, None] + np.arange(kw)[None, :]
    dc_idx = col_idx - col[:, None] + (NA_WIN_COLS - 1)

    def row_block(i):
        rs = jnp.clip(i - kh // 2, 0, rows - kh)
        q_i = lax.dynamic_index_in_dim(q, i, axis=2, keepdims=False)
        k_win = lax.dynamic_slice_in_dim(k, rs, kh, axis=2)[:, :, :, col_idx]
        v_win = lax.dynamic_slice_in_dim(v, rs, kh, axis=2)[:, :, :, col_idx]
        s = jnp.einsum('bhqd,bhrqwd->bhqrw', q_i, k_win).astype(jnp.float32)
        dr_idx = rs + jnp.arange(kh) - i + (NA_WIN_ROWS_MAX - 1)
        bias = rpb[:, dr_idx][:, :, dc_idx]
        s = s + jnp.transpose(bias, (0, 2, 1, 3)).astype(jnp.float32)[None]
        pr = jax.nn.softmax(s.reshape(B, H, GRID_W, kh * kw), axis=-1).reshape(B, H, GRID_W, kh, kw)
        return jnp.einsum('bhqrw,bhrqwd->bhqd', pr.astype(v.dtype), v_win)

    o = lax.map(row_block, jnp.arange(rows))
    return jnp.transpose(o, (1, 0, 3, 2, 4)).reshape(B, L, D)


def conv_ffn(h, norm_g, w_up, conv_w, conv_b, w_down):
    u = dwconv_centred(rmsnorm(h, norm_g) @ w_up, conv_w, conv_b)
    return (jax.nn.gelu(u[..., :D_FF]) * u[..., D_FF:]) @ w_down


def per_layer_embedding(h, p_i, norm_g, w_gate, w_proj):
    return jax.nn.sigmoid(rmsnorm(h, norm_g) @ w_gate) * (p_i @ w_proj)


def setup_inputs(seed: int = 0) -> dict:
    key = jax.random.key(seed)
    keys = jax.random.split(key, 64)
    counter = [0]
    f32 = jnp.float32

    def nk():
        counter[0] += 1
        return keys[counter[0] - 1]

    def nrm(shape, scale):
        return scale * jax.random.normal(nk(), shape, f32)

    def gain(shape):
        return 1.0 + 0.02 * jax.random.normal(nk(), shape, f32)

    def unif(shape, lo, hi):
        return jax.random.uniform(nk(), shape, f32, lo, hi)

    D, E, O, C, Ch = D_MODEL, N_EVEN, N_ODD, RWKV_WIDTH, HYENA_WIDTH
    return {
        "x": nrm((BATCH, SEQ, D), 1.0),
        "p": nrm((DEPTH, BATCH, SEQ, PLE_DIM), 1.0),
        "mix_norm": gain((E, D)),
        "mix_w_in": nrm((E, D, IN_COLS), D ** -0.5),
        "rwkv_mu": unif((E, 2, RWKV_SHIFT_COLS), 0.0, 1.0),
        "rwkv_w0": unif((E, 2, C), -6.0, -0.5),
        "rwkv_w_up": nrm((E, 2, DECAY_RANK, C), 0.1),
        "rwkv_a0": nrm((E, 2, C), 0.1),
        "rwkv_a_up": nrm((E, 2, ICLR_RANK, C), 0.1),
        "rwkv_g_up": nrm((E, GATE_RANK, C), GATE_RANK ** -0.5),
        "rwkv_k_k": 0.85 + 0.02 * jax.random.normal(nk(), (E, C), f32),
        "rwkv_k_a": gain((E, C)),
        "rwkv_r_k": nrm((E, RWKV_HEADS, RWKV_HEAD_DIM), 0.1),
        "rwkv_ln_w": gain((E, C)),
        "rwkv_ln_b": nrm((E, C), 0.02),
        "hy_short_w": nrm((E, SHORT_CONV, HYENA_COLS), SHORT_CONV ** -0.5),
        "hy_short_b": nrm((E, HYENA_COLS), 0.02),
        "hy_f_w1": nrm((E, FILTER_EMB, FILTER_HIDDEN), FILTER_EMB ** -0.5),
        "hy_f_b1": nrm((E, FILTER_HIDDEN), 0.1),
        "hy_f_w2": nrm((E, FILTER_HIDDEN, FILTER_HIDDEN), FILTER_HIDDEN ** -0.5),
        "hy_f_b2": nrm((E, FILTER_HIDDEN), 0.1),
        "hy_f_w3": nrm((E, FILTER_HIDDEN, FILTER_HIDDEN), FILTER_HIDDEN ** -0.5),
        "hy_f_b3": nrm((E, FILTER_HIDDEN), 0.1),
        "hy_f_w4": nrm((E, FILTER_HIDDEN, 2 * Ch), FILTER_HIDDEN ** -0.5),
        "hy_f_freq": gain((E, FILTER_HIDDEN)),
        "hy_bias": nrm((E, Ch), 1.0),
        "mix_w_out": nrm((E, D, D), D ** -0.5),
        "na_norm": gain((O, D)),
        "na_w_qkv": nrm((O, D, 3 * D), D ** -0.5),
        "na_q_g": gain((O, NA_HEAD_DIM)),
        "na_k_g": gain((O, NA_HEAD_DIM)),
        "na_rpb": nrm((O, NA_HEADS, 2 * NA_WIN_ROWS_MAX - 1, 2 * NA_WIN_COLS - 1), 0.1),
        "na_w_out": nrm((O, D, D), D ** -0.5),
        "ffn_norm": gain((DEPTH, D)),
        "ffn_w_up": nrm((DEPTH, D, 2 * D_FF), D ** -0.5),
        "ffn_conv_w": nrm((DEPTH, FFN_CONV, 2 * D_FF), FFN_CONV ** -0.5),
        "ffn_conv_b": nrm((DEPTH, 2 * D_FF), 0.02),
        "ffn_w_down": nrm((DEPTH, D_FF, D), D_FF ** -0.5),
        "ple_norm": gain((DEPTH, D)),
        "ple_w_gate": nrm((DEPTH, D, D), D ** -0.5),
        "ple_w_proj": nrm((DEPTH, PLE_DIM, D), PLE_DIM ** -0.5),
    }


def reference(x, p, mix_norm, mix_w_in, rwkv_mu, rwkv_w0, rwkv_w_up, rwkv_a0, rwkv_a_up, rwkv_g_up,
              rwkv_k_k, rwkv_k_a, rwkv_r_k, rwkv_ln_w, rwkv_ln_b, hy_short_w, hy_short_b,
              hy_f_w1, hy_f_b1, hy_f_w2, hy_f_b2, hy_f_w3, hy_f_b3, hy_f_w4, hy_f_freq, hy_bias,
              mix_w_out, na_norm, na_w_qkv, na_q_g, na_k_g, na_rpb, na_w_out,
              ffn_norm, ffn_w_up, ffn_conv_w, ffn_conv_b, ffn_w_down,
              ple_norm, ple_w_gate, ple_w_proj):
    h = x
    for i in range(DEPTH):
        j = i // 2
        if i % 2 == 0:
            z = rmsnorm(h, mix_norm[j]) @ mix_w_in[j]
            y_a = rwkv7_bidir(z[..., :RWKV_COLS], rwkv_mu[j], rwkv_w0[j], rwkv_w_up[j], rwkv_a0[j],
                              rwkv_a_up[j], rwkv_g_up[j], rwkv_k_k[j], rwkv_k_a[j], rwkv_r_k[j],
                              rwkv_ln_w[j], rwkv_ln_b[j])
            y_b = hyena_bidir(z[..., RWKV_COLS:], hy_short_w[j], hy_short_b[j], hy_f_w1[j], hy_f_b1[j],
                              hy_f_w2[j], hy_f_b2[j], hy_f_w3[j], hy_f_b3[j], hy_f_w4[j], hy_f_freq[j],
                              hy_bias[j])
            h = h + jnp.concatenate([y_a, y_b], axis=-1) @ mix_w_out[j]
        else:
            z = rmsnorm(h, na_norm[j]) @ na_w_qkv[j]
            h = h + neighbourhood_attention(z, na_q_g[j], na_k_g[j], na_rpb[j]) @ na_w_out[j]
        h = h + conv_ffn(h, ffn_norm[i], ffn_w_up[i], ffn_conv_w[i], ffn_conv_b[i], ffn_w_down[i])
        h = h + per_layer_embedding(h, p[i], ple_norm[i], ple_w_gate[i], ple_w_proj[i])
    return h
```

```python
from contextlib import ExitStack
import math
import numpy as np
import concourse.bass as bass
import concourse.mybir as mybir
from concourse.bass_utils import run_bass_kernel_spmd

F32 = mybir.dt.float32
BF16 = mybir.dt.bfloat16
AF = mybir.ActivationFunctionType
ALU = mybir.AluOpType
AX = mybir.AxisListType

NCORES = 8
D = 1024
B = 4
L = 8192
EPS = 1e-6


class Prog:
    NDMA = 24
    SEM_LIMIT = 6000

    def __init__(self, nc, stack):
        self.nc = nc
        self.st = stack
        self.E = {'pe': nc.tensor, 'dve': nc.vector, 'act': nc.scalar, 'pool': nc.gpsimd, 'sp': nc.sync}
        self.sem = {e: stack.enter_context(nc.semaphore("s_" + e)) for e in ('pe', 'dve', 'act', 'pool')}
        self.dsem = [stack.enter_context(nc.semaphore("d%d" % i)) for i in range(self.NDMA)]
        self.dcnt = [0] * self.NDMA
        self.dnext = 0
        self.cnt = {e: 0 for e in self.sem}
        self.gen = {e: 0 for e in self.sem}
        self.allsem = {(e, 0): self.sem[e] for e in self.sem}
        self.seen = {e: {} for e in self.E}
        self.lastw = {}
        self.readers = {}
        self.nuniq = 0
        self.psum_names = set()

    def sb(self, name, shape, dt=F32):
        return self.st.enter_context(self.nc.sbuf_tensor(name, list(shape), dt))

    def ps(self, name, shape, dt=F32):
        self.psum_names.add(name)
        return self.st.enter_context(self.nc.psum_tensor(name, list(shape), dt))

    @staticmethod
    def key(ap):
        if isinstance(ap, str):
            return ap
        return ap.tensor.name

    def _need(self, eng, ev, waits):
        if ev is None:
            return
        src, n = ev
        if src[0] == eng and eng == 'pe':
            return
        if self.seen[eng].get(src, 0) >= n:
            return
        waits[src] = max(waits.get(src, 0), n)

    def _deps(self, eng, reads, writes):
        waits = {}
        for k in reads:
            self._need(eng, self.lastw.get(k), waits)
            if k in self.psum_names:
                for ev in self.readers.get(k, {}).items():
                    if ev[0][0] != eng:
                        self._need(eng, ev, waits)
        for k in writes:
            self._need(eng, self.lastw.get(k), waits)
            for ev in self.readers.get(k, {}).items():
                self._need(eng, ev, waits)
        for src, n in waits.items():
            s = self.dsem[src[1]] if src[0] == 'd' else self.allsem[src]
            self.E[eng].wait_ge(s, n)
            self.seen[eng][src] = n

    def _commit(self, ev, reads, writes):
        for k in writes:
            self.lastw[k] = ev
            self.readers[k] = {}
        for k in reads:
            r = self.readers.setdefault(k, {})
            r[ev[0]] = max(r.get(ev[0], 0), ev[1])

    def op(self, eng, fn, reads, writes):
        reads = [self.key(a) for a in reads if a is not None and not isinstance(a, (int, float))]
        writes = [self.key(a) for a in writes if a is not None]
        self._deps(eng, reads, writes)
        if self.cnt[eng] >= self.SEM_LIMIT:
            self.gen[eng] += 1
            self.cnt[eng] = 0
            self.sem[eng] = self.st.enter_context(self.nc.semaphore("s_%s_%d" % (eng, self.gen[eng])))
            self.allsem[(eng, self.gen[eng])] = self.sem[eng]
        ins = fn(self.E[eng])
        self.cnt[eng] += 1
        ins.then_inc(self.sem[eng], 1)
        self._commit(((eng, self.gen[eng]), self.cnt[eng]), reads, writes)
        return ins

    def dma(self, out, in_, q='sp', **kw):
        reads = [self.key(in_)]
        writes = [self.key(out)]
        i = self.dnext
        self.dnext = (self.dnext + 1) % self.NDMA
        if self.dcnt[i] > 0:
            w = {}
            self._need(q, (('d', i), self.dcnt[i]), w)
            for src, n in w.items():
                self.E[q].wait_ge(self.dsem[i], n)
                self.seen[q][src] = n
        self._deps(q, reads, writes)
        ins = self.E[q].dma_start(out=out, in_=in_, **kw)
        self.dcnt[i] += 16
        ins.then_inc(self.dsem[i], 16)
        self._commit((('d', i), self.dcnt[i]), reads, writes)
        return ins

    def barrier(self):
        for eng in self.E:
            for src in self.sem:
                key = (src, self.gen[src])
                if src != eng and self.cnt[src] > self.seen[eng].get(key, 0):
                    self.E[eng].wait_ge(self.sem[src], self.cnt[src])
                    self.seen[eng][key] = self.cnt[src]
            for i in range(self.NDMA):
                if self.dcnt[i] > self.seen[eng].get(('d', i), 0):
                    self.E[eng].wait_ge(self.dsem[i], self.dcnt[i])
                    self.seen[eng][('d', i)] = self.dcnt[i]

    def finish(self, keys):
        for eng in ('sp', 'pool'):
            self._deps(eng, [self.key(k) for k in keys], [])

    def mm(self, out, lhsT, rhs, start=True, stop=True):
        return self.op('pe', lambda e: e.matmul(out, lhsT, rhs, start=start, stop=stop), [lhsT, rhs], [out])

    def tr(self, out, in_, ident):
        return self.op('pe', lambda e: e.transpose(out, in_, ident), [in_, ident], [out])

    def act(self, out, in_, func, bias=0.0, scale=1.0, accum_out=None):
        kw = {}
        if accum_out is not None:
            kw['accum_out'] = accum_out
        return self.op('act', lambda e: e.activation(out, in_, func, bias=bias, scale=scale, **kw),
                       [in_, bias, scale], [out, accum_out])

    def tt(self, out, a, b, op, eng='dve'):
        return self.op(eng, lambda e: e.tensor_tensor(out, a, b, op), [a, b], [out])

    def ts(self, out, a, s1, s2, op0, op1=None, eng='dve', accum_out=None):
        kw = {}
        if accum_out is not None:
            kw['accum_out'] = accum_out
        if op1 is None:
            return self.op(eng, lambda e: e.tensor_scalar(out, a, s1, None, op0, **kw), [a, s1], [out, accum_out])
        return self.op(eng, lambda e: e.tensor_scalar(out, a, s1, s2, op0, op1, **kw), [a, s1, s2],
                       [out, accum_out])

    def stt(self, out, a, s, b, op0, op1, eng='dve'):
        return self.op(eng, lambda e: e.scalar_tensor_tensor(out, a, s, b, op0, op1), [a, s, b], [out])

    def copy(self, out, a, eng='dve'):
        if eng == 'act':
            return self.op('act', lambda e: e.copy(out, a), [a], [out])
        return self.op(eng, lambda e: e.tensor_copy(out, a), [a], [out])

    def recip(self, out, a):
        return self.op('dve', lambda e: e.reciprocal(out, a), [a], [out])

    def memset(self, out, v, eng='dve'):
        return self.op(eng, lambda e: e.memset(out, v), [], [out])

    def reduce(self, out, a, op, axis=AX.X, eng='dve'):
        return self.op(eng, lambda e: e.tensor_reduce(out, a, axis, op), [a], [out])


class Rot:
    def __init__(self, tiles):
        self.t = tiles
        self.i = 0

    def get(self):
        t = self.t[self.i % len(self.t)]
        self.i += 1
        return t


def new_nc():
    return bass.Bass("TRN2", target_bir_lowering=False)


def dram_in(nc, name, shape, dt=F32):
    return nc.dram_tensor(name, list(shape), dt, kind="ExternalInput").ap()


def dram_out(nc, name, shape, dt=F32):
    return nc.dram_tensor(name, list(shape), dt, kind="ExternalOutput").ap()


def load_weight_bf16(P, w_sb, w_dram, K, N, stg, engs=('dve', 'pool')):
    wv = w_dram.rearrange("(kc p) n -> p kc n", p=128)
    i = 0
    for kc in range(K // 128):
        for n0 in range(0, N, 2048):
            n1 = min(N, n0 + 2048)
            s = stg.get()
            P.dma(s[:, 0:n1 - n0], wv[:, kc, n0:n1])
            P.copy(w_sb[:, kc, n0:n1], s[:, 0:n1 - n0], eng=engs[i % len(engs)])
            i += 1


def load_cols(P, dst, vec_dram, n):
    P.dma(dst[:, 0:n], vec_dram.rearrange("(c p) -> p c", p=128), allow_slow_non_contiguous=True)


def rmsnorm_T(P, xn, aT, g_sb, ones_bf, sq, ssq_ps, rstd, KC, n, dmodel):
    for kc in range(KC):
        P.act(sq[:, kc, 0:n], aT[:, kc, 0:n], AF.Square)
    for kc in range(KC):
        P.mm(ssq_ps[:, 0:n], ones_bf[:], sq[:, kc, 0:n], start=(kc == 0), stop=(kc == KC - 1))
    P.ts(rstd[:, 0:n], ssq_ps[:, 0:n], 1.0 / dmodel, EPS, ALU.mult, ALU.add)
    P.act(rstd[:, 0:n], rstd[:, 0:n], AF.Sqrt)
    P.recip(rstd[:, 0:n], rstd[:, 0:n])
    for kc in range(KC):
        P.stt(xn[:, kc, 0:n], aT[:, kc, 0:n], g_sb[:, kc:kc + 1], rstd[:, 0:n], ALU.mult, ALU.mult)


def build_linear(T, K, N, norm, mode, K2=0):
    nc = new_nc()
    aT = dram_in(nc, "aT", [K, T])
    W = dram_in(nc, "W", [K, N])
    g = dram_in(nc, "g", [K]) if norm else None
    resT = dram_in(nc, "resT", [N, T]) if mode == 'res' else None
    if mode == 'ple':
        W2 = dram_in(nc, "W2", [K2, N])
        a2T = dram_in(nc, "a2T", [K2, T])
    oT = dram_out(nc, "oT", [N, T])
    KC, NCH, n = K // 128, N // 128, 512
    with ExitStack() as st:
        P = Prog(nc, st)
        w_sb = P.sb("w_sb", [128, KC, N], BF16)
        stg = Rot([P.sb("wstg%d" % i, [128, 2048]) for i in range(2)])
        load_weight_bf16(P, w_sb, W, K, N, stg)
        if mode == 'ple':
            w2_sb = P.sb("w2_sb", [128, K2 // 128, N], BF16)
            load_weight_bf16(P, w2_sb, W2, K2, N, stg)
        ones_bf = P.sb("ones_bf", [128, 128], BF16)
        P.memset(ones_bf[:], 1.0)
        if norm:
            g_sb = P.sb("g_sb", [128, KC])
            load_cols(P, g_sb, g, KC)
        a_t = Rot([P.sb("a_t%d" % i, [128, KC, n]) for i in range(2)])
        xn_t = Rot([P.sb("xn_t%d" % i, [128, KC, n], BF16) for i in range(2)])
        sq = P.sb("sq", [128, KC, n], BF16)
        rstd = P.sb("rstd", [128, n])
        ssq_ps = P.ps("ssq_ps", [128, n])
        pss = Rot([P.ps("ps%d" % i, [128, n]) for i in range(4)])
        outs = Rot([P.sb("o%d" % i, [128, n]) for i in range(4)])
        if mode == 'res':
            res_t = Rot([P.sb("res%d" % i, [128, n]) for i in range(3)])
        if mode == 'ple':
            a2_t = Rot([P.sb("a2_t%d" % i, [128, K2 // 128, n]) for i in range(2)])
            a2b_t = Rot([P.sb("a2b_t%d" % i, [128, K2 // 128, n], BF16) for i in range(2)])
            ps2s = Rot([P.ps("ps2_%d" % i, [128, n]) for i in range(2)])
            sg_t = Rot([P.sb("sg%d" % i, [128, n]) for i in range(2)])
        aTv = aT.rearrange("(kc p) t -> p kc t", p=128)
        for t0 in range(0, T, n):
            a = a_t.get()
            P.dma(a[:], aTv[:, :, t0:t0 + n])
            xn = xn_t.get()
            if norm:
                rmsnorm_T(P, xn, a, g_sb, ones_bf, sq, ssq_ps, rstd, KC, n, K)
            else:
                for kc in range(KC):
                    P.copy(xn[:, kc, :], a[:, kc, :], eng=('dve' if kc % 2 == 0 else 'pool'))
            if mode == 'ple':
                a2 = a2_t.get()
                P.dma(a2[:], a2T.rearrange("(kc p) t -> p kc t", p=128)[:, :, t0:t0 + n])
                a2b = a2b_t.get()
                P.copy(a2b[:], a2[:], eng='pool')
            for c in range(NCH):
                ps = pss.get()
                for kc in range(KC):
                    P.mm(ps[:], w_sb[:, kc, c * 128:(c + 1) * 128], xn[:, kc, :], start=(kc == 0), stop=(kc == KC - 1))
                o = outs.get()
                if mode == 'plain':
                    if c % 2 == 0:
                        P.copy(o[:], ps[:], eng='dve')
                    else:
                        P.copy(o[:], ps[:], eng='act')
                elif mode == 'res':
                    r = res_t.get()
                    P.dma(r[:], resT[c * 128:(c + 1) * 128, t0:t0 + n])
                    P.tt(o[:], ps[:], r[:], ALU.add)
                else:
                    ps2 = ps2s.get()
                    for kc in range(K2 // 128):
                        P.mm(ps2[:], w2_sb[:, kc, c * 128:(c + 1) * 128], a2b[:, kc, :], start=(kc == 0),
                             stop=(kc == K2 // 128 - 1))
                    sg = sg_t.get()
                    P.act(sg[:], ps[:], AF.Sigmoid)
                    P.tt(sg[:], sg[:], ps2[:], ALU.mult)
                    P.tt(o[:], sg[:], a[:, c, :], ALU.add, eng='pool')
                P.dma(oT[c * 128:(c + 1) * 128, t0:t0 + n], o[:], q='act')
        P.finish([oT])
    return nc


DFF = 2816


def build_ffn(T):
    nc = new_nc()
    hTp = dram_in(nc, "hTp", [D, T + 2])
    g = dram_in(nc, "g", [D])
    Wu = dram_in(nc, "Wu", [D, 2 * DFF])
    cw = dram_in(nc, "cw", [3, 2 * DFF])
    cb = dram_in(nc, "cb", [2 * DFF])
    Wd = dram_in(nc, "Wd", [DFF, D])
    oT = dram_out(nc, "oT", [D, T])
    KC, n, FC = D // 128, 256, DFF // 128
    with ExitStack() as st:
        P = Prog(nc, st)
        wu_sb = P.sb("wu_sb", [128, KC, 2 * DFF], BF16)
        wd_sb = P.sb("wd_sb", [128, FC, D], BF16)
        stg = Rot([P.sb("wstg%d" % i, [128, 2048]) for i in range(2)])
        load_weight_bf16(P, wu_sb, Wu, D, 2 * DFF, stg)
        load_weight_bf16(P, wd_sb, Wd, DFF, D, stg)
        ones_bf = P.sb("ones_bf", [128, 128], BF16)
        P.memset(ones_bf[:], 1.0)
        g_sb = P.sb("g_sb", [128, KC])
        load_cols(P, g_sb, g, KC)
        cw_sb = P.sb("cw_sb", [128, 3, 2 * FC])
        for j in range(3):
            load_cols(P, cw_sb[:, j, :], cw[j], 2 * FC)
        cb_sb = P.sb("cb_sb", [128, 2 * FC])
        load_cols(P, cb_sb, cb, 2 * FC)
        a_t = Rot([P.sb("a_t%d" % i, [128, KC, n + 2]) for i in range(2)])
        xn = P.sb("xn", [128, KC, n + 2], BF16)
        sq = P.sb("sq", [128, KC, n + 2], BF16)
        rstd = P.sb("rstd", [128, n + 2])
        ssq_ps = P.ps("ssq_ps", [128, n + 2])
        pss = Rot([P.ps("ps%d" % i, [128, n + 2]) for i in range(4)])
        psd = Rot([P.ps("psd%d" % i, [128, n]) for i in range(2)])
        gT = P.sb("gT", [128, FC, n], BF16)
        ca_t = Rot([P.sb("ca%d" % i, [128, n]) for i in range(2)])
        cb_t = Rot([P.sb("cbv%d" % i, [128, n]) for i in range(2)])
        t1_t = Rot([P.sb("t1_%d" % i, [128, n]) for i in range(2)])
        t2_t = Rot([P.sb("t2_%d" % i, [128, n]) for i in range(2)])
        outs = Rot([P.sb("o%d" % i, [128, n]) for i in range(3)])
        hv = hTp.rearrange("(kc p) t -> p kc t", p=128)

        def conv(dst, ps, col):
            P.act(dst[:], ps[:, 1:n + 1], AF.Identity, bias=cb_sb[:, col:col + 1], scale=cw_sb[:, 1, col:col + 1])
            P.stt(dst[:], ps[:, 0:n], cw_sb[:, 0, col:col + 1], dst[:], ALU.mult, ALU.add)
            P.stt(dst[:], ps[:, 2:n + 2], cw_sb[:, 2, col:col + 1], dst[:], ALU.mult, ALU.add)

        for t0 in range(0, T, n):
            a = a_t.get()
            P.dma(a[:], hv[:, :, t0:t0 + n + 2])
            rmsnorm_T(P, xn, a, g_sb, ones_bf, sq, ssq_ps, rstd, KC, n + 2, D)
            for fc in range(FC):
                psa, psb = pss.get(), pss.get()
                for kc in range(KC):
                    P.mm(psa[:], wu_sb[:, kc, fc * 128:(fc + 1) * 128], xn[:, kc, :], start=(kc == 0),
                         stop=(kc == KC - 1))
                for kc in range(KC):
                    P.mm(psb[:], wu_sb[:, kc, DFF + fc * 128:DFF + (fc + 1) * 128], xn[:, kc, :], start=(kc == 0),
                         stop=(kc == KC - 1))
                ca, cbv, t1, t2 = ca_t.get(), cb_t.get(), t1_t.get(), t2_t.get()
                conv(ca, psa, fc)
                conv(cbv, psb, FC + fc)
                P.act(t1[:], ca[:], AF.Square)
                P.ts(t1[:], t1[:], 0.044715, 1.0, ALU.mult, ALU.add, eng='pool')
                P.tt(t1[:], t1[:], ca[:], ALU.mult, eng='pool')
                P.act(t2[:], t1[:], AF.Sigmoid, scale=1.5957691216057308)
                P.tt(t2[:], t2[:], ca[:], ALU.mult, eng='pool')
                P.tt(gT[:, fc, :], t2[:], cbv[:], ALU.mult, eng='pool')
            for c in range(KC):
                ps = psd.get()
                for fc in range(FC):
                    P.mm(ps[:], wd_sb[:, fc, c * 128:(c + 1) * 128], gT[:, fc, :], start=(fc == 0), stop=(fc == FC - 1))
                o = outs.get()
                P.tt(o[:], ps[:], a[:, c, 1:n + 1], ALU.add)
                P.dma(oT[c * 128:(c + 1) * 128, t0:t0 + n], o[:], q='sp')
        P.finish([oT])
    return nc


def run(nc, in_maps):
    res = run_bass_kernel_spmd(nc, in_maps, core_ids=list(range(NCORES)))
    return res.results


GW = 64
NROWS = L // GW


def build_na(dbg_rows=NROWS, dbg_pv=True):
    nc = new_nc()
    qT = dram_in(nc, "qT", [512, L])
    kT = dram_in(nc, "kT", [512, L])
    v = dram_in(nc, "v", [L, 512])
    bias = dram_in(nc, "bias", [19 * 64, 512])
    qg = dram_in(nc, "qg", [64])
    kg = dram_in(nc, "kg", [64])
    o = dram_out(nc, "o", [L, 512])
    n = 512
    NT = L // n
    with ExitStack() as st:
        P = Prog(nc, st)
        bd = P.sb("bd", [128, 128], BF16)
        P.memset(bd[:], 0.0)
        P.memset(bd[0:64, 0:64], 1.0)
        P.memset(bd[64:128, 64:128], 1.0)
        g2 = P.sb("g2", [128, 2])
        for hh in range(2):
            P.dma(g2[hh * 64:(hh + 1) * 64, 0:1], qg.rearrange("(p o) -> p o", o=1))
            P.dma(g2[hh * 64:(hh + 1) * 64, 1:2], kg.rearrange("(p o) -> p o", o=1))
        P.ts(g2[:, 0:1], g2[:, 0:1], 0.125, None, ALU.mult)
        bias_sb = P.sb("bias_sb", [128, 16, 512])
        for dr0 in range(14):
            P.dma(bias_sb[:, dr0, :], bias[dr0 * 64:dr0 * 64 + 128, :])
        for x in range(2):
            P.dma(bias_sb[:, 14 + x, :], bias[(15 + 2 * x) * 64:(15 + 2 * x) * 64 + 128, :])
        kTn = P.sb("kTn", [128, 4, L], BF16)
        V1 = P.sb("V1", [128, L // 128, 8, 65], BF16)
        P.memset(V1[:, :, :, 64:65], 1.0, eng='pool')
        raw = Rot([P.sb("raw%d" % i, [128, 4, n]) for i in range(1)])
        sq = P.sb("sq", [128, 4, n], BF16)
        rstd = P.sb("rstd", [128, 4, n])
        ssq = Rot([P.ps("ssq%d" % i, [128, n]) for i in range(2)])
        qTn_t = Rot([P.sb("qz%d" % i, [128, 8, n], BF16) for i in range(2)])
        for t_ in qTn_t.t:
            P.memset(t_[:], 0.0, eng='pool')
        vst = raw

        def headnorm(dst, src_dram, t0, gcol, sep=False):
            r = raw.get()
            P.dma(r[:], src_dram.rearrange("(c p) t -> p c t", p=128)[:, :, t0:t0 + n])
            for c in range(4):
                P.act(sq[:, c, :], r[:, c, :], AF.Square)
            for c in range(4):
                ps = ssq.get()
                P.mm(ps[:], bd[:], sq[:, c, :])
                P.ts(rstd[:, c, :], ps[:], 1.0 / 64, EPS, ALU.mult, ALU.add)
            P.act(rstd[:], rstd[:], AF.Sqrt)
            P.recip(rstd[:], rstd[:])
            for c in range(4):
                if sep:
                    for hh in range(2):
                        pr = slice(hh * 64, hh * 64 + 64)
                        P.stt(dst[pr, 2 * c + hh, :], r[pr, c, :], g2[pr, gcol:gcol + 1], rstd[pr, c, :], ALU.mult,
                              ALU.mult)
                else:
                    P.stt(dst[:, c, :], r[:, c, :], g2[:, gcol:gcol + 1], rstd[:, c, :], ALU.mult, ALU.mult)

        for j in range(NT):
            headnorm(kTn[:, :, j * n:(j + 1) * n], kT, j * n, 1)
            s = vst.get()
            P.dma(s[:], v[j * n:(j + 1) * n, :].rearrange("(a p) c -> p a c", p=128))
            P.copy(V1[:, j * 4:(j + 1) * 4, :, 0:64], s[:].rearrange("p a (h d) -> p a h d", d=64), eng='pool')

        st_ps = Rot([P.ps("st%d" % i, [128, 512]) for i in range(2)])
        o_ps = Rot([P.ps("ops%d" % i, [64, 512])[:, 0:260].rearrange("p (h d) -> p h d", d=65) for i in range(4)])
        sb_t = Rot([P.sb("sbt%d" % i, [128, 512]) for i in range(2)])
        e_t = Rot([P.sb("et%d" % i, [128, 512], BF16) for i in range(2)])
        rec_t = Rot([P.sb("rec%d" % i, [64, 8]) for i in range(2)])
        ob_t = Rot([P.sb("ob%d" % i, [64, 512]) for i in range(2)])
        for j in range(NT):
            qTn = qTn_t.get()
            headnorm(qTn, qT, j * n, 0, sep=True)
            for ii in range(8):
                i = j * 8 + ii
                if i >= dbg_rows:
                    break
                rs = min(max(i - 4, 0), NROWS - 8)
                dl = i - rs
                oa, ob = o_ps.get(), o_ps.get()
                odd = rs % 2
                nkc = 5 if odd else 4
                for kc in range(nkc):
                    k0 = (rs - odd + 2 * kc) * GW
                    if not odd:
                        bt = 2 * kc - dl + 7
                    else:
                        assert dl == 4
                        bt = 14 if kc == 0 else (15 if kc == 4 else 2 * kc - 1 - dl + 7)
                    sp = st_ps.get()
                    for h in range(8):
                        P.mm(sp[:, h * 64:(h + 1) * 64], kTn[:, h // 2, k0:k0 + 128], qTn[:, h, ii * 64:(ii + 1) * 64])
                    sb_ = sb_t.get()
                    P.tt(sb_[:], sp[:], bias_sb[:, bt, :], ALU.add)
                    e = e_t.get()
                    P.act(e[:], sb_[:], AF.Exp)
                    for h in range(8):
                        op_ = oa if h < 4 else ob
                        P.mm(op_[:, h % 4, :], e[:, h * 64:(h + 1) * 64], V1[:, k0 // 128, h, :],
                             start=(kc == 0 and h % 4 == 0), stop=(kc == nkc - 1 and h % 4 == 3))
                rec = rec_t.get()
                P.recip(rec[:, 0:4], oa[:, :, 64])
                P.recip(rec[:, 4:8], ob[:, :, 64])
                obuf = ob_t.get()
                for h in range(8):
                    op_ = oa if h < 4 else ob
                    P.act(obuf[:, h * 64:(h + 1) * 64], op_[:, h % 4, 0:64], AF.Copy, scale=rec[:, h:h + 1])
                P.dma(o[i * 64:(i + 1) * 64, :], obuf[:], q='act')
        P.finish([o])
    return nc


def na_bias_table(rpb, hg):
    col = np.arange(GW)
    cstart = np.clip(col - 8, 0, GW - 16)
    cp = np.arange(GW)[:, None]
    cq = np.arange(GW)[None, :]
    valid = (cp >= cstart[None, :]) & (cp < cstart[None, :] + 16)
    dc = np.clip(cp - cq + 15, 0, 30)
    t = rpb[hg * 8:(hg + 1) * 8][:, :, dc]
    t = np.where(valid[None, None], t, np.float32(-30000.0)).astype(np.float32)
    t = t.transpose(1, 2, 0, 3).reshape(15, 64, 8 * 64)
    m = np.full((1, 64, 512), -30000.0, np.float32)
    return np.ascontiguousarray(np.concatenate([t, m, t[3:4], t[10:11], m], axis=0).reshape(19 * 64, 512))


HG = 64
NFFT = 2 * L


def hyena_consts():
    f64 = np.float64
    i64 = np.arange(64, dtype=f64)[:, None]
    i128 = np.arange(128, dtype=f64)
    a = 2 * np.pi * i64 * i128[None, :] / 128.0
    c = {}
    c["F1"] = np.concatenate([np.cos(a), -np.sin(a)], axis=1)
    t = 2 * np.pi * i128[:, None] * i128[None, :] / NFFT
    c["TW1"] = np.stack([np.cos(t), -np.sin(t)], axis=1)
    c["TW2"] = np.stack([np.cos(t), np.sin(t)], axis=1)
    b = 2 * np.pi * i128[:, None] * i128[None, :] / 128.0
    c["F3"] = np.stack([np.cos(b), -np.sin(b), np.sin(b)], axis=1)
    c["G1"] = np.concatenate([np.cos(b), np.sin(b)], axis=1)
    c["G2"] = np.concatenate([-np.sin(b), np.cos(b)], axis=1)
    a2 = 2 * np.pi * i128[:, None] * i64.T / 128.0
    c["FI"] = np.stack([np.cos(a2) / NFFT, -np.sin(a2) / NFFT], axis=1)
    f32 = np.float32
    tt_ = np.linspace(0.0, 1.0, L, dtype=f32)[:, None]
    bands = 8
    ang = (f32(2.0 * math.pi / L) * np.arange(L, dtype=f32)[:, None]) * np.linspace(1e-4, bands - 1, bands, dtype=f32)[None]
    feats = np.concatenate([tt_, np.cos(ang), -np.sin(ang)], axis=-1)
    c["featsT"] = np.ascontiguousarray(feats.T)
    c["tpos"] = tt_[:, 0]
    return {k: np.ascontiguousarray(v, dtype=np.float32) for k, v in c.items()}


def hyena_window(core):
    deltas = np.abs(np.linspace(DECAY_MIN_, DECAY_MAX_, 512, dtype=np.float32))[core * HG:(core + 1) * HG]
    t = np.linspace(0.0, 1.0, L, dtype=np.float32)
    w = np.exp(-t[:, None] * deltas[None, :])
    return np.ascontiguousarray(w.reshape(64, 128, HG).transpose(0, 2, 1))


DECAY_MIN_ = math.log(1e-2) / 1.5
DECAY_MAX_ = math.log(1e-2) / 0.3


def build_hyena():
    nc = new_nc()
    hz = dram_in(nc, "hz", [B, 3, 64, HG, 130])
    cw = dram_in(nc, "cw", [3 * 3 * HG])
    cbv = dram_in(nc, "cbv", [3 * HG])
    hb = dram_in(nc, "hb", [HG])
    w1 = dram_in(nc, "w1", [17, 64]); b1 = dram_in(nc, "b1", [64])
    w2 = dram_in(nc, "w2", [64, 64]); b2 = dram_in(nc, "b2", [64])
    w3 = dram_in(nc, "w3", [64, 64]); b3 = dram_in(nc, "b3", [64])
    w4 = dram_in(nc, "w4", [64, 2 * HG])
    fr = dram_in(nc, "fr", [64])
    win = dram_in(nc, "win", [64, HG, 128])
    cF1 = dram_in(nc, "F1", [64, 256]); cTW1 = dram_in(nc, "TW1", [128, 2, 128]); cTW2 = dram_in(nc, "TW2", [128, 2, 128])
    cF3 = dram_in(nc, "F3", [128, 3, 128]); cG1 = dram_in(nc, "G1", [128, 256]); cG2 = dram_in(nc, "G2", [128, 256])
    cFI = dram_in(nc, "FI", [128, 2, 64]); featsT = dram_in(nc, "featsT", [17, L])
    o = dram_out(nc, "o", [B, 64, HG, 128])
    TWO_PI = 2.0 * math.pi
    with ExitStack() as st:
        P = Prog(nc, st)
        F1 = P.sb("F1s", [64, 256]); TW1 = P.sb("TW1s", [128, 2, 128]); TW2 = P.sb("TW2s", [128, 2, 128])
        F3 = P.sb("F3s", [128, 3, 128]); G1 = P.sb("G1s", [128, 256]); G2 = P.sb("G2s", [128, 256])
        FI = P.sb("FIs", [128, 2, 64])
        for t_, d_ in ((F1, cF1), (TW1, cTW1), (TW2, cTW2), (F3, cF3), (G1, cG1), (G2, cG2), (FI, cFI)):
            P.dma(t_[:], d_)
        HK = P.sb("HK", [128, 2, HG, 128])
        At = Rot([P.sb("At%d" % i, [128, 2, 4, 128]) for i in range(1)])
        Yt = Rot([P.sb("Yt%d" % i, [128, 2, 4, 128]) for i in range(1)])
        Bt = Rot([P.sb("Bt%d" % i, [128, 2, 4, 128]) for i in range(1)])
        tm = Rot([P.sb("tm%d" % i, [128, 4, 128]) for i in range(4)])
        ps1 = Rot([P.ps("ps1_%d" % i, [128, 2, 256]) for i in range(2)])
        psX = [P.ps("psXr", [128, 4, 128]), P.ps("psXi", [128, 4, 128])]
        psB = Rot([P.ps("psB%d" % i, [128, 2, 256]) for i in range(2)])
        psY = P.ps("psY", [64, 4, 128])
        pi_c = P.sb("pi_c", [128, 1])
        P.memset(pi_c[:], -math.pi)

        def cmul(out_r, out_i, ar, ai, br, bi, ns):
            t1, t2, t3, t4 = [tm.get()[:, 0:ns, :] for _ in range(4)]
            P.tt(t1, ar, br, ALU.mult)
            P.tt(t2, ai, bi, ALU.mult)
            P.tt(out_r, t1, t2, ALU.subtract, eng='pool')
            P.tt(t3, ar, bi, ALU.mult)
            P.tt(t4, ai, br, ALU.mult)
            P.tt(out_i, t3, t4, ALU.add, eng='pool')

        def bc(tw, k, ns):
            return tw[:, k, :].unsqueeze(1).to_broadcast([128, ns, 128])

        def fwd4(sig):
            a = At.get()
            for pr in range(2):
                p1 = ps1.get()
                for i in range(2):
                    P.mm(p1[:, i, :], sig[2 * pr + i], F1[:])
                cmul(a[:, 0, 2 * pr:2 * pr + 2, :], a[:, 1, 2 * pr:2 * pr + 2, :], p1[:, :, 0:128], p1[:, :, 128:256],
                     bc(TW1, 0, 2), bc(TW1, 1, 2), 2)
            ar = a[:, 0, :, :].rearrange("p s k -> p (s k)")
            ai = a[:, 1, :, :].rearrange("p s k -> p (s k)")
            xr = psX[0][:].rearrange("p s k -> p (s k)")
            xi = psX[1][:].rearrange("p s k -> p (s k)")
            P.mm(xr, F3[:, 0, :], ar, start=True, stop=False)
            P.mm(xr, F3[:, 2, :], ai, start=False, stop=True)
            P.mm(xi, F3[:, 0, :], ai, start=True, stop=False)
            P.mm(xi, F3[:, 1, :], ar, start=False, stop=True)

        with ExitStack() as st2:
            def sb2(name, shape):
                return st2.enter_context(nc.sbuf_tensor(name, list(shape), F32))
            Hk = sb2("Hk", [64, 2 * HG, 128])
            h3 = sb2("h3", [64, L])
            w1s = sb2("w1s", [17, 64]); w2s = sb2("w2s", [64, 64]); w3s = sb2("w3s", [64, 64]); w4s = sb2("w4s", [64, 2 * HG])
            P.dma(w1s[:], w1); P.dma(w2s[:], w2); P.dma(w3s[:], w3); P.dma(w4s[:], w4)
            bs = sb2("bs", [64, 4])
            for i, v_ in enumerate((b1, b2, b3, fr)):
                P.dma(bs[:, i:i + 1], v_.rearrange("(p o) -> p o", o=1))
            targ = Rot([sb2("targ%d" % i, [64, 512]) for i in range(2)])
            hdt = Rot([sb2("hdt%d" % i, [64, 512]) for i in range(2)])
            fTt = Rot([sb2("fTt%d" % i, [17, 512]) for i in range(2)])
            winc = Rot([sb2("winc%d" % i, [64, 8, 128]) for i in range(1)])
            psm = [psB.t[0], psB.t[1]]
            for j in range(L // 512):
                src = fTt.get()
                P.dma(src[:], featsT[:, j * 512:(j + 1) * 512])
                for layer, wl in enumerate((w1s, w2s, w3s)):
                    pm = psm[layer % 2][0:64, :, :].rearrange("p a b -> p (a b)")
                    P.mm(pm, wl[:], src[:])
                    ta = targ.get()
                    P.ts(ta[:], pm, bs[:, layer:layer + 1], bs[:, 3:4], ALU.add, ALU.mult)
                    tk = targ.get()
                    P.ts(tk[:], ta[:], 1.0 / TWO_PI, 12582912.0, ALU.mult, ALU.add)
                    P.ts(tk[:], tk[:], 12582912.0, -TWO_PI, ALU.subtract, ALU.mult)
                    P.tt(ta[:], ta[:], tk[:], ALU.add)
                    P.ts(ta[:], ta[:], -3.1415925, 3.1415925, ALU.max, ALU.min)
                    dst = h3[:, j * 512:(j + 1) * 512] if layer == 2 else hdt.get()[:]
                    P.act(dst, ta[:], AF.Sin)
                    src = dst if layer == 2 else hdt.t[(hdt.i - 1) % 2]
            for q4 in range(32):
                p4 = psm[q4 % 2][0:64, :, :].rearrange("p a (b c) -> p (a b) c", c=128)
                for a_ in range(4):
                    n2 = q4 * 4 + a_
                    P.mm(p4[:, a_, :], h3[:, n2:L:128], w4s[:])
                P.copy(Hk[:, :, q4 * 4:q4 * 4 + 4], p4.rearrange("p a c -> p c a"), eng=('dve' if q4 % 2 == 0 else 'act'))
            for q in range(8):
                wc = winc.get()
                P.dma(wc[:], win[:, q * 8:(q + 1) * 8, :])
                for d_ in range(2):
                    hv = Hk[:, d_ * HG + q * 8:d_ * HG + (q + 1) * 8, :]
                    P.tt(hv, hv, wc[:], ALU.mult, eng=('dve' if d_ == 0 else 'pool'))
            r1 = sb2("r1", [64, 2 * HG]); r2 = sb2("r2", [64, HG]); sc = sb2("sc", [64, HG])
            ones64 = sb2("ones64", [64, 64])
            P.memset(ones64[:], 1.0)
            for q in range(16):
                sv = At.t[0][0:64, :, :, :].rearrange("p a s k -> p (a s) k")
                P.tt(sv, Hk[:, q * 8:(q + 1) * 8, :], Hk[:, q * 8:(q + 1) * 8, :], ALU.mult, eng='pool')
                P.reduce(r1[:, q * 8:(q + 1) * 8], sv, ALU.add)
            P.tt(r2[:], r1[:, 0:HG], r1[:, HG:2 * HG], ALU.add)
            pn = psY[:, 0, 0:HG]
            P.mm(pn, ones64[:], r2[:])
            P.ts(sc[:], pn, EPS, None, ALU.add)
            P.act(sc[:], sc[:], AF.Sqrt)
            P.recip(sc[:], sc[:])
            for d_ in range(2):
                for q in range(4):
                    hv = Hk[:, d_ * HG + q * 16:d_ * HG + (q + 1) * 16, :]
                    P.tt(hv, hv, sc[:, q * 16:(q + 1) * 16].unsqueeze(2).to_broadcast([64, 16, 128]), ALU.mult,
                         eng=('dve' if q % 2 == 0 else 'pool'))
            P.memset(Hk[0:1, HG:2 * HG, 0:1], 0.0)
            for d_ in range(2):
                for g in range(HG // 4):
                    fwd4([Hk[:, d_ * HG + g * 4 + i, :] for i in range(4)])
                    hr, hi = HK[:, 0, g * 4:g * 4 + 4, :], HK[:, 1, g * 4:g * 4 + 4, :]
                    if d_ == 0:
                        P.copy(hr, psX[0][:])
                        P.copy(hi, psX[1][:], eng='act')
                    else:
                        P.tt(hr, hr, psX[0][:], ALU.add)
                        P.tt(hi, hi, psX[1][:], ALU.subtract)
            P.barrier()
        zin = [P.sb("zin%d" % i, [64, 16, 130]) for i in range(3)]
        ut = [P.sb("ut%d" % i, [64, 16, 128]) for i in range(3)]
        tcv = [P.sb("tcv%d" % i, [64, 16, 128]) for i in range(2)]
        og = Rot([P.sb("og%d" % i, [64, 16, 128]) for i in range(2)])
        cws = P.sb("cws", [64, 9 * HG]); cbs = P.sb("cbs", [64, 3 * HG]); hbs = P.sb("hbs", [64, HG])
        P.dma(cws[:], cw.partition_broadcast(64))
        P.dma(cbs[:], cbv.partition_broadcast(64))
        P.dma(hbs[:], hb.partition_broadcast(64))

        def chb(t_, off, c0):
            return t_[:, off + c0:off + c0 + 16].unsqueeze(2).to_broadcast([64, 16, 128])

        for b_ in range(B):
            for cg in range(HG // 16):
                c0 = cg * 16
                for wh in range(3):
                    P.dma(zin[wh][:], hz[b_, wh, :, c0:c0 + 16, :])
                    eng = 'dve' if wh != 1 else 'pool'
                    u, t2 = ut[wh], tcv[0 if wh != 1 else 1]
                    P.tt(u[:], zin[wh][:, :, 0:128], chb(cws, (wh * 3 + 0) * HG, c0), ALU.mult, eng=eng)
                    P.tt(t2[:], zin[wh][:, :, 1:129], chb(cws, (wh * 3 + 1) * HG, c0), ALU.mult, eng=eng)
                    P.tt(u[:], u[:], t2[:], ALU.add, eng=eng)
                    P.tt(t2[:], zin[wh][:, :, 2:130], chb(cws, (wh * 3 + 2) * HG, c0), ALU.mult, eng=eng)
                    P.tt(u[:], u[:], t2[:], ALU.add, eng=eng)
                    P.tt(u[:], u[:], chb(cbs, wh * HG, c0), ALU.add, eng=eng)
                x0, s_, sb_ = ut[0], ut[2], ut[1]
                P.tt(s_[:], ut[2][:], ut[1][:], ALU.mult)
                P.tt(sb_[:], s_[:], chb(hbs, 0, c0), ALU.mult, eng='pool')
                ogt = og.get()
                for sg in range(4):
                    ch0 = c0 + sg * 4
                    fwd4([s_[:, sg * 4 + i, :] for i in range(4)])
                    y = Yt.get()
                    cmul(y[:, 0, :, :], y[:, 1, :, :], psX[0][:], psX[1][:], HK[:, 0, ch0:ch0 + 4, :], HK[:, 1, ch0:ch0 + 4, :], 4)
                    bt = Bt.get()
                    for pr in range(2):
                        pb = psB.get()
                        for i in range(2):
                            P.mm(pb[:, i, :], y[:, 0, 2 * pr + i, :], G1[:], start=True, stop=False)
                            P.mm(pb[:, i, :], y[:, 1, 2 * pr + i, :], G2[:], start=False, stop=True)
                        cmul(bt[:, 0, 2 * pr:2 * pr + 2, :], bt[:, 1, 2 * pr:2 * pr + 2, :], pb[:, :, 0:128], pb[:, :, 128:256],
                             bc(TW2, 0, 2), bc(TW2, 1, 2), 2)
                    py = psY[:].rearrange("p s k -> p (s k)")
                    P.mm(py, FI[:, 0, :], bt[:, 0, :, :].rearrange("p s k -> p (s k)"), start=True, stop=False)
                    P.mm(py, FI[:, 1, :], bt[:, 1, :, :].rearrange("p s k -> p (s k)"), start=False, stop=True)
                    ov = ogt[:, sg * 4:sg * 4 + 4, :]
                    P.tt(ov, psY[:], sb_[:, sg * 4:sg * 4 + 4, :], ALU.add)
                    P.tt(ov, ov, x0[:, sg * 4:sg * 4 + 4, :], ALU.mult, eng='pool')
                P.dma(o[b_, :, c0:c0 + 16, :], ogt[:], q='act')
        P.finish([o])
    return nc


RS = 1664
DEC = math.exp(-0.5)


def rwkv_consts():
    s = np.arange(128)[:, None]
    t = np.arange(128)[None, :]
    return np.ascontiguousarray(np.stack([(s <= t), (s < t), (s > t), (s == t)], axis=1).astype(np.float32))


def build_rwkv(Lc=L, dbg=9):
    nc = new_nc()
    zp = dram_in(nc, "zp", [Lc + 1, RS])
    mu = dram_in(nc, "mu", [RS])
    w0 = dram_in(nc, "w0", [512]); wup = dram_in(nc, "wup", [64, 512])
    a0 = dram_in(nc, "a0", [512]); aup = dram_in(nc, "aup", [64, 512])
    kkv = dram_in(nc, "kk", [512]); kav = dram_in(nc, "ka", [512]); rkv = dram_in(nc, "rk", [512])
    tri = dram_in(nc, "tri", [128, 4, 128])
    ys = dram_out(nc, "ys", [Lc, 512])
    bon = dram_out(nc, "bon", [Lc, 512])
    NCH = Lc // 128
    with ExitStack() as st:
        P = Prog(nc, st)
        tr_ = P.sb("tri_s", [128, 4, 128])
        P.dma(tr_[:], tri)
        mU, mSU, mSL, ident = tr_[:, 0, :], tr_[:, 1, :], tr_[:, 2, :], tr_[:, 3, :]
        mu_s = P.sb("mu_s", [128, RS]); P.dma(mu_s[:], mu.partition_broadcast(128))
        kk_s = P.sb("kk_s", [128, 512]); P.dma(kk_s[:], kkv.partition_broadcast(128))
        ka_s = P.sb("ka_s", [128, 512]); P.dma(ka_s[:], kav.partition_broadcast(128))
        rk_s = P.sb("rk_s", [128, 512]); P.dma(rk_s[:], rkv.partition_broadcast(128))
        wup_s = P.sb("wup_s", [64, 512]); P.dma(wup_s[:], wup)
        aup_s = P.sb("aup_s", [64, 512]); P.dma(aup_s[:], aup)
        rows = P.sb("rows", [1, 2, 512])
        P.dma(rows[:, 0, :], w0.rearrange("(o n) -> o n", o=1))
        P.dma(rows[:, 1, :], a0.rearrange("(o n) -> o n", o=1))
        ones = P.sb("ones", [128, 128]); P.memset(ones[:], 1.0)
        ST = [P.sb("ST%d" % h, [64, 64]) for h in range(8)]
        for h in range(8):
            P.memset(ST[h][:], 0.0, eng='pool')
        prev_t = Rot([P.sb("prev%d" % i, [128, RS]) for i in range(2)])
        cur_t = Rot([P.sb("cur%d" % i, [128, RS]) for i in range(2)])
        zd_t = Rot([P.sb("zd%d" % i, [128, RS]) for i in range(2)])
        Q4_t = Rot([P.sb("Q4_%d" % i, [128, 4, 512]) for i in range(2)])
        QT_t = Rot([P.sb("QT_%d" % i, [64, 4, 8, 128]) for i in range(2)])
        KH_t = Rot([P.sb("KH_%d" % i, [128, 2, 512]) for i in range(2)])
        gC_t = Rot([P.sb("gC_%d" % i, [64, 8]) for i in range(2)])
        wT = P.sb("wT", [64, 2, 128])
        sg = P.sb("sg", [128, 512]); av = P.sb("av", [128, 512])
        Gt = P.sb("Gt", [128, 512]); Gp = P.sb("Gp", [128, 512]); Gi = P.sb("Gi", [128, 512]); Gr = P.sb("Gr", [128, 512])
        kap = P.sb("kap", [128, 512]); kmod = P.sb("kmod", [128, 512]); beta = P.sb("beta", [128, 512])
        T1 = P.sb("T1", [128, 512]); T2 = P.sb("T2", [128, 512])
        s8 = P.sb("s8", [128, 4, 8])
        ys_t = Rot([P.sb("ys_t%d" % i, [128, 512]) for i in range(2)])
        bn_t = Rot([P.sb("bn_t%d" % i, [128, 512]) for i in range(2)])
        mats = {nm: Rot([P.sb("%s%d" % (nm, i), [128, 128]) for i in range(n_)]) for nm, n_ in
                (("A", 2), ("AT", 2), ("B", 3), ("BT", 3), ("IBT", 2), ("Pm", 3), ("MkT", 2), ("Nk", 2), ("nNb", 2), ("W2", 2))}
        W1_t = Rot([P.sb("W1_%d" % i, [64, 128]) for i in range(2)])
        U_t = Rot([P.sb("U_%d" % i, [128, 64]) for i in range(2)])
        bkP = Rot([P.ps("bkP%d" % i, [128, 512]) for i in range(2)])
        bkS = Rot([P.ps("bkS%d" % i, [128, 512]) for i in range(2)])
        bkI = Rot([P.ps("bkI%d" % i, [128, 512]) for i in range(2)])
        bkC = Rot([P.ps("bkC%d" % i, [128, 512]) for i in range(2)])

        def v8(t_):
            return t_.rearrange("p (h j) -> p h j", j=64)

        def bc8(t_):
            return t_.unsqueeze(2).to_broadcast([128, 8, 64])

        for ci in range(NCH):
            prev, cur, zd = prev_t.get(), cur_t.get(), zd_t.get()
            P.dma(prev[:], zp[ci * 128:ci * 128 + 128, :])
            P.dma(cur[:], zp[ci * 128 + 1:ci * 128 + 129, :])
            P.tt(zd[:], prev[:], cur[:], ALU.subtract, eng='pool')
            P.tt(zd[:], zd[:], mu_s[:], ALU.mult, eng='pool')
            P.tt(zd[:], zd[:], cur[:], ALU.add, eng='pool')
            r, k, v = zd[:, 0:512], zd[:, 512:1024], zd[:, 1024:1536]
            pb = bkP.get()
            pT = pb[0:64, 0:256].rearrange("p (a t) -> p a t", a=2)
            P.tr(pT[:, 0, :], zd[:, 1536:1600], ident)
            P.tr(pT[:, 1, :], zd[:, 1600:1664], ident)
            P.act(wT[:, 0, :], pT[:, 0, :], AF.Tanh)
            P.copy(wT[:, 1, :], pT[:, 1, :])
            pb = bkP.get()
            P.mm(pb[:], wT[:, 0, :], wup_s[:], start=True, stop=False)
            P.mm(pb[:], ones[0:1, :], rows[:, 0, :], start=False, stop=True)
            P.act(sg[:], pb[:], AF.Sigmoid)
            pb = bkP.get()
            P.mm(pb[:], wT[:, 1, :], aup_s[:], start=True, stop=False)
            P.mm(pb[:], ones[0:1, :], rows[:, 1, :], start=False, stop=True)
            P.act(av[:], pb[:], AF.Sigmoid)
            pb = bkP.get()
            P.mm(pb[:], mU, sg[:])
            P.act(Gt[:], pb[:], AF.Exp, scale=-DEC)
            P.act(Gi[:], pb[:], AF.Exp, scale=DEC)
            pb = bkP.get()
            P.mm(pb[:], mSU, sg[:])
            P.act(Gp[:], pb[:], AF.Exp, scale=-DEC)
            pb = bkP.get()
            P.mm(pb[:], mSL, sg[:])
            P.act(Gr[:], pb[:], AF.Exp, scale=-DEC)
            pb = bkP.get()
            for h in range(8):
                P.mm(pb[0:64, h:h + 1], sg[:, h * 64:(h + 1) * 64], ones[:, 0:1])
            gC = gC_t.get()
            P.act(gC[:], pb[0:64, 0:8], AF.Exp, scale=-DEC)
            P.tt(T1[:], k, kk_s[:], ALU.mult)
            P.tt(T2[:], T1[:], T1[:], ALU.mult, eng='pool')
            P.reduce(s8[:, 0, :], v8(T2[:]), ALU.add)
            P.act(s8[:, 1, :], s8[:, 0, :], AF.Sqrt)
            P.ts(s8[:, 1, :], s8[:, 1, :], 1e-12, None, ALU.max)
            P.recip(s8[:, 1, :], s8[:, 1, :])
            P.tt(v8(kap[:]), v8(T1[:]), bc8(s8[:, 1, :]), ALU.mult)
            P.stt(T2[:], av[:], -1.0, ka_s[:], ALU.add, ALU.mult)
            P.stt(kmod[:], T2[:], 1.0, k, ALU.add, ALU.mult)
            P.tt(beta[:], kap[:], av[:], ALU.mult, eng='pool')
            P.tt(T1[:], r, kmod[:], ALU.mult, eng='pool')
            P.tt(T1[:], T1[:], rk_s[:], ALU.mult, eng='pool')
            P.reduce(s8[:, 2, :], v8(T1[:]), ALU.add)
            bn = bn_t.get()
            P.tt(v8(bn[:]), v8(v), bc8(s8[:, 2, :]), ALU.mult, eng='pool')
            P.dma(bon[ci * 128:(ci + 1) * 128, :], bn[:], q='act')
            Q4, KH = Q4_t.get(), KH_t.get()
            P.tt(Q4[:, 0, :], kap[:], Gp[:], ALU.mult)
            P.tt(Q4[:, 1, :], r, Gt[:], ALU.mult, eng='pool')
            P.tt(Q4[:, 2, :], beta[:], Gi[:], ALU.mult)
            P.tt(Q4[:, 3, :], kmod[:], Gi[:], ALU.mult, eng='pool')
            P.tt(KH[:, 0, :], kmod[:], Gr[:], ALU.mult, eng='pool')
            P.stt(KH[:, 1, :], beta[:], -1.0, Gr[:], ALU.mult, ALU.mult)
            QT = QT_t.get()
            for h in range(8):
                pb = bkP.get()
                pq = pb[0:64, :].rearrange("p (q t) -> p q t", q=4)
                for q in range(4):
                    P.tr(pq[:, q, :], Q4[:, q, h * 64:(h + 1) * 64], ident)
                P.copy(QT[:, :, h, :], pq, eng=('act' if h % 2 == 0 else 'dve'))
            yst = ys_t.get()
            if dbg < 9:
                P.memset(yst[:], 0.0)
            import os as _os
            for h in range(int(_os.environ.get("RW_HEADS", "8"))):
                if dbg < 2:
                    break
                hs = slice(h * 64, (h + 1) * 64)
                ps = bkS.get()
                P.mm(ps[:, 0:256].rearrange("p (a t) -> p a t", a=2), QT[:, 0, h, :], QT[:, 2:4, h, :])
                P.mm(ps[:, 256:512].rearrange("p (a t) -> p a t", a=2), QT[:, 2, h, :], QT[:, 0:2, h, :])
                A, AT, MkT, nNb, Nk = (mats[n_].get() for n_ in ("A", "AT", "MkT", "nNb", "Nk"))
                P.stt(AT[:], ps[:, 0:128], -1.0, mSL, ALU.mult, ALU.mult)
                P.tt(MkT[:], ps[:, 128:256], mSL, ALU.mult)
                P.stt(A[:], ps[:, 256:384], -1.0, mSU, ALU.mult, ALU.mult)
                P.stt(nNb[:], ps[:, 384:512], -1.0, mU, ALU.mult, ALU.mult)
                ps3 = bkS.get()
                P.mm(ps3[:, 0:128], QT[:, 3, h, :], QT[:, 1, h, :])
                P.tt(Nk[:], ps3[:, 0:128], mU, ALU.mult)
                if dbg < 2.05:
                    continue
                Pm = mats["Pm"].get()
                P.tt(Pm[:], A[:], ident, ALU.add, eng='pool')
                Bm, BTm = A, AT
                for kk_ in range(1, 7):
                    if dbg < 3 and kk_ > int(round((dbg - 2) * 10)) - 1:
                        break
                    pi = bkI.get()
                    last = (kk_ == 6)
                    if not last:
                        P.mm(pi[:, 0:128], BTm[:], Bm[:])
                    P.mm(pi[:, 128:256], Bm[:], BTm[:])
                    IBT = mats["IBT"].get()
                    P.tt(IBT[:], pi[:, 128:256], ident, ALU.add)
                    if not last:
                        Bn, BTn = mats["B"].get(), mats["BT"].get()
                        P.copy(Bn[:], pi[:, 0:128])
                        P.copy(BTn[:], pi[:, 128:256])
                    P.mm(pi[:, 256:384], IBT[:], Pm[:])
                    Pn = mats["Pm"].get()
                    P.copy(Pn[:], pi[:, 256:384])
                    Pm = Pn
                    if not last:
                        Bm, BTm = Bn, BTn
                if dbg < 4:
                    continue
                pw = bkS.get()
                P.mm(pw[0:64, 0:128], Q4[:, 0, hs], Pm[:])
                P.mm(pw[:, 128:256], MkT[:], Pm[:])
                W1, W2 = W1_t.get(), mats["W2"].get()
                P.copy(W1[:], pw[0:64, 0:128], eng='act')
                P.copy(W2[:], pw[:, 128:256], eng='act')
                if dbg < 5:
                    continue
                pc = bkC.get()
                vh = zd[:, 1024 + h * 64:1024 + (h + 1) * 64]
                P.mm(pc[:, 0:64], W2[:], vh, start=True, stop=False)
                P.mm(pc[:, 0:64], W1[:], ST[h][:], start=False, stop=True)
                U = U_t.get()
                P.copy(U[:], pc[:, 0:64])
                P.mm(pc[:, 64:128], QT[:, 1, h, :], ST[h][:], start=True, stop=False)
                P.mm(pc[:, 64:128], Nk[:], vh, start=False, stop=False)
                P.mm(pc[:, 64:128], nNb[:], U[:], start=False, stop=True)
                P.mm(pc[0:64, 128:192], KH[:, 0, hs], vh, start=True, stop=False)
                P.mm(pc[0:64, 128:192], KH[:, 1, hs], U[:], start=False, stop=True)
                P.copy(yst[:, hs], pc[:, 64:128], eng='act')
                P.stt(ST[h][:], ST[h][:], gC[:, h:h + 1], pc[0:64, 128:192], ALU.mult, ALU.add)
            P.dma(ys[ci * 128:(ci + 1) * 128, :], yst[:], q='act')
        P.finish([ys, bon])
    return nc


def build_rwkv_post(T):
    nc = new_nc()
    ysf = dram_in(nc, "ysf", [T, 512]); ysb = dram_in(nc, "ysb", [T, 512])
    bf = dram_in(nc, "bf", [T, 512]); bb = dram_in(nc, "bb", [T, 512])
    gdT = dram_in(nc, "gdT", [128, T])
    gup = dram_in(nc, "gup", [128, 512])
    lnw = dram_in(nc, "lnw", [512]); lnb = dram_in(nc, "lnb", [512])
    ya = dram_out(nc, "ya", [T, 512])
    with ExitStack() as st:
        P = Prog(nc, st)
        gup_s = P.sb("gup_s", [128, 512]); P.dma(gup_s[:], gup)
        lnw_s = P.sb("lnw_s", [128, 512]); P.dma(lnw_s[:], lnw.partition_broadcast(128))
        lnb_s = P.sb("lnb_s", [128, 512]); P.dma(lnb_s[:], lnb.partition_broadcast(128))
        gd_s = P.sb("gd_s", [128, T]); P.dma(gd_s[:], gdT)
        P.act(gd_s[:], gd_s[:], AF.Sigmoid)
        ins_t = [Rot([P.sb("in%d_%d" % (q, i), [128, 512]) for i in range(2)]) for q in range(4)]
        y_t = Rot([P.sb("y%d" % i, [128, 512]) for i in range(2)])
        sq = P.sb("sq", [128, 512])
        s8 = P.sb("s8", [128, 3, 8])
        o_t = Rot([P.sb("o%d" % i, [128, 512]) for i in range(2)])
        psg = Rot([P.ps("psg%d" % i, [128, 512]) for i in range(2)])

        def v8(t_):
            return t_.rearrange("p (h j) -> p h j", j=64)

        def bc8(t_):
            return t_.unsqueeze(2).to_broadcast([128, 8, 64])

        for t0 in range(0, T, 128):
            tl = [r_.get() for r_ in ins_t]
            for q, src in enumerate((ysf, ysb, bf, bb)):
                P.dma(tl[q][:], src[t0:t0 + 128, :])
            y = y_t.get()
            P.tt(y[:], tl[0][:], tl[1][:], ALU.add)
            P.reduce(s8[:, 0, :], v8(y[:]), ALU.add)
            P.ts(s8[:, 0, :], s8[:, 0, :], -1.0 / 64, None, ALU.mult)
            P.tt(v8(y[:]), v8(y[:]), bc8(s8[:, 0, :]), ALU.add)
            P.tt(sq[:], y[:], y[:], ALU.mult, eng='pool')
            P.reduce(s8[:, 1, :], v8(sq[:]), ALU.add)
            P.ts(s8[:, 1, :], s8[:, 1, :], 1.0 / 64, 64e-5, ALU.mult, ALU.add)
            P.act(s8[:, 1, :], s8[:, 1, :], AF.Sqrt)
            P.recip(s8[:, 1, :], s8[:, 1, :])
            P.tt(v8(y[:]), v8(y[:]), bc8(s8[:, 1, :]), ALU.mult)
            P.tt(y[:], y[:], lnw_s[:], ALU.mult, eng='pool')
            P.tt(y[:], y[:], lnb_s[:], ALU.add, eng='pool')
            P.tt(tl[2][:], tl[2][:], tl[3][:], ALU.add, eng='pool')
            P.tt(y[:], y[:], tl[2][:], ALU.add, eng='pool')
            pg = psg.get()
            P.mm(pg[:], gd_s[:, t0:t0 + 128], gup_s[:])
            o = o_t.get()
            P.tt(o[:], pg[:], y[:], ALU.mult)
            P.dma(ya[t0:t0 + 128, :], o[:], q='act')
        P.finish([ya])
    return nc


TSH = L // 2


def _shardT(a):
    return [np.ascontiguousarray(a[c // 2, (c % 2) * TSH:(c % 2 + 1) * TSH].T) for c in range(NCORES)]


def _unshardT(lst):
    out = np.empty((B, L, lst[0].shape[0]), np.float32)
    for c in range(NCORES):
        out[c // 2, (c % 2) * TSH:(c % 2 + 1) * TSH] = lst[c].T
    return out


def _shard_tok(a):
    return [np.ascontiguousarray(a[c // 2, (c % 2) * TSH:(c % 2 + 1) * TSH]) for c in range(NCORES)]


def _unshard_tok(lst):
    out = np.empty((B, L, lst[0].shape[1]), np.float32)
    for c in range(NCORES):
        out[c // 2, (c % 2) * TSH:(c % 2 + 1) * TSH] = lst[c]
    return out


_NC_CACHE = {}


def _nc(key, fn):
    if key not in _NC_CACHE:
        _NC_CACHE[key] = fn()
    return _NC_CACHE[key]


def linear_launch(aT_sh, W, g=None, mode='plain', resT_sh=None, W2=None, a2T_sh=None):
    K_, N_ = W.shape
    K2 = 0 if W2 is None else W2.shape[0]
    nc = _nc(("lin", K_, N_, g is not None, mode, K2), lambda: build_linear(TSH, K_, N_, g is not None, mode, K2))
    ins = []
    for c in range(NCORES):
        m = {"aT": aT_sh[c], "W": np.ascontiguousarray(W)}
        if g is not None:
            m["g"] = np.ascontiguousarray(g)
        if mode == 'res':
            m["resT"] = resT_sh[c]
        if mode == 'ple':
            m["W2"] = np.ascontiguousarray(W2)
            m["a2T"] = a2T_sh[c]
        ins.append(m)
    return [r["oT"] for r in run(nc, ins)]


def ffn_launch(hT_sh, g, Wu, cw, cb, Wd):
    nc = _nc(("ffn",), lambda: build_ffn(TSH))
    ins = []
    for c in range(NCORES):
        left = hT_sh[c - 1][:, -1:] if c % 2 == 1 else np.zeros((D, 1), np.float32)
        right = hT_sh[c + 1][:, :1] if c % 2 == 0 else np.zeros((D, 1), np.float32)
        ins.append({"hTp": np.ascontiguousarray(np.concatenate([left, hT_sh[c], right], axis=1)), "g": np.ascontiguousarray(g),
                    "Wu": np.ascontiguousarray(Wu), "cw": np.ascontiguousarray(cw), "cb": np.ascontiguousarray(cb),
                    "Wd": np.ascontiguousarray(Wd)})
    return [r["oT"] for r in run(nc, ins)]


def rwkv_launch(z, p):
    tri = rwkv_consts()
    ins = []
    for c in range(NCORES):
        b, dr = c // 2, c % 2
        zs = z[b, :, :RS]
        if dr == 1:
            zs = zs[::-1]
        zp = np.concatenate([np.zeros((1, RS), np.float32), zs], 0)
        ins.append({"zp": np.ascontiguousarray(zp), "mu": np.ascontiguousarray(p["rwkv_mu"][0, dr]),
                    "w0": np.ascontiguousarray(p["rwkv_w0"][0, dr]), "wup": np.ascontiguousarray(p["rwkv_w_up"][0, dr]),
                    "a0": np.ascontiguousarray(p["rwkv_a0"][0, dr]), "aup": np.ascontiguousarray(p["rwkv_a_up"][0, dr]),
                    "kk": np.ascontiguousarray(p["rwkv_k_k"][0]), "ka": np.ascontiguousarray(p["rwkv_k_a"][0]),
                    "rk": np.ascontiguousarray(p["rwkv_r_k"][0].reshape(-1)), "tri": tri})
    res = run(_nc(("rwkv",), lambda: build_rwkv(L)), ins)
    ysf = np.stack([res[2 * b]["ys"] for b in range(B)])
    ysb = np.stack([res[2 * b + 1]["ys"][::-1] for b in range(B)])
    bf = np.stack([res[2 * b]["bon"] for b in range(B)])
    bb = np.stack([res[2 * b + 1]["bon"][::-1] for b in range(B)])
    gdT = _shardT(z[:, :, RS:RS + 128])
    sh = [_shard_tok(a) for a in (ysf, ysb, bf, bb)]
    ins = [{"ysf": sh[0][c], "ysb": sh[1][c], "bf": sh[2][c], "bb": sh[3][c], "gdT": gdT[c],
            "gup": np.ascontiguousarray(p["rwkv_g_up"][0]), "lnw": np.ascontiguousarray(p["rwkv_ln_w"][0]),
            "lnb": np.ascontiguousarray(p["rwkv_ln_b"][0])} for c in range(NCORES)]
    res = run(_nc(("rwkvpost",), lambda: build_rwkv_post(TSH)), ins)
    return _unshard_tok([r["ya"] for r in res])


def hyena_launch(z, p):
    C = hyena_consts()
    zz = z[:, :, 1792:]
    zpad = np.pad(zz, ((0, 0), (1, 1), (0, 0)))
    idx = (np.arange(64)[:, None] * 128 + np.arange(130)[None, :])
    sw, sbias, w4 = p["hy_short_w"][0], p["hy_short_b"][0], p["hy_f_w4"][0]
    ins = []
    for c in range(NCORES):
        hz = np.empty((B, 3, 64, HG, 130), np.float32)
        for wh in range(3):
            cols = wh * 512 + c * HG + np.arange(HG)
            hz[:, wh] = zpad[:, :, cols][:, idx, :].transpose(0, 1, 3, 2)
        cw = np.stack([sw[:, wh * 512 + c * HG: wh * 512 + (c + 1) * HG] for wh in range(3)])
        cb = np.stack([sbias[wh * 512 + c * HG: wh * 512 + (c + 1) * HG] for wh in range(3)])
        w4c = np.concatenate([w4[:, dd * 512 + c * HG: dd * 512 + (c + 1) * HG] for dd in range(2)], axis=1)
        m = {"hz": hz, "cw": np.ascontiguousarray(cw.reshape(-1)), "cbv": np.ascontiguousarray(cb.reshape(-1)),
             "hb": np.ascontiguousarray(p["hy_bias"][0][c * HG:(c + 1) * HG]),
             "w1": np.ascontiguousarray(p["hy_f_w1"][0]), "b1": np.ascontiguousarray(p["hy_f_b1"][0]),
             "w2": np.ascontiguousarray(p["hy_f_w2"][0]), "b2": np.ascontiguousarray(p["hy_f_b2"][0]),
             "w3": np.ascontiguousarray(p["hy_f_w3"][0]), "b3": np.ascontiguousarray(p["hy_f_b3"][0]),
             "w4": np.ascontiguousarray(w4c), "fr": np.ascontiguousarray(p["hy_f_freq"][0]), "win": hyena_window(c)}
        for k_ in ("F1", "TW1", "TW2", "F3", "G1", "G2", "FI", "featsT"):
            m[k_] = C[k_]
        ins.append(m)
    res = run(_nc(("hyena",), build_hyena), ins)
    yb = np.empty((B, L, 512), np.float32)
    for c in range(NCORES):
        yb[:, :, c * HG:(c + 1) * HG] = res[c]["o"].transpose(0, 1, 3, 2).reshape(B, L, HG)
    return yb


def na_launch(z, p):
    ins = []
    for c in range(NCORES):
        b, hg = c // 2, c % 2
        ins.append({"qT": np.ascontiguousarray(z[b, :, hg * 512:(hg + 1) * 512].T),
                    "kT": np.ascontiguousarray(z[b, :, D + hg * 512:D + (hg + 1) * 512].T),
                    "v": np.ascontiguousarray(z[b, :, 2 * D + hg * 512:2 * D + (hg + 1) * 512]),
                    "bias": na_bias_table(p["na_rpb"][0], hg), "qg": np.ascontiguousarray(p["na_q_g"][0]),
                    "kg": np.ascontiguousarray(p["na_k_g"][0])})
    res = run(_nc(("na",), build_na), ins)
    att = np.empty((B, L, D), np.float32)
    for c in range(NCORES):
        att[c // 2, :, (c % 2) * 512:(c % 2 + 1) * 512] = res[c]["o"]
    return att


def kernel(**p):
    p = {k: np.asarray(v, dtype=np.float32) for k, v in p.items()}
    hT = _shardT(p["x"])
    zT = linear_launch(hT, p["mix_w_in"][0], g=p["mix_norm"][0])
    z = _unshardT(zT)
    ya = rwkv_launch(z, p)
    yb = hyena_launch(z, p)
    yT = _shardT(np.concatenate([ya, yb], axis=-1))
    hT = linear_launch(yT, p["mix_w_out"][0], mode='res', resT_sh=hT)
    hT = ffn_launch(hT, p["ffn_norm"][0], p["ffn_w_up"][0], p["ffn_conv_w"][0], p["ffn_conv_b"][0], p["ffn_w_down"][0])
    hT = linear_launch(hT, p["ple_w_gate"][0], g=p["ple_norm"][0], mode='ple', W2=p["ple_w_proj"][0],
                       a2T_sh=_shardT(p["p"][0]))
    zT = linear_launch(hT, p["na_w_qkv"][0], g=p["na_norm"][0])
    att = na_launch(_unshardT(zT), p)
    hT = linear_launch(_shardT(att), p["na_w_out"][0], mode='res', resT_sh=hT)
    hT = ffn_launch(hT, p["ffn_norm"][1], p["ffn_w_up"][1], p["ffn_conv_w"][1], p["ffn_conv_b"][1], p["ffn_w_down"][1])
    hT = linear_launch(hT, p["ple_w_gate"][1], g=p["ple_norm"][1], mode='ple', W2=p["ple_w_proj"][1],
                       a2T_sh=_shardT(p["p"][1]))
    return _unshardT(hT)
```

```python
from contextlib import ExitStack
import math
import numpy as np
import concourse.bass as bass
import concourse.mybir as mybir
from concourse.bass_utils import run_bass_kernel_spmd

F32 = mybir.dt.float32
BF16 = mybir.dt.bfloat16
AF = mybir.ActivationFunctionType
ALU = mybir.AluOpType
AX = mybir.AxisListType

NCORES = 8
D = 1024
B = 4
L = 8192
EPS = 1e-6


class Prog:
    NDMA = 24
    SEM_LIMIT = 6000

    def __init__(self, nc, stack):
        self.nc = nc
        self.st = stack
        self.E = {'pe': nc.tensor, 'dve': nc.vector, 'act': nc.scalar, 'pool': nc.gpsimd, 'sp': nc.sync}
        self.sem = {e: stack.enter_context(nc.semaphore("s_" + e)) for e in ('pe', 'dve', 'act', 'pool')}
        self.dsem = [stack.enter_context(nc.semaphore("d%d" % i)) for i in range(self.NDMA)]
        self.dcnt = [0] * self.NDMA
        self.dnext = 0
        self.cnt = {e: 0 for e in self.sem}
        self.gen = {e: 0 for e in self.sem}
        self.allsem = {(e, 0): self.sem[e] for e in self.sem}
        self.seen = {e: {} for e in self.E}
        self.lastw = {}
        self.readers = {}
        self.nuniq = 0
        self.psum_names = set()

    def sb(self, name, shape, dt=F32):
        return self.st.enter_context(self.nc.sbuf_tensor(name, list(shape), dt))

    def ps(self, name, shape, dt=F32):
        self.psum_names.add(name)
        return self.st.enter_context(self.nc.psum_tensor(name, list(shape), dt))

    @staticmethod
    def key(ap):
        if isinstance(ap, str):
            return ap
        return ap.tensor.name

    def _need(self, eng, ev, waits):
        if ev is None:
            return
        src, n = ev
        if src[0] == eng and eng == 'pe':
            return
        if self.seen[eng].get(src, 0) >= n:
            return
        waits[src] = max(waits.get(src, 0), n)

    def _deps(self, eng, reads, writes):
        waits = {}
        for k in reads:
            self._need(eng, self.lastw.get(k), waits)
            if k in self.psum_names:
                for ev in self.readers.get(k, {}).items():
                    if ev[0][0] != eng:
                        self._need(eng, ev, waits)
        for k in writes:
            self._need(eng, self.lastw.get(k), waits)
            for ev in self.readers.get(k, {}).items():
                self._need(eng, ev, waits)
        for src, n in waits.items():
            s = self.dsem[src[1]] if src[0] == 'd' else self.allsem[src]
            self.E[eng].wait_ge(s, n)
            self.seen[eng][src] = n

    def _commit(self, ev, reads, writes):
        for k in writes:
            self.lastw[k] = ev
            self.readers[k] = {}
        for k in reads:
            r = self.readers.setdefault(k, {})
            r[ev[0]] = max(r.get(ev[0], 0), ev[1])

    def op(self, eng, fn, reads, writes):
        reads = [self.key(a) for a in reads if a is not None and not isinstance(a, (int, float))]
        writes = [self.key(a) for a in writes if a is not None]
        self._deps(eng, reads, writes)
        if self.cnt[eng] >= self.SEM_LIMIT:
            self.gen[eng] += 1
            self.cnt[eng] = 0
            self.sem[eng] = self.st.enter_context(self.nc.semaphore("s_%s_%d" % (eng, self.gen[eng])))
            self.allsem[(eng, self.gen[eng])] = self.sem[eng]
        ins = fn(self.E[eng])
        self.cnt[eng] += 1
        ins.then_inc(self.sem[eng], 1)
        self._commit(((eng, self.gen[eng]), self.cnt[eng]), reads, writes)
        return ins

    def dma(self, out, in_, q='sp', **kw):
        reads = [self.key(in_)]
        writes = [self.key(out)]
        i = self.dnext
        self.dnext = (self.dnext + 1) % self.NDMA
        if self.dcnt[i] > 0:
            w = {}
            self._need(q, (('d', i), self.dcnt[i]), w)
            for src, n in w.items():
                self.E[q].wait_ge(self.dsem[i], n)
                self.seen[q][src] = n
        self._deps(q, reads, writes)
        ins = self.E[q].dma_start(out=out, in_=in_, **kw)
        self.dcnt[i] += 16
        ins.then_inc(self.dsem[i], 16)
        self._commit((('d', i), self.dcnt[i]), reads, writes)
        return ins

    def barrier(self):
        for eng in self.E:
            for src in self.sem:
                key = (src, self.gen[src])
                if src != eng and self.cnt[src] > self.seen[eng].get(key, 0):
                    self.E[eng].wait_ge(self.sem[src], self.cnt[src])
                    self.seen[eng][key] = self.cnt[src]
            for i in range(self.NDMA):
                if self.dcnt[i] > self.seen[eng].get(('d', i), 0):
                    self.E[eng].wait_ge(self.dsem[i], self.dcnt[i])
                    self.seen[eng][('d', i)] = self.dcnt[i]

    def finish(self, keys):
        for eng in ('sp', 'pool'):
            self._deps(eng, [self.key(k) for k in keys], [])

    def mm(self, out, lhsT, rhs, start=True, stop=True):
        return self.op('pe', lambda e: e.matmul(out, lhsT, rhs, start=start, stop=stop), [lhsT, rhs], [out])

    def tr(self, out, in_, ident):
        return self.op('pe', lambda e: e.transpose(out, in_, ident), [in_, ident], [out])

    def act(self, out, in_, func, bias=0.0, scale=1.0, accum_out=None):
        kw = {}
        if accum_out is not None:
            kw['accum_out'] = accum_out
        return self.op('act', lambda e: e.activation(out, in_, func, bias=bias, scale=scale, **kw),
                       [in_, bias, scale], [out, accum_out])

    def tt(self, out, a, b, op, eng='dve'):
        return self.op(eng, lambda e: e.tensor_tensor(out, a, b, op), [a, b], [out])

    def ts(self, out, a, s1, s2, op0, op1=None, eng='dve', accum_out=None):
        kw = {}
        if accum_out is not None:
            kw['accum_out'] = accum_out
        if op1 is None:
            return self.op(eng, lambda e: e.tensor_scalar(out, a, s1, None, op0, **kw), [a, s1], [out, accum_out])
        return self.op(eng, lambda e: e.tensor_scalar(out, a, s1, s2, op0, op1, **kw), [a, s1, s2],
                       [out, accum_out])

    def stt(self, out, a, s, b, op0, op1, eng='dve'):
        return self.op(eng, lambda e: e.scalar_tensor_tensor(out, a, s, b, op0, op1), [a, s, b], [out])

    def copy(self, out, a, eng='dve'):
        if eng == 'act':
            return self.op('act', lambda e: e.copy(out, a), [a], [out])
        return self.op(eng, lambda e: e.tensor_copy(out, a), [a], [out])

    def recip(self, out, a):
        return self.op('dve', lambda e: e.reciprocal(out, a), [a], [out])

    def memset(self, out, v, eng='dve'):
        return self.op(eng, lambda e: e.memset(out, v), [], [out])

    def reduce(self, out, a, op, axis=AX.X, eng='dve'):
        return self.op(eng, lambda e: e.tensor_reduce(out, a, axis, op), [a], [out])


class Rot:
    def __init__(self, tiles):
        self.t = tiles
        self.i = 0

    def get(self):
        t = self.t[self.i % len(self.t)]
        self.i += 1
        return t


def new_nc():
    return bass.Bass("TRN2", target_bir_lowering=False)


def dram_in(nc, name, shape, dt=F32):
    return nc.dram_tensor(name, list(shape), dt, kind="ExternalInput").ap()


def dram_out(nc, name, shape, dt=F32):
    return nc.dram_tensor(name, list(shape), dt, kind="ExternalOutput").ap()


def load_weight_bf16(P, w_sb, w_dram, K, N, stg, engs=('dve', 'pool')):
    wv = w_dram.rearrange("(kc p) n -> p kc n", p=128)
    i = 0
    for kc in range(K // 128):
        for n0 in range(0, N, 2048):
            n1 = min(N, n0 + 2048)
            s = stg.get()
            P.dma(s[:, 0:n1 - n0], wv[:, kc, n0:n1])
            P.copy(w_sb[:, kc, n0:n1], s[:, 0:n1 - n0], eng=engs[i % len(engs)])
            i += 1


def load_cols(P, dst, vec_dram, n):
    P.dma(dst[:, 0:n], vec_dram.rearrange("(c p) -> p c", p=128), allow_slow_non_contiguous=True)


def rmsnorm_T(P, xn, aT, g_sb, ones_bf, sq, ssq_ps, rstd, KC, n, dmodel):
    for kc in range(KC):
        P.act(sq[:, kc, 0:n], aT[:, kc, 0:n], AF.Square)
    for kc in range(KC):
        P.mm(ssq_ps[:, 0:n], ones_bf[:], sq[:, kc, 0:n], start=(kc == 0), stop=(kc == KC - 1))
    P.ts(rstd[:, 0:n], ssq_ps[:, 0:n], 1.0 / dmodel, EPS, ALU.mult, ALU.add)
    P.act(rstd[:, 0:n], rstd[:, 0:n], AF.Sqrt)
    P.recip(rstd[:, 0:n], rstd[:, 0:n])
    for kc in range(KC):
        P.stt(xn[:, kc, 0:n], aT[:, kc, 0:n], g_sb[:, kc:kc + 1], rstd[:, 0:n], ALU.mult, ALU.mult)


def build_linear(T, K, N, norm, mode, K2=0):
    nc = new_nc()
    aT = dram_in(nc, "aT", [K, T])
    W = dram_in(nc, "W", [K, N])
    g = dram_in(nc, "g", [K]) if norm else None
    resT = dram_in(nc, "resT", [N, T]) if mode == 'res' else None
    if mode == 'ple':
        W2 = dram_in(nc, "W2", [K2, N])
        a2T = dram_in(nc, "a2T", [K2, T])
    oT = dram_out(nc, "oT", [N, T])
    KC, NCH, n = K // 128, N // 128, 512
    with ExitStack() as st:
        P = Prog(nc, st)
        w_sb = P.sb("w_sb", [128, KC, N], BF16)
        stg = Rot([P.sb("wstg%d" % i, [128, 2048]) for i in range(2)])
        load_weight_bf16(P, w_sb, W, K, N, stg)
        if mode == 'ple':
            w2_sb = P.sb("w2_sb", [128, K2 // 128, N], BF16)
            load_weight_bf16(P, w2_sb, W2, K2, N, stg)
        ones_bf = P.sb("ones_bf", [128, 128], BF16)
        P.memset(ones_bf[:], 1.0)
        if norm:
            g_sb = P.sb("g_sb", [128, KC])
            load_cols(P, g_sb, g, KC)
        a_t = Rot([P.sb("a_t%d" % i, [128, KC, n]) for i in range(2)])
        xn_t = Rot([P.sb("xn_t%d" % i, [128, KC, n], BF16) for i in range(2)])
        sq = P.sb("sq", [128, KC, n], BF16)
        rstd = P.sb("rstd", [128, n])
        ssq_ps = P.ps("ssq_ps", [128, n])
        pss = Rot([P.ps("ps%d" % i, [128, n]) for i in range(4)])
        outs = Rot([P.sb("o%d" % i, [128, n]) for i in range(4)])
        if mode == 'res':
            res_t = Rot([P.sb("res%d" % i, [128, n]) for i in range(3)])
        if mode == 'ple':
            a2_t = Rot([P.sb("a2_t%d" % i, [128, K2 // 128, n]) for i in range(2)])
            a2b_t = Rot([P.sb("a2b_t%d" % i, [128, K2 // 128, n], BF16) for i in range(2)])
            ps2s = Rot([P.ps("ps2_%d" % i, [128, n]) for i in range(2)])
            sg_t = Rot([P.sb("sg%d" % i, [128, n]) for i in range(2)])
        aTv = aT.rearrange("(kc p) t -> p kc t", p=128)
        for t0 in range(0, T, n):
            a = a_t.get()
            P.dma(a[:], aTv[:, :, t0:t0 + n])
            xn = xn_t.get()
            if norm:
                rmsnorm_T(P, xn, a, g_sb, ones_bf, sq, ssq_ps, rstd, KC, n, K)
            else:
                for kc in range(KC):
                    P.copy(xn[:, kc, :], a[:, kc, :], eng=('dve' if kc % 2 == 0 else 'pool'))
            if mode == 'ple':
                a2 = a2_t.get()
                P.dma(a2[:], a2T.rearrange("(kc p) t -> p kc t", p=128)[:, :, t0:t0 + n])
                a2b = a2b_t.get()
                P.copy(a2b[:], a2[:], eng='pool')
            for c in range(NCH):
                ps = pss.get()
                for kc in range(KC):
                    P.mm(ps[:], w_sb[:, kc, c * 128:(c + 1) * 128], xn[:, kc, :], start=(kc == 0), stop=(kc == KC - 1))
                o = outs.get()
                if mode == 'plain':
                    if c % 2 == 0:
                        P.copy(o[:], ps[:], eng='dve')
                    else:
                        P.copy(o[:], ps[:], eng='act')
                elif mode == 'res':
                    r = res_t.get()
                    P.dma(r[:], resT[c * 128:(c + 1) * 128, t0:t0 + n])
                    P.tt(o[:], ps[:], r[:], ALU.add)
                else:
                    ps2 = ps2s.get()
                    for kc in range(K2 // 128):
                        P.mm(ps2[:], w2_sb[:, kc, c * 128:(c + 1) * 128], a2b[:, kc, :], start=(kc == 0),
                             stop=(kc == K2 // 128 - 1))
                    sg = sg_t.get()
                    P.act(sg[:], ps[:], AF.Sigmoid)
                    P.tt(sg[:], sg[:], ps2[:], ALU.mult)
                    P.tt(o[:], sg[:], a[:, c, :], ALU.add, eng='pool')
                P.dma(oT[c * 128:(c + 1) * 128, t0:t0 + n], o[:], q='act')
        P.finish([oT])
    return nc


DFF = 2816


def build_ffn(T):
    nc = new_nc()
    hTp = dram_in(nc, "hTp", [D, T + 2])
    g = dram_in(nc, "g", [D])
    Wu = dram_in(nc, "Wu", [D, 2 * DFF])
    cw = dram_in(nc, "cw", [3, 2 * DFF])
    cb = dram_in(nc, "cb", [2 * DFF])
    Wd = dram_in(nc, "Wd", [DFF, D])
    oT = dram_out(nc, "oT", [D, T])
    KC, n, FC = D // 128, 256, DFF // 128
    with ExitStack() as st:
        P = Prog(nc, st)
        wu_sb = P.sb("wu_sb", [128, KC, 2 * DFF], BF16)
        wd_sb = P.sb("wd_sb", [128, FC, D], BF16)
        stg = Rot([P.sb("wstg%d" % i, [128, 2048]) for i in range(2)])
        load_weight_bf16(P, wu_sb, Wu, D, 2 * DFF, stg)
        load_weight_bf16(P, wd_sb, Wd, DFF, D, stg)
        ones_bf = P.sb("ones_bf", [128, 128], BF16)
        P.memset(ones_bf[:], 1.0)
        g_sb = P.sb("g_sb", [128, KC])
        load_cols(P, g_sb, g, KC)
        cw_sb = P.sb("cw_sb", [128, 3, 2 * FC])
        for j in range(3):
            load_cols(P, cw_sb[:, j, :], cw[j], 2 * FC)
        cb_sb = P.sb("cb_sb", [128, 2 * FC])
        load_cols(P, cb_sb, cb, 2 * FC)
        a_t = Rot([P.sb("a_t%d" % i, [128, KC, n + 2]) for i in range(2)])
        xn = P.sb("xn", [128, KC, n + 2], BF16)
        sq = P.sb("sq", [128, KC, n + 2], BF16)
        rstd = P.sb("rstd", [128, n + 2])
        ssq_ps = P.ps("ssq_ps", [128, n + 2])
        pss = Rot([P.ps("ps%d" % i, [128, n + 2]) for i in range(4)])
        psd = Rot([P.ps("psd%d" % i, [128, n]) for i in range(2)])
        gT = P.sb("gT", [128, FC, n], BF16)
        ca_t = Rot([P.sb("ca%d" % i, [128, n]) for i in range(2)])
        cb_t = Rot([P.sb("cbv%d" % i, [128, n]) for i in range(2)])
        t1_t = Rot([P.sb("t1_%d" % i, [128, n]) for i in range(2)])
        t2_t = Rot([P.sb("t2_%d" % i, [128, n]) for i in range(2)])
        outs = Rot([P.sb("o%d" % i, [128, n]) for i in range(3)])
        hv = hTp.rearrange("(kc p) t -> p kc t", p=128)

        def conv(dst, ps, col):
            P.act(dst[:], ps[:, 1:n + 1], AF.Identity, bias=cb_sb[:, col:col + 1], scale=cw_sb[:, 1, col:col + 1])
            P.stt(dst[:], ps[:, 0:n], cw_sb[:, 0, col:col + 1], dst[:], ALU.mult, ALU.add)
            P.stt(dst[:], ps[:, 2:n + 2], cw_sb[:, 2, col:col + 1], dst[:], ALU.mult, ALU.add)

        for t0 in range(0, T, n):
            a = a_t.get()
            P.dma(a[:], hv[:, :, t0:t0 + n + 2])
            rmsnorm_T(P, xn, a, g_sb, ones_bf, sq, ssq_ps, rstd, KC, n + 2, D)
            for fc in range(FC):
                psa, psb = pss.get(), pss.get()
                for kc in range(KC):
                    P.mm(psa[:], wu_sb[:, kc, fc * 128:(fc + 1) * 128], xn[:, kc, :], start=(kc == 0),
                         stop=(kc == KC - 1))
                for kc in range(KC):
                    P.mm(psb[:], wu_sb[:, kc, DFF + fc * 128:DFF + (fc + 1) * 128], xn[:, kc, :], start=(kc == 0),
                         stop=(kc == KC - 1))
                ca, cbv, t1, t2 = ca_t.get(), cb_t.get(), t1_t.get(), t2_t.get()
                conv(ca, psa, fc)
                conv(cbv, psb, FC + fc)
                P.act(t1[:], ca[:], AF.Square)
                P.ts(t1[:], t1[:], 0.044715, 1.0, ALU.mult, ALU.add, eng='pool')
                P.tt(t1[:], t1[:], ca[:], ALU.mult, eng='pool')
                P.act(t2[:], t1[:], AF.Sigmoid, scale=1.5957691216057308)
                P.tt(t2[:], t2[:], ca[:], ALU.mult, eng='pool')
                P.tt(gT[:, fc, :], t2[:], cbv[:], ALU.mult, eng='pool')
            for c in range(KC):
                ps = psd.get()
                for fc in range(FC):
                    P.mm(ps[:], wd_sb[:, fc, c * 128:(c + 1) * 128], gT[:, fc, :], start=(fc == 0), stop=(fc == FC - 1))
                o = outs.get()
                P.tt(o[:], ps[:], a[:, c, 1:n + 1], ALU.add)
                P.dma(oT[c * 128:(c + 1) * 128, t0:t0 + n], o[:], q='sp')
        P.finish([oT])
    return nc


def run(nc, in_maps, trace=False):
    res = run_bass_kernel_spmd(nc, in_maps, core_ids=list(range(NCORES)), trace=trace)
    if trace:
        print("exec_time_ns", res.exec_time_ns, flush=True)
    return res.results


GW = 64
NROWS = L // GW


def build_na(dbg_rows=NROWS, dbg_pv=True):
    nc = new_nc()
    qT = dram_in(nc, "qT", [512, L])
    kT = dram_in(nc, "kT", [512, L])
    v = dram_in(nc, "v", [L, 512])
    bias = dram_in(nc, "bias", [19 * 64, 512])
    qg = dram_in(nc, "qg", [64])
    kg = dram_in(nc, "kg", [64])
    o = dram_out(nc, "o", [L, 512])
    n = 512
    NT = L // n
    with ExitStack() as st:
        P = Prog(nc, st)
        bd = P.sb("bd", [128, 128], BF16)
        P.memset(bd[:], 0.0)
        P.memset(bd[0:64, 0:64], 1.0)
        P.memset(bd[64:128, 64:128], 1.0)
        g2 = P.sb("g2", [128, 2])
        for hh in range(2):
            P.dma(g2[hh * 64:(hh + 1) * 64, 0:1], qg.rearrange("(p o) -> p o", o=1))
            P.dma(g2[hh * 64:(hh + 1) * 64, 1:2], kg.rearrange("(p o) -> p o", o=1))
        P.ts(g2[:, 0:1], g2[:, 0:1], 0.125, None, ALU.mult)
        bias_sb = P.sb("bias_sb", [128, 16, 512])
        for dr0 in range(14):
            P.dma(bias_sb[:, dr0, :], bias[dr0 * 64:dr0 * 64 + 128, :])
        for x in range(2):
            P.dma(bias_sb[:, 14 + x, :], bias[(15 + 2 * x) * 64:(15 + 2 * x) * 64 + 128, :])
        kTn = P.sb("kTn", [128, 4, L], BF16)
        V1 = P.sb("V1", [128, L // 128, 8, 65], BF16)
        P.memset(V1[:, :, :, 64:65], 1.0, eng='pool')
        raw = Rot([P.sb("raw%d" % i, [128, 4, n]) for i in range(1)])
        sq = P.sb("sq", [128, 4, n], BF16)
        rstd = P.sb("rstd", [128, 4, n])
        ssq = Rot([P.ps("ssq%d" % i, [128, n]) for i in range(2)])
        qTn_t = Rot([P.sb("qz%d" % i, [128, 8, n], BF16) for i in range(2)])
        for t_ in qTn_t.t:
            P.memset(t_[:], 0.0, eng='pool')
        vst = raw

        def headnorm(dst, src_dram, t0, gcol, sep=False):
            r = raw.get()
            P.dma(r[:], src_dram.rearrange("(c p) t -> p c t", p=128)[:, :, t0:t0 + n])
            for c in range(4):
                P.act(sq[:, c, :], r[:, c, :], AF.Square)
            for c in range(4):
                ps = ssq.get()
                P.mm(ps[:], bd[:], sq[:, c, :])
                P.ts(rstd[:, c, :], ps[:], 1.0 / 64, EPS, ALU.mult, ALU.add)
            P.act(rstd[:], rstd[:], AF.Sqrt)
            P.recip(rstd[:], rstd[:])
            for c in range(4):
                if sep:
                    for hh in range(2):
                        pr = slice(hh * 64, hh * 64 + 64)
                        P.stt(dst[pr, 2 * c + hh, :], r[pr, c, :], g2[pr, gcol:gcol + 1], rstd[pr, c, :], ALU.mult,
                              ALU.mult)
                else:
                    P.stt(dst[:, c, :], r[:, c, :], g2[:, gcol:gcol + 1], rstd[:, c, :], ALU.mult, ALU.mult)

        for j in range(NT):
            headnorm(kTn[:, :, j * n:(j + 1) * n], kT, j * n, 1)
            s = vst.get()
            P.dma(s[:], v[j * n:(j + 1) * n, :].rearrange("(a p) c -> p a c", p=128))
            P.copy(V1[:, j * 4:(j + 1) * 4, :, 0:64], s[:].rearrange("p a (h d) -> p a h d", d=64), eng='pool')

        st_ps = Rot([P.ps("st%d" % i, [128, 512]) for i in range(2)])
        o_ps = Rot([P.ps("ops%d" % i, [64, 512])[:, 0:260].rearrange("p (h d) -> p h d", d=65) for i in range(4)])
        sb_t = Rot([P.sb("sbt%d" % i, [128, 512]) for i in range(2)])
        e_t = Rot([P.sb("et%d" % i, [128, 512], BF16) for i in range(2)])
        rec_t = Rot([P.sb("rec%d" % i, [64, 8]) for i in range(2)])
        ob_t = Rot([P.sb("ob%d" % i, [64, 512]) for i in range(2)])
        for j in range(NT):
            qTn = qTn_t.get()
            headnorm(qTn, qT, j * n, 0, sep=True)
            for ii in range(8):
                i = j * 8 + ii
                if i >= dbg_rows:
                    break
                rs = min(max(i - 4, 0), NROWS - 8)
                dl = i - rs
                oa, ob = o_ps.get(), o_ps.get()
                odd = rs % 2
                nkc = 5 if odd else 4
                for kc in range(nkc):
                    k0 = (rs - odd + 2 * kc) * GW
                    if not odd:
                        bt = 2 * kc - dl + 7
                    else:
                        assert dl == 4
                        bt = 14 if kc == 0 else (15 if kc == 4 else 2 * kc - 1 - dl + 7)
                    sp = st_ps.get()
                    for h in range(8):
                        P.mm(sp[:, h * 64:(h + 1) * 64], kTn[:, h // 2, k0:k0 + 128], qTn[:, h, ii * 64:(ii + 1) * 64])
                    sb_ = sb_t.get()
                    P.tt(sb_[:], sp[:], bias_sb[:, bt, :], ALU.add)
                    e = e_t.get()
                    P.act(e[:], sb_[:], AF.Exp)
                    for h in range(8):
                        op_ = oa if h < 4 else ob
                        P.mm(op_[:, h % 4, :], e[:, h * 64:(h + 1) * 64], V1[:, k0 // 128, h, :],
                             start=(kc == 0 and h % 4 == 0), stop=(kc == nkc - 1 and h % 4 == 3))
                rec = rec_t.get()
                P.recip(rec[:, 0:4], oa[:, :, 64])
                P.recip(rec[:, 4:8], ob[:, :, 64])
                obuf = ob_t.get()
                for h in range(8):
                    op_ = oa if h < 4 else ob
                    P.act(obuf[:, h * 64:(h + 1) * 64], op_[:, h % 4, 0:64], AF.Copy, scale=rec[:, h:h + 1])
                P.dma(o[i * 64:(i + 1) * 64, :], obuf[:], q='act')
        P.finish([o])
    return nc


def na_bias_table(rpb, hg):
    col = np.arange(GW)
    cstart = np.clip(col - 8, 0, GW - 16)
    cp = np.arange(GW)[:, None]
    cq = np.arange(GW)[None, :]
    valid = (cp >= cstart[None, :]) & (cp < cstart[None, :] + 16)
    dc = np.clip(cp - cq + 15, 0, 30)
    t = rpb[hg * 8:(hg + 1) * 8][:, :, dc]
    t = np.where(valid[None, None], t, np.float32(-30000.0)).astype(np.float32)
    t = t.transpose(1, 2, 0, 3).reshape(15, 64, 8 * 64)
    m = np.full((1, 64, 512), -30000.0, np.float32)
    return np.ascontiguousarray(np.concatenate([t, m, t[3:4], t[10:11], m], axis=0).reshape(19 * 64, 512))


HG = 64
NFFT = 2 * L


def hyena_consts():
    f64 = np.float64
    i64 = np.arange(64, dtype=f64)[:, None]
    i128 = np.arange(128, dtype=f64)
    a = 2 * np.pi * i64 * i128[None, :] / 128.0
    c = {}
    c["F1"] = np.concatenate([np.cos(a), -np.sin(a)], axis=1)
    t = 2 * np.pi * i128[:, None] * i128[None, :] / NFFT
    c["TW1"] = np.stack([np.cos(t), -np.sin(t)], axis=1)
    c["TW2"] = np.stack([np.cos(t), np.sin(t)], axis=1)
    b = 2 * np.pi * i128[:, None] * i128[None, :] / 128.0
    c["F3"] = np.stack([np.cos(b), -np.sin(b), np.sin(b)], axis=1)
    c["G1"] = np.concatenate([np.cos(b), np.sin(b)], axis=1)
    c["G2"] = np.concatenate([-np.sin(b), np.cos(b)], axis=1)
    a2 = 2 * np.pi * i128[:, None] * i64.T / 128.0
    c["FI"] = np.stack([np.cos(a2) / NFFT, -np.sin(a2) / NFFT], axis=1)
    f32 = np.float32
    tt_ = np.linspace(0.0, 1.0, L, dtype=f32)[:, None]
    bands = 8
    ang = (f32(2.0 * math.pi / L) * np.arange(L, dtype=f32)[:, None]) * np.linspace(1e-4, bands - 1, bands, dtype=f32)[None]
    feats = np.concatenate([tt_, np.cos(ang), -np.sin(ang)], axis=-1)
    c["featsT"] = np.ascontiguousarray(feats.T)
    c["tpos"] = tt_[:, 0]
    return {k: np.ascontiguousarray(v, dtype=np.float32) for k, v in c.items()}


def hyena_window(core):
    deltas = np.abs(np.linspace(DECAY_MIN_, DECAY_MAX_, 512, dtype=np.float32))[core * HG:(core + 1) * HG]
    t = np.linspace(0.0, 1.0, L, dtype=np.float32)
    w = np.exp(-t[:, None] * deltas[None, :])
    return np.ascontiguousarray(w.reshape(64, 128, HG).transpose(0, 2, 1))


DECAY_MIN_ = math.log(1e-2) / 1.5
DECAY_MAX_ = math.log(1e-2) / 0.3


def build_hyena():
    nc = new_nc()
    hz = dram_in(nc, "hz", [B, 3, 64, HG, 130])
    cw = dram_in(nc, "cw", [3 * 3 * HG])
    cbv = dram_in(nc, "cbv", [3 * HG])
    hb = dram_in(nc, "hb", [HG])
    w1 = dram_in(nc, "w1", [17, 64]); b1 = dram_in(nc, "b1", [64])
    w2 = dram_in(nc, "w2", [64, 64]); b2 = dram_in(nc, "b2", [64])
    w3 = dram_in(nc, "w3", [64, 64]); b3 = dram_in(nc, "b3", [64])
    w4 = dram_in(nc, "w4", [64, 2 * HG])
    fr = dram_in(nc, "fr", [64])
    win = dram_in(nc, "win", [64, HG, 128])
    cF1 = dram_in(nc, "F1", [64, 256]); cTW1 = dram_in(nc, "TW1", [128, 2, 128]); cTW2 = dram_in(nc, "TW2", [128, 2, 128])
    cF3 = dram_in(nc, "F3", [128, 3, 128]); cG1 = dram_in(nc, "G1", [128, 256]); cG2 = dram_in(nc, "G2", [128, 256])
    cFI = dram_in(nc, "FI", [128, 2, 64]); featsT = dram_in(nc, "featsT", [17, L])
    o = dram_out(nc, "o", [B, 64, HG, 128])
    TWO_PI = 2.0 * math.pi
    with ExitStack() as st:
        P = Prog(nc, st)
        F1 = P.sb("F1s", [64, 256]); TW1 = P.sb("TW1s", [128, 2, 128]); TW2 = P.sb("TW2s", [128, 2, 128])
        F3 = P.sb("F3s", [128, 3, 128]); G1 = P.sb("G1s", [128, 256]); G2 = P.sb("G2s", [128, 256])
        FI = P.sb("FIs", [128, 2, 64])
        for t_, d_ in ((F1, cF1), (TW1, cTW1), (TW2, cTW2), (F3, cF3), (G1, cG1), (G2, cG2), (FI, cFI)):
            P.dma(t_[:], d_)
        HK = P.sb("HK", [128, 2, HG, 128])
        At = Rot([P.sb("At%d" % i, [128, 2, 4, 128]) for i in range(1)])
        Yt = Rot([P.sb("Yt%d" % i, [128, 2, 4, 128]) for i in range(1)])
        Bt = Rot([P.sb("Bt%d" % i, [128, 2, 4, 128]) for i in range(1)])
        tm = Rot([P.sb("tm%d" % i, [128, 4, 128]) for i in range(4)])
        ps1 = Rot([P.ps("ps1_%d" % i, [128, 2, 256]) for i in range(2)])
        psX = [P.ps("psXr", [128, 4, 128]), P.ps("psXi", [128, 4, 128])]
        psB = Rot([P.ps("psB%d" % i, [128, 2, 256]) for i in range(2)])
        psY = P.ps("psY", [64, 4, 128])
        pi_c = P.sb("pi_c", [128, 1])
        P.memset(pi_c[:], -math.pi)

        def cmul(out_r, out_i, ar, ai, br, bi, ns):
            t1, t2, t3, t4 = [tm.get()[:, 0:ns, :] for _ in range(4)]
            P.tt(t1, ar, br, ALU.mult)
            P.tt(t2, ai, bi, ALU.mult)
            P.tt(out_r, t1, t2, ALU.subtract, eng='pool')
            P.tt(t3, ar, bi, ALU.mult)
            P.tt(t4, ai, br, ALU.mult)
            P.tt(out_i, t3, t4, ALU.add, eng='pool')

        def bc(tw, k, ns):
            return tw[:, k, :].unsqueeze(1).to_broadcast([128, ns, 128])

        def fwd4(sig):
            a = At.get()
            for pr in range(2):
                p1 = ps1.get()
                for i in range(2):
                    P.mm(p1[:, i, :], sig[2 * pr + i], F1[:])
                cmul(a[:, 0, 2 * pr:2 * pr + 2, :], a[:, 1, 2 * pr:2 * pr + 2, :], p1[:, :, 0:128], p1[:, :, 128:256],
                     bc(TW1, 0, 2), bc(TW1, 1, 2), 2)
            ar = a[:, 0, :, :].rearrange("p s k -> p (s k)")
            ai = a[:, 1, :, :].rearrange("p s k -> p (s k)")
            xr = psX[0][:].rearrange("p s k -> p (s k)")
            xi = psX[1][:].rearrange("p s k -> p (s k)")
            P.mm(xr, F3[:, 0, :], ar, start=True, stop=False)
            P.mm(xr, F3[:, 2, :], ai, start=False, stop=True)
            P.mm(xi, F3[:, 0, :], ai, start=True, stop=False)
            P.mm(xi, F3[:, 1, :], ar, start=False, stop=True)

        with ExitStack() as st2:
            def sb2(name, shape):
                return st2.enter_context(nc.sbuf_tensor(name, list(shape), F32))
            Hk = sb2("Hk", [64, 2 * HG, 128])
            h3 = sb2("h3", [64, L])
            w1s = sb2("w1s", [17, 64]); w2s = sb2("w2s", [64, 64]); w3s = sb2("w3s", [64, 64]); w4s = sb2("w4s", [64, 2 * HG])
            P.dma(w1s[:], w1); P.dma(w2s[:], w2); P.dma(w3s[:], w3); P.dma(w4s[:], w4)
            bs = sb2("bs", [64, 4])
            for i, v_ in enumerate((b1, b2, b3, fr)):
                P.dma(bs[:, i:i + 1], v_.rearrange("(p o) -> p o", o=1))
            targ = Rot([sb2("targ%d" % i, [64, 512]) for i in range(2)])
            hdt = Rot([sb2("hdt%d" % i, [64, 512]) for i in range(2)])
            fTt = Rot([sb2("fTt%d" % i, [17, 512]) for i in range(2)])
            winc = Rot([sb2("winc%d" % i, [64, 8, 128]) for i in range(1)])
            psm = [psB.t[0], psB.t[1]]
            for j in range(L // 512):
                src = fTt.get()
                P.dma(src[:], featsT[:, j * 512:(j + 1) * 512])
                for layer, wl in enumerate((w1s, w2s, w3s)):
                    pm = psm[layer % 2][0:64, :, :].rearrange("p a b -> p (a b)")
                    P.mm(pm, wl[:], src[:])
                    ta = targ.get()
                    P.ts(ta[:], pm, bs[:, layer:layer + 1], bs[:, 3:4], ALU.add, ALU.mult)
                    tk = targ.get()
                    P.ts(tk[:], ta[:], 1.0 / TWO_PI, 12582912.0, ALU.mult, ALU.add)
                    P.ts(tk[:], tk[:], 12582912.0, -TWO_PI, ALU.subtract, ALU.mult)
                    P.tt(ta[:], ta[:], tk[:], ALU.add)
                    P.ts(ta[:], ta[:], -3.1415925, 3.1415925, ALU.max, ALU.min)
                    dst = h3[:, j * 512:(j + 1) * 512] if layer == 2 else hdt.get()[:]
                    P.act(dst, ta[:], AF.Sin)
                    src = dst if layer == 2 else hdt.t[(hdt.i - 1) % 2]
            for q4 in range(32):
                p4 = psm[q4 % 2][0:64, :, :].rearrange("p a (b c) -> p (a b) c", c=128)
                for a_ in range(4):
                    n2 = q4 * 4 + a_
                    P.mm(p4[:, a_, :], h3[:, n2:L:128], w4s[:])
                P.copy(Hk[:, :, q4 * 4:q4 * 4 + 4], p4.rearrange("p a c -> p c a"), eng=('dve' if q4 % 2 == 0 else 'act'))
            for q in range(8):
                wc = winc.get()
                P.dma(wc[:], win[:, q * 8:(q + 1) * 8, :])
                for d_ in range(2):
                    hv = Hk[:, d_ * HG + q * 8:d_ * HG + (q + 1) * 8, :]
                    P.tt(hv, hv, wc[:], ALU.mult, eng=('dve' if d_ == 0 else 'pool'))
            r1 = sb2("r1", [64, 2 * HG]); r2 = sb2("r2", [64, HG]); sc = sb2("sc", [64, HG])
            ones64 = sb2("ones64", [64, 64])
            P.memset(ones64[:], 1.0)
            for q in range(16):
                sv = At.t[0][0:64, :, :, :].rearrange("p a s k -> p (a s) k")
                P.tt(sv, Hk[:, q * 8:(q + 1) * 8, :], Hk[:, q * 8:(q + 1) * 8, :], ALU.mult, eng='pool')
                P.reduce(r1[:, q * 8:(q + 1) * 8], sv, ALU.add)
            P.tt(r2[:], r1[:, 0:HG], r1[:, HG:2 * HG], ALU.add)
            pn = psY[:, 0, 0:HG]
            P.mm(pn, ones64[:], r2[:])
            P.ts(sc[:], pn, EPS, None, ALU.add)
            P.act(sc[:], sc[:], AF.Sqrt)
            P.recip(sc[:], sc[:])
            for d_ in range(2):
                for q in range(4):
                    hv = Hk[:, d_ * HG + q * 16:d_ * HG + (q + 1) * 16, :]
                    P.tt(hv, hv, sc[:, q * 16:(q + 1) * 16].unsqueeze(2).to_broadcast([64, 16, 128]), ALU.mult,
                         eng=('dve' if q % 2 == 0 else 'pool'))
            P.memset(Hk[0:1, HG:2 * HG, 0:1], 0.0)
            for d_ in range(2):
                for g in range(HG // 4):
                    fwd4([Hk[:, d_ * HG + g * 4 + i, :] for i in range(4)])
                    hr, hi = HK[:, 0, g * 4:g * 4 + 4, :], HK[:, 1, g * 4:g * 4 + 4, :]
                    if d_ == 0:
                        P.copy(hr, psX[0][:])
                        P.copy(hi, psX[1][:], eng='act')
                    else:
                        P.tt(hr, hr, psX[0][:], ALU.add)
                        P.tt(hi, hi, psX[1][:], ALU.subtract)
            P.barrier()
        zin = [P.sb("zin%d" % i, [64, 16, 130]) for i in range(3)]
        ut = [P.sb("ut%d" % i, [64, 16, 128]) for i in range(3)]
        tcv = [P.sb("tcv%d" % i, [64, 16, 128]) for i in range(2)]
        og = Rot([P.sb("og%d" % i, [64, 16, 128]) for i in range(2)])
        cws = P.sb("cws", [64, 9 * HG]); cbs = P.sb("cbs", [64, 3 * HG]); hbs = P.sb("hbs", [64, HG])
        P.dma(cws[:], cw.partition_broadcast(64))
        P.dma(cbs[:], cbv.partition_broadcast(64))
        P.dma(hbs[:], hb.partition_broadcast(64))

        def chb(t_, off, c0):
            return t_[:, off + c0:off + c0 + 16].unsqueeze(2).to_broadcast([64, 16, 128])

        for b_ in range(B):
            for cg in range(HG // 16):
                c0 = cg * 16
                for wh in range(3):
                    P.dma(zin[wh][:], hz[b_, wh, :, c0:c0 + 16, :])
                    eng = 'dve' if wh != 1 else 'pool'
                    u, t2 = ut[wh], tcv[0 if wh != 1 else 1]
                    P.tt(u[:], zin[wh][:, :, 0:128], chb(cws, (wh * 3 + 0) * HG, c0), ALU.mult, eng=eng)
                    P.tt(t2[:], zin[wh][:, :, 1:129], chb(cws, (wh * 3 + 1) * HG, c0), ALU.mult, eng=eng)
                    P.tt(u[:], u[:], t2[:], ALU.add, eng=eng)
                    P.tt(t2[:], zin[wh][:, :, 2:130], chb(cws, (wh * 3 + 2) * HG, c0), ALU.mult, eng=eng)
                    P.tt(u[:], u[:], t2[:], ALU.add, eng=eng)
                    P.tt(u[:], u[:], chb(cbs, wh * HG, c0), ALU.add, eng=eng)
                x0, s_, sb_ = ut[0], ut[2], ut[1]
                P.tt(s_[:], ut[2][:], ut[1][:], ALU.mult)
                P.tt(sb_[:], s_[:], chb(hbs, 0, c0), ALU.mult, eng='pool')
                ogt = og.get()
                for sg in range(4):
                    ch0 = c0 + sg * 4
                    fwd4([s_[:, sg * 4 + i, :] for i in range(4)])
                    y = Yt.get()
                    cmul(y[:, 0, :, :], y[:, 1, :, :], psX[0][:], psX[1][:], HK[:, 0, ch0:ch0 + 4, :], HK[:, 1, ch0:ch0 + 4, :], 4)
                    bt = Bt.get()
                    for pr in range(2):
                        pb = psB.get()
                        for i in range(2):
                            P.mm(pb[:, i, :], y[:, 0, 2 * pr + i, :], G1[:], start=True, stop=False)
                            P.mm(pb[:, i, :], y[:, 1, 2 * pr + i, :], G2[:], start=False, stop=True)
                        cmul(bt[:, 0, 2 * pr:2 * pr + 2, :], bt[:, 1, 2 * pr:2 * pr + 2, :], pb[:, :, 0:128], pb[:, :, 128:256],
                             bc(TW2, 0, 2), bc(TW2, 1, 2), 2)
                    py = psY[:].rearrange("p s k -> p (s k)")
                    P.mm(py, FI[:, 0, :], bt[:, 0, :, :].rearrange("p s k -> p (s k)"), start=True, stop=False)
                    P.mm(py, FI[:, 1, :], bt[:, 1, :, :].rearrange("p s k -> p (s k)"), start=False, stop=True)
                    ov = ogt[:, sg * 4:sg * 4 + 4, :]
                    P.tt(ov, psY[:], sb_[:, sg * 4:sg * 4 + 4, :], ALU.add)
                    P.tt(ov, ov, x0[:, sg * 4:sg * 4 + 4, :], ALU.mult, eng='pool')
                P.dma(o[b_, :, c0:c0 + 16, :], ogt[:], q='act')
        P.finish([o])
    return nc


RS = 1664
DEC = math.exp(-0.5)


def rwkv_consts():
    s = np.arange(128)[:, None]
    t = np.arange(128)[None, :]
    mU, mSU, mSL, idn = (s <= t), (s < t), (s > t), (s == t)
    f = lambda m: m.astype(np.float32)
    return np.ascontiguousarray(np.stack([f(mU), f(mSU), f(mSL), f(idn), -f(mSL), f(mSL), -f(mSU), -f(mU)], axis=1))


def build_rwkv(Lc=L, dbg=9):
    nc = new_nc()
    zp = dram_in(nc, "zp", [Lc + 1, RS])
    mu = dram_in(nc, "mu", [RS])
    w0 = dram_in(nc, "w0", [512]); wup = dram_in(nc, "wup", [64, 512])
    a0 = dram_in(nc, "a0", [512]); aup = dram_in(nc, "aup", [64, 512])
    kkv = dram_in(nc, "kk", [512]); kav = dram_in(nc, "ka", [512]); rkv = dram_in(nc, "rk", [512])
    tri = dram_in(nc, "tri", [128, 8, 128])
    ys = dram_out(nc, "ys", [Lc, 512])
    bon = dram_out(nc, "bon", [Lc, 512])
    NCH = Lc // 128
    NS = 4
    with ExitStack() as st:
        P = Prog(nc, st)
        tr_ = P.sb("tri_s", [128, 8, 128])
        P.dma(tr_[:], tri)
        mU, mSU, mSL, ident = tr_[:, 0, :], tr_[:, 1, :], tr_[:, 2, :], tr_[:, 3, :]
        mask4 = tr_[:, 4:8, :]
        mu_s = P.sb("mu_s", [128, RS]); P.dma(mu_s[:], mu.partition_broadcast(128))
        kk_s = P.sb("kk_s", [128, 512]); P.dma(kk_s[:], kkv.partition_broadcast(128))
        ka_s = P.sb("ka_s", [128, 512]); P.dma(ka_s[:], kav.partition_broadcast(128))
        rk_s = P.sb("rk_s", [128, 512]); P.dma(rk_s[:], rkv.partition_broadcast(128))
        wup_s = P.sb("wup_s", [64, 512]); P.dma(wup_s[:], wup)
        aup_s = P.sb("aup_s", [64, 512]); P.dma(aup_s[:], aup)
        rows = P.sb("rows", [1, 2, 512])
        P.dma(rows[:, 0, :], w0.rearrange("(o n) -> o n", o=1))
        P.dma(rows[:, 1, :], a0.rearrange("(o n) -> o n", o=1))
        ones = P.sb("ones", [128, 128]); P.memset(ones[:], 1.0)
        ST = [P.sb("ST%d" % h, [64, 64]) for h in range(8)]
        for h in range(8):
            P.memset(ST[h][:], 0.0, eng='pool')
        prev_t = Rot([P.sb("prev%d" % i, [128, RS]) for i in range(2)])
        cur_t = Rot([P.sb("cur%d" % i, [128, RS]) for i in range(2)])
        zd_t = Rot([P.sb("zd%d" % i, [128, RS]) for i in range(2)])
        Q4_t = Rot([P.sb("Q4_%d" % i, [128, 4, 512]) for i in range(2)])
        QT_t = Rot([P.sb("QT_%d" % i, [64, 4, 8, 128]) for i in range(2)])
        KH_t = Rot([P.sb("KH_%d" % i, [128, 2, 512]) for i in range(2)])
        gC_t = Rot([P.sb("gC_%d" % i, [64, 8]) for i in range(2)])
        wT = P.sb("wT", [64, 2, 128])
        sg = P.sb("sg", [128, 512]); av = P.sb("av", [128, 512])
        Gt = P.sb("Gt", [128, 512]); Gp = P.sb("Gp", [128, 512]); Gi = P.sb("Gi", [128, 512]); Gr = P.sb("Gr", [128, 512])
        kap = P.sb("kap", [128, 512]); kmod = P.sb("kmod", [128, 512]); beta = P.sb("beta", [128, 512])
        T1 = P.sb("T1", [128, 512]); T2 = P.sb("T2", [128, 512])
        s8 = P.sb("s8", [128, 4, 8])
        ys_t = Rot([P.sb("ys_t%d" % i, [128, 512]) for i in range(2)])
        bn_t = Rot([P.sb("bn_t%d" % i, [128, 512]) for i in range(2)])
        SL = []
        for s_ in range(NS):
            d_ = {"SCr": P.sb("SCr%d" % s_, [128, 4, 128]), "SC": P.sb("SC%d" % s_, [128, 4, 128]),
                  "Nk": P.sb("Nk%d" % s_, [128, 128]), "W2": P.sb("W2_%d" % s_, [128, 128]),
                  "W1": P.sb("W1_%d" % s_, [64, 128]), "U": P.sb("U_%d" % s_, [128, 64])}
            for nm in ("B", "BT", "IBT", "Pm"):
                d_[nm] = Rot([P.sb("%s%d_%d" % (nm, s_, i), [128, 128]) for i in range(2)])
            SL.append(d_)
        bkP = Rot([P.ps("bkP%d" % i, [128, 512]) for i in range(2)])
        bkA = [P.ps("bkA%d" % i, [128, 512]) for i in range(NS)]
        bkD = [P.ps("bkD%d" % i, [128, 512]) for i in range(NS // 2)]

        def v8(t_):
            return t_.rearrange("p (h j) -> p h j", j=64)

        def bc8(t_):
            return t_.unsqueeze(2).to_broadcast([128, 8, 64])

        def head_gen(h, s_, zd, Q4, QT, KH, gC, yst):
            hs = slice(h * 64, (h + 1) * 64)
            T_ = SL[s_]
            A_, D_ = bkA[s_], bkD[s_ // 2]
            c0 = (s_ % 2) * 256
            SCr, SC, Nk, W1, W2, U = T_["SCr"], T_["SC"], T_["Nk"], T_["W1"], T_["W2"], T_["U"]
            P.mm(A_[:, 0:256].rearrange("p (a t) -> p a t", a=2), QT[:, 0, h, :], QT[:, 2:4, h, :])
            P.mm(A_[:, 256:512].rearrange("p (a t) -> p a t", a=2), QT[:, 2, h, :], QT[:, 0:2, h, :])
            P.mm(D_[:, c0 + 128:c0 + 256], QT[:, 3, h, :], QT[:, 1, h, :])
            P.copy(SCr[:].rearrange("p a t -> p (a t)"), A_[:], eng='act')
            P.tt(SC[:], SCr[:], mask4, ALU.mult, eng='pool')
            P.tt(Nk[:], D_[:, c0 + 128:c0 + 256], mU, ALU.mult)
            AT, MkT, A, nNb = SC[:, 0, :], SC[:, 1, :], SC[:, 2, :], SC[:, 3, :]
            Pm = T_["Pm"].get()
            P.tt(Pm[:], A, ident, ALU.add, eng='pool')
            yield
            Bm, BTm = A, AT
            for kk_ in range(1, 7):
                last = (kk_ == 6)
                if not last:
                    P.mm(A_[:, 0:128], BTm, Bm)
                P.mm(D_[:, c0:c0 + 128], Bm, BTm)
                IBT = T_["IBT"].get()
                P.tt(IBT[:], D_[:, c0:c0 + 128], ident, ALU.add)
                if not last:
                    Bn, BTn = T_["B"].get(), T_["BT"].get()
                    P.copy(BTn[:], D_[:, c0:c0 + 128])
                    P.copy(Bn[:], A_[:, 0:128], eng='act')
                yield
                P.mm(A_[:, 128:256], IBT[:], Pm[:])
                Pn = T_["Pm"].get()
                P.copy(Pn[:], A_[:, 128:256], eng='act')
                Pm = Pn
                if not last:
                    Bm, BTm = Bn[:], BTn[:]
                yield
            P.mm(A_[0:64, 256:384], Q4[:, 0, hs], Pm[:])
            P.mm(A_[:, 384:512], MkT, Pm[:])
            P.copy(W1[:], A_[0:64, 256:384], eng='act')
            P.copy(W2[:], A_[:, 384:512], eng='act')
            yield
            vh = zd[:, 1024 + h * 64:1024 + (h + 1) * 64]
            P.mm(D_[:, c0:c0 + 64], W2[:], vh, start=True, stop=False)
            P.mm(D_[:, c0:c0 + 64], W1[:], ST[h][:], start=False, stop=True)
            P.copy(U[:], D_[:, c0:c0 + 64])
            yield
            P.mm(D_[:, c0 + 64:c0 + 128], QT[:, 1, h, :], ST[h][:], start=True, stop=False)
            P.mm(D_[:, c0 + 64:c0 + 128], Nk[:], vh, start=False, stop=False)
            P.mm(D_[:, c0 + 64:c0 + 128], nNb, U[:], start=False, stop=True)
            P.mm(D_[0:64, c0 + 128:c0 + 192], KH[:, 0, hs], vh, start=True, stop=False)
            P.mm(D_[0:64, c0 + 128:c0 + 192], KH[:, 1, hs], U[:], start=False, stop=True)
            P.copy(yst[:, hs], D_[:, c0 + 64:c0 + 128])
            P.stt(ST[h][:], ST[h][:], gC[:, h:h + 1], D_[0:64, c0 + 128:c0 + 192], ALU.mult, ALU.add)

        for ci in range(NCH):
            prev, cur, zd = prev_t.get(), cur_t.get(), zd_t.get()
            P.dma(prev[:], zp[ci * 128:ci * 128 + 128, :])
            P.dma(cur[:], zp[ci * 128 + 1:ci * 128 + 129, :])
            P.tt(zd[:], prev[:], cur[:], ALU.subtract, eng='pool')
            P.tt(zd[:], zd[:], mu_s[:], ALU.mult, eng='pool')
            P.tt(zd[:], zd[:], cur[:], ALU.add, eng='pool')
            r, k, v = zd[:, 0:512], zd[:, 512:1024], zd[:, 1024:1536]
            pb = bkP.get()
            pT = pb[0:64, 0:256].rearrange("p (a t) -> p a t", a=2)
            P.tr(pT[:, 0, :], zd[:, 1536:1600], ident)
            P.tr(pT[:, 1, :], zd[:, 1600:1664], ident)
            P.act(wT[:, 0, :], pT[:, 0, :], AF.Tanh)
            P.copy(wT[:, 1, :], pT[:, 1, :], eng='act')
            pb = bkP.get()
            P.mm(pb[:], wT[:, 0, :], wup_s[:], start=True, stop=False)
            P.mm(pb[:], ones[0:1, :], rows[:, 0, :], start=False, stop=True)
            P.act(sg[:], pb[:], AF.Sigmoid)
            pb = bkP.get()
            P.mm(pb[:], wT[:, 1, :], aup_s[:], start=True, stop=False)
            P.mm(pb[:], ones[0:1, :], rows[:, 1, :], start=False, stop=True)
            P.act(av[:], pb[:], AF.Sigmoid)
            pb = bkP.get()
            P.mm(pb[:], mU, sg[:])
            P.act(Gt[:], pb[:], AF.Exp, scale=-DEC)
            P.act(Gi[:], pb[:], AF.Exp, scale=DEC)
            pb = bkP.get()
            P.mm(pb[:], mSU, sg[:])
            P.act(Gp[:], pb[:], AF.Exp, scale=-DEC)
            pb = bkP.get()
            P.mm(pb[:], mSL, sg[:])
            P.act(Gr[:], pb[:], AF.Exp, scale=-DEC)
            pb = bkP.get()
            for h in range(8):
                P.mm(pb[0:64, h:h + 1], sg[:, h * 64:(h + 1) * 64], ones[:, 0:1])
            gC = gC_t.get()
            P.act(gC[:], pb[0:64, 0:8], AF.Exp, scale=-DEC)
            P.tt(T1[:], k, kk_s[:], ALU.mult)
            P.tt(T2[:], T1[:], T1[:], ALU.mult, eng='pool')
            P.reduce(s8[:, 0, :], v8(T2[:]), ALU.add)
            P.act(s8[:, 1, :], s8[:, 0, :], AF.Sqrt)
            P.ts(s8[:, 1, :], s8[:, 1, :], 1e-12, None, ALU.max)
            P.recip(s8[:, 1, :], s8[:, 1, :])
            P.tt(v8(kap[:]), v8(T1[:]), bc8(s8[:, 1, :]), ALU.mult)
            P.stt(T2[:], av[:], -1.0, ka_s[:], ALU.add, ALU.mult)
            P.stt(kmod[:], T2[:], 1.0, k, ALU.add, ALU.mult)
            P.tt(beta[:], kap[:], av[:], ALU.mult, eng='pool')
            P.tt(T1[:], r, kmod[:], ALU.mult, eng='pool')
            P.tt(T1[:], T1[:], rk_s[:], ALU.mult, eng='pool')
            P.reduce(s8[:, 2, :], v8(T1[:]), ALU.add)
            bn = bn_t.get()
            P.tt(v8(bn[:]), v8(v), bc8(s8[:, 2, :]), ALU.mult, eng='pool')
            P.dma(bon[ci * 128:(ci + 1) * 128, :], bn[:], q='act')
            Q4, KH = Q4_t.get(), KH_t.get()
            P.tt(Q4[:, 0, :], kap[:], Gp[:], ALU.mult)
            P.tt(Q4[:, 1, :], r, Gt[:], ALU.mult, eng='pool')
            P.tt(Q4[:, 2, :], beta[:], Gi[:], ALU.mult)
            P.tt(Q4[:, 3, :], kmod[:], Gi[:], ALU.mult, eng='pool')
            P.tt(KH[:, 0, :], kmod[:], Gr[:], ALU.mult, eng='pool')
            P.stt(KH[:, 1, :], beta[:], -1.0, Gr[:], ALU.mult, ALU.mult)
            QT = QT_t.get()
            for h in range(8):
                pb = bkP.get()
                pq = pb[0:64, :].rearrange("p (q t) -> p q t", q=4)
                for q in range(4):
                    P.tr(pq[:, q, :], Q4[:, q, h * 64:(h + 1) * 64], ident)
                P.copy(QT[:, :, h, :], pq, eng=('act' if h % 2 == 0 else 'dve'))
            yst = ys_t.get()
            for g0 in range(0, 8, NS):
                gens = [head_gen(g0 + s_, s_, zd, Q4, QT, KH, gC, yst) for s_ in range(NS)]
                while gens:
                    for g_ in list(gens):
                        try:
                            next(g_)
                        except StopIteration:
                            gens.remove(g_)
            P.dma(ys[ci * 128:(ci + 1) * 128, :], yst[:], q='act')
        P.finish([ys, bon])
    return nc


def build_rwkv_post(T):
    nc = new_nc()
    ysf = dram_in(nc, "ysf", [T, 512]); ysb = dram_in(nc, "ysb", [T, 512])
    bf = dram_in(nc, "bf", [T, 512]); bb = dram_in(nc, "bb", [T, 512])
    gdT = dram_in(nc, "gdT", [128, T])
    gup = dram_in(nc, "gup", [128, 512])
    lnw = dram_in(nc, "lnw", [512]); lnb = dram_in(nc, "lnb", [512])
    ya = dram_out(nc, "ya", [T, 512])
    with ExitStack() as st:
        P = Prog(nc, st)
        gup_s = P.sb("gup_s", [128, 512]); P.dma(gup_s[:], gup)
        lnw_s = P.sb("lnw_s", [128, 512]); P.dma(lnw_s[:], lnw.partition_broadcast(128))
        lnb_s = P.sb("lnb_s", [128, 512]); P.dma(lnb_s[:], lnb.partition_broadcast(128))
        gd_s = P.sb("gd_s", [128, T]); P.dma(gd_s[:], gdT)
        P.act(gd_s[:], gd_s[:], AF.Sigmoid)
        ins_t = [Rot([P.sb("in%d_%d" % (q, i), [128, 512]) for i in range(2)]) for q in range(4)]
        y_t = Rot([P.sb("y%d" % i, [128, 512]) for i in range(2)])
        sq = P.sb("sq", [128, 512])
        s8 = P.sb("s8", [128, 3, 8])
        o_t = Rot([P.sb("o%d" % i, [128, 512]) for i in range(2)])
        psg = Rot([P.ps("psg%d" % i, [128, 512]) for i in range(2)])

        def v8(t_):
            return t_.rearrange("p (h j) -> p h j", j=64)

        def bc8(t_):
            return t_.unsqueeze(2).to_broadcast([128, 8, 64])

        for t0 in range(0, T, 128):
            tl = [r_.get() for r_ in ins_t]
            for q, src in enumerate((ysf, ysb, bf, bb)):
                P.dma(tl[q][:], src[t0:t0 + 128, :])
            y = y_t.get()
            P.tt(y[:], tl[0][:], tl[1][:], ALU.add)
            P.reduce(s8[:, 0, :], v8(y[:]), ALU.add)
            P.ts(s8[:, 0, :], s8[:, 0, :], -1.0 / 64, None, ALU.mult)
            P.tt(v8(y[:]), v8(y[:]), bc8(s8[:, 0, :]), ALU.add)
            P.tt(sq[:], y[:], y[:], ALU.mult, eng='pool')
            P.reduce(s8[:, 1, :], v8(sq[:]), ALU.add)
            P.ts(s8[:, 1, :], s8[:, 1, :], 1.0 / 64, 64e-5, ALU.mult, ALU.add)
            P.act(s8[:, 1, :], s8[:, 1, :], AF.Sqrt)
            P.recip(s8[:, 1, :], s8[:, 1, :])
            P.tt(v8(y[:]), v8(y[:]), bc8(s8[:, 1, :]), ALU.mult)
            P.tt(y[:], y[:], lnw_s[:], ALU.mult, eng='pool')
            P.tt(y[:], y[:], lnb_s[:], ALU.add, eng='pool')
            P.tt(tl[2][:], tl[2][:], tl[3][:], ALU.add, eng='pool')
            P.tt(y[:], y[:], tl[2][:], ALU.add, eng='pool')
            pg = psg.get()
            P.mm(pg[:], gd_s[:, t0:t0 + 128], gup_s[:])
            o = o_t.get()
            P.tt(o[:], pg[:], y[:], ALU.mult)
            P.dma(ya[t0:t0 + 128, :], o[:], q='act')
        P.finish([ya])
    return nc


TSH = L // 2


def _shardT(a):
    return [np.ascontiguousarray(a[c // 2, (c % 2) * TSH:(c % 2 + 1) * TSH].T) for c in range(NCORES)]


def _unshardT(lst):
    out = np.empty((B, L, lst[0].shape[0]), np.float32)
    for c in range(NCORES):
        out[c // 2, (c % 2) * TSH:(c % 2 + 1) * TSH] = lst[c].T
    return out


def _shard_tok(a):
    return [np.ascontiguousarray(a[c // 2, (c % 2) * TSH:(c % 2 + 1) * TSH]) for c in range(NCORES)]


def _unshard_tok(lst):
    out = np.empty((B, L, lst[0].shape[1]), np.float32)
    for c in range(NCORES):
        out[c // 2, (c % 2) * TSH:(c % 2 + 1) * TSH] = lst[c]
    return out


_NC_CACHE = {}


def _nc(key, fn):
    if key not in _NC_CACHE:
        _NC_CACHE[key] = fn()
    return _NC_CACHE[key]


def linear_launch(aT_sh, W, g=None, mode='plain', resT_sh=None, W2=None, a2T_sh=None):
    K_, N_ = W.shape
    K2 = 0 if W2 is None else W2.shape[0]
    nc = _nc(("lin", K_, N_, g is not None, mode, K2), lambda: build_linear(TSH, K_, N_, g is not None, mode, K2))
    ins = []
    for c in range(NCORES):
        m = {"aT": aT_sh[c], "W": np.ascontiguousarray(W)}
        if g is not None:
            m["g"] = np.ascontiguousarray(g)
        if mode == 'res':
            m["resT"] = resT_sh[c]
        if mode == 'ple':
            m["W2"] = np.ascontiguousarray(W2)
            m["a2T"] = a2T_sh[c]
        ins.append(m)
    return [r["oT"] for r in run(nc, ins)]


def ffn_launch(hT_sh, g, Wu, cw, cb, Wd):
    nc = _nc(("ffn",), lambda: build_ffn(TSH))
    ins = []
    for c in range(NCORES):
        left = hT_sh[c - 1][:, -1:] if c % 2 == 1 else np.zeros((D, 1), np.float32)
        right = hT_sh[c + 1][:, :1] if c % 2 == 0 else np.zeros((D, 1), np.float32)
        ins.append({"hTp": np.ascontiguousarray(np.concatenate([left, hT_sh[c], right], axis=1)), "g": np.ascontiguousarray(g),
                    "Wu": np.ascontiguousarray(Wu), "cw": np.ascontiguousarray(cw), "cb": np.ascontiguousarray(cb),
                    "Wd": np.ascontiguousarray(Wd)})
    return [r["oT"] for r in run(nc, ins)]


def rwkv_launch(z, p):
    tri = rwkv_consts()
    ins = []
    for c in range(NCORES):
        b, dr = c // 2, c % 2
        zs = z[b, :, :RS]
        if dr == 1:
            zs = zs[::-1]
        zp = np.concatenate([np.zeros((1, RS), np.float32), zs], 0)
        ins.append({"zp": np.ascontiguousarray(zp), "mu": np.ascontiguousarray(p["rwkv_mu"][0, dr]),
                    "w0": np.ascontiguousarray(p["rwkv_w0"][0, dr]), "wup": np.ascontiguousarray(p["rwkv_w_up"][0, dr]),
                    "a0": np.ascontiguousarray(p["rwkv_a0"][0, dr]), "aup": np.ascontiguousarray(p["rwkv_a_up"][0, dr]),
                    "kk": np.ascontiguousarray(p["rwkv_k_k"][0]), "ka": np.ascontiguousarray(p["rwkv_k_a"][0]),
                    "rk": np.ascontiguousarray(p["rwkv_r_k"][0].reshape(-1)), "tri": tri})
    res = run(_nc(("rwkv",), lambda: build_rwkv(L)), ins)
    ysf = np.stack([res[2 * b]["ys"] for b in range(B)])
    ysb = np.stack([res[2 * b + 1]["ys"][::-1] for b in range(B)])
    bf = np.stack([res[2 * b]["bon"] for b in range(B)])
    bb = np.stack([res[2 * b + 1]["bon"][::-1] for b in range(B)])
    gdT = _shardT(z[:, :, RS:RS + 128])
    sh = [_shard_tok(a) for a in (ysf, ysb, bf, bb)]
    ins = [{"ysf": sh[0][c], "ysb": sh[1][c], "bf": sh[2][c], "bb": sh[3][c], "gdT": gdT[c],
            "gup": np.ascontiguousarray(p["rwkv_g_up"][0]), "lnw": np.ascontiguousarray(p["rwkv_ln_w"][0]),
            "lnb": np.ascontiguousarray(p["rwkv_ln_b"][0])} for c in range(NCORES)]
    res = run(_nc(("rwkvpost",), lambda: build_rwkv_post(TSH)), ins)
    return _unshard_tok([r["ya"] for r in res])


def hyena_launch(z, p):
    C = hyena_consts()
    zz = z[:, :, 1792:]
    zpad = np.pad(zz, ((0, 0), (1, 1), (0, 0)))
    idx = (np.arange(64)[:, None] * 128 + np.arange(130)[None, :])
    sw, sbias, w4 = p["hy_short_w"][0], p["hy_short_b"][0], p["hy_f_w4"][0]
    ins = []
    for c in range(NCORES):
        hz = np.empty((B, 3, 64, HG, 130), np.float32)
        for wh in range(3):
            cols = wh * 512 + c * HG + np.arange(HG)
            hz[:, wh] = zpad[:, :, cols][:, idx, :].transpose(0, 1, 3, 2)
        cw = np.stack([sw[:, wh * 512 + c * HG: wh * 512 + (c + 1) * HG] for wh in range(3)])
        cb = np.stack([sbias[wh * 512 + c * HG: wh * 512 + (c + 1) * HG] for wh in range(3)])
        w4c = np.concatenate([w4[:, dd * 512 + c * HG: dd * 512 + (c + 1) * HG] for dd in range(2)], axis=1)
        m = {"hz": hz, "cw": np.ascontiguousarray(cw.reshape(-1)), "cbv": np.ascontiguousarray(cb.reshape(-1)),
             "hb": np.ascontiguousarray(p["hy_bias"][0][c * HG:(c + 1) * HG]),
             "w1": np.ascontiguousarray(p["hy_f_w1"][0]), "b1": np.ascontiguousarray(p["hy_f_b1"][0]),
             "w2": np.ascontiguousarray(p["hy_f_w2"][0]), "b2": np.ascontiguousarray(p["hy_f_b2"][0]),
             "w3": np.ascontiguousarray(p["hy_f_w3"][0]), "b3": np.ascontiguousarray(p["hy_f_b3"][0]),
             "w4": np.ascontiguousarray(w4c), "fr": np.ascontiguousarray(p["hy_f_freq"][0]), "win": hyena_window(c)}
        for k_ in ("F1", "TW1", "TW2", "F3", "G1", "G2", "FI", "featsT"):
            m[k_] = C[k_]
        ins.append(m)
    res = run(_nc(("hyena",), build_hyena), ins)
    yb = np.empty((B, L, 512), np.float32)
    for c in range(NCORES):
        yb[:, :, c * HG:(c + 1) * HG] = res[c]["o"].transpose(0, 1, 3, 2).reshape(B, L, HG)
    return yb


def na_launch(z, p):
    ins = []
    for c in range(NCORES):
        b, hg = c // 2, c % 2
        ins.append({"qT": np.ascontiguousarray(z[b, :, hg * 512:(hg + 1) * 512].T),
                    "kT": np.ascontiguousarray(z[b, :, D + hg * 512:D + (hg + 1) * 512].T),
                    "v": np.ascontiguousarray(z[b, :, 2 * D + hg * 512:2 * D + (hg + 1) * 512]),
                    "bias": na_bias_table(p["na_rpb"][0], hg), "qg": np.ascontiguousarray(p["na_q_g"][0]),
                    "kg": np.ascontiguousarray(p["na_k_g"][0])})
    res = run(_nc(("na",), build_na), ins)
    att = np.empty((B, L, D), np.float32)
    for c in range(NCORES):
        att[c // 2, :, (c % 2) * 512:(c % 2 + 1) * 512] = res[c]["o"]
    return att


def kernel(**p):
    p = {k: np.asarray(v, dtype=np.float32) for k, v in p.items()}
    hT = _shardT(p["x"])
    zT = linear_launch(hT, p["mix_w_in"][0], g=p["mix_norm"][0])
    z = _unshardT(zT)
    ya = rwkv_launch(z, p)
    yb = hyena_launch(z, p)
    yT = _shardT(np.concatenate([ya, yb], axis=-1))
    hT = linear_launch(yT, p["mix_w_out"][0], mode='res', resT_sh=hT)
    hT = ffn_launch(hT, p["ffn_norm"][0], p["ffn_w_up"][0], p["ffn_conv_w"][0], p["ffn_conv_b"][0], p["ffn_w_down"][0])
    hT = linear_launch(hT, p["ple_w_gate"][0], g=p["ple_norm"][0], mode='ple', W2=p["ple_w_proj"][0],
                       a2T_sh=_shardT(p["p"][0]))
    zT = linear_launch(hT, p["na_w_qkv"][0], g=p["na_norm"][0])
    att = na_launch(_unshardT(zT), p)
    hT = linear_launch(_shardT(att), p["na_w_out"][0], mode='res', resT_sh=hT)
    hT = ffn_launch(hT, p["ffn_norm"][1], p["ffn_w_up"][1], p["ffn_conv_w"][1], p["ffn_conv_b"][1], p["ffn_w_down"][1])
    hT = linear_launch(hT, p["ple_w_gate"][1], g=p["ple_norm"][1], mode='ple', W2=p["ple_w_proj"][1],
                       a2T_sh=_shardT(p["p"][1]))
    return _unshardT(hT)
```

```python
from contextlib import ExitStack
import math
import numpy as np
import concourse.bass as bass
import concourse.mybir as mybir
from concourse.bass_utils import run_bass_kernel_spmd

F32 = mybir.dt.float32
BF16 = mybir.dt.bfloat16
AF = mybir.ActivationFunctionType
ALU = mybir.AluOpType
AX = mybir.AxisListType

NCORES = 8
D = 1024
B = 4
L = 8192
EPS = 1e-6


class Prog:
    NDMA = 24
    SEM_LIMIT = 6000
    EMBED = True

    def __init__(self, nc, stack):
        self.nc = nc
        self.st = stack
        self.E = {'pe': nc.tensor, 'dve': nc.vector, 'act': nc.scalar, 'pool': nc.gpsimd, 'sp': nc.sync}
        self.sem = {e: stack.enter_context(nc.semaphore("s_" + e)) for e in ('pe', 'dve', 'act', 'pool')}
        self.dsem = [stack.enter_context(nc.semaphore("d%d" % i)) for i in range(self.NDMA)]
        self.dcnt = [0] * self.NDMA
        self.dnext = 0
        self.cnt = {e: 0 for e in self.sem}
        self.gen = {e: 0 for e in self.sem}
        self.allsem = {(e, 0): self.sem[e] for e in self.sem}
        self.seen = {e: {} for e in self.E}
        self.lastw = {}
        self.readers = {}
        self.nuniq = 0
        self.psum_names = set()

    def sb(self, name, shape, dt=F32):
        return self.st.enter_context(self.nc.sbuf_tensor(name, list(shape), dt))

    def ps(self, name, shape, dt=F32):
        self.psum_names.add(name)
        return self.st.enter_context(self.nc.psum_tensor(name, list(shape), dt))

    @staticmethod
    def key(ap):
        if isinstance(ap, str):
            return ap
        return ap.tensor.name

    def _need(self, eng, ev, waits):
        if ev is None:
            return
        src, n = ev
        if src[0] == eng and eng == 'pe':
            return
        if self.seen[eng].get(src, 0) >= n:
            return
        waits[src] = max(waits.get(src, 0), n)

    def _deps(self, eng, reads, writes, embed=False):
        waits = {}
        for k in reads:
            self._need(eng, self.lastw.get(k), waits)
            if k in self.psum_names:
                for ev in self.readers.get(k, {}).items():
                    if ev[0][0] != eng:
                        self._need(eng, ev, waits)
        for k in writes:
            self._need(eng, self.lastw.get(k), waits)
            for ev in self.readers.get(k, {}).items():
                self._need(eng, ev, waits)
        items = list(waits.items())
        emb = None
        if embed and items:
            emb = items.pop()
        for src, n in items:
            s = self.dsem[src[1]] if src[0] == 'd' else self.allsem[src]
            self.E[eng].wait_ge(s, n)
            self.seen[eng][src] = n
        if emb is not None:
            src, n = emb
            self.seen[eng][src] = n
            return (self.dsem[src[1]] if src[0] == 'd' else self.allsem[src], n)
        return None

    def _commit(self, ev, reads, writes):
        for k in writes:
            self.lastw[k] = ev
            self.readers[k] = {}
        for k in reads:
            r = self.readers.setdefault(k, {})
            r[ev[0]] = max(r.get(ev[0], 0), ev[1])

    def op(self, eng, fn, reads, writes):
        reads = [self.key(a) for a in reads if a is not None and not isinstance(a, (int, float))]
        writes = [self.key(a) for a in writes if a is not None]
        emb = self._deps(eng, reads, writes, embed=self.EMBED)
        if self.cnt[eng] >= self.SEM_LIMIT:
            self.gen[eng] += 1
            self.cnt[eng] = 0
            self.sem[eng] = self.st.enter_context(self.nc.semaphore("s_%s_%d" % (eng, self.gen[eng])))
            self.allsem[(eng, self.gen[eng])] = self.sem[eng]
        ins = fn(self.E[eng])
        if emb is not None:
            ins._wait_ge(emb[0], emb[1])
        self.cnt[eng] += 1
        ins.then_inc(self.sem[eng], 1)
        self._commit(((eng, self.gen[eng]), self.cnt[eng]), reads, writes)
        return ins

    def dma(self, out, in_, q='sp', **kw):
        reads = [self.key(in_)]
        writes = [self.key(out)]
        i = self.dnext
        self.dnext = (self.dnext + 1) % self.NDMA
        if self.dcnt[i] > 0:
            w = {}
            self._need(q, (('d', i), self.dcnt[i]), w)
            for src, n in w.items():
                self.E[q].wait_ge(self.dsem[i], n)
                self.seen[q][src] = n
        self._deps(q, reads, writes)
        ins = self.E[q].dma_start(out=out, in_=in_, **kw)
        self.dcnt[i] += 16
        ins.then_inc(self.dsem[i], 16)
        self._commit((('d', i), self.dcnt[i]), reads, writes)
        return ins

    def barrier(self):
        for eng in self.E:
            for src in self.sem:
                key = (src, self.gen[src])
                if src != eng and self.cnt[src] > self.seen[eng].get(key, 0):
                    self.E[eng].wait_ge(self.sem[src], self.cnt[src])
                    self.seen[eng][key] = self.cnt[src]
            for i in range(self.NDMA):
                if self.dcnt[i] > self.seen[eng].get(('d', i), 0):
                    self.E[eng].wait_ge(self.dsem[i], self.dcnt[i])
                    self.seen[eng][('d', i)] = self.dcnt[i]

    def finish(self, keys):
        for eng in ('sp', 'pool'):
            self._deps(eng, [self.key(k) for k in keys], [])

    def mm(self, out, lhsT, rhs, start=True, stop=True):
        return self.op('pe', lambda e: e.matmul(out, lhsT, rhs, start=start, stop=stop), [lhsT, rhs], [out])

    def tr(self, out, in_, ident):
        return self.op('pe', lambda e: e.transpose(out, in_, ident), [in_, ident], [out])

    def act(self, out, in_, func, bias=0.0, scale=1.0, accum_out=None):
        kw = {}
        if accum_out is not None:
            kw['accum_out'] = accum_out
        return self.op('act', lambda e: e.activation(out, in_, func, bias=bias, scale=scale, **kw),
                       [in_, bias, scale], [out, accum_out])

    def tt(self, out, a, b, op, eng='dve'):
        return self.op(eng, lambda e: e.tensor_tensor(out, a, b, op), [a, b], [out])

    def ts(self, out, a, s1, s2, op0, op1=None, eng='dve', accum_out=None):
        kw = {}
        if accum_out is not None:
            kw['accum_out'] = accum_out
        if op1 is None:
            return self.op(eng, lambda e: e.tensor_scalar(out, a, s1, None, op0, **kw), [a, s1], [out, accum_out])
        return self.op(eng, lambda e: e.tensor_scalar(out, a, s1, s2, op0, op1, **kw), [a, s1, s2],
                       [out, accum_out])

    def stt(self, out, a, s, b, op0, op1, eng='dve'):
        return self.op(eng, lambda e: e.scalar_tensor_tensor(out, a, s, b, op0, op1), [a, s, b], [out])

    def copy(self, out, a, eng='dve'):
        if eng == 'act':
            return self.op('act', lambda e: e.copy(out, a), [a], [out])
        return self.op(eng, lambda e: e.tensor_copy(out, a), [a], [out])

    def recip(self, out, a):
        return self.op('dve', lambda e: e.reciprocal(out, a), [a], [out])

    def memset(self, out, v, eng='dve'):
        return self.op(eng, lambda e: e.memset(out, v), [], [out])

    def reduce(self, out, a, op, axis=AX.X, eng='dve'):
        return self.op(eng, lambda e: e.tensor_reduce(out, a, axis, op), [a], [out])


class Rot:
    def __init__(self, tiles):
        self.t = tiles
        self.i = 0

    def get(self):
        t = self.t[self.i % len(self.t)]
        self.i += 1
        return t


def new_nc():
    return bass.Bass("TRN2", target_bir_lowering=False)


def dram_in(nc, name, shape, dt=F32):
    return nc.dram_tensor(name, list(shape), dt, kind="ExternalInput").ap()


def dram_out(nc, name, shape, dt=F32):
    return nc.dram_tensor(name, list(shape), dt, kind="ExternalOutput").ap()


def load_weight_bf16(P, w_sb, w_dram, K, N, stg, engs=('dve', 'pool')):
    wv = w_dram.rearrange("(kc p) n -> p kc n", p=128)
    i = 0
    for kc in range(K // 128):
        for n0 in range(0, N, 2048):
            n1 = min(N, n0 + 2048)
            s = stg.get()
            P.dma(s[:, 0:n1 - n0], wv[:, kc, n0:n1])
            P.copy(w_sb[:, kc, n0:n1], s[:, 0:n1 - n0], eng=engs[i % len(engs)])
            i += 1


def load_cols(P, dst, vec_dram, n):
    P.dma(dst[:, 0:n], vec_dram.rearrange("(c p) -> p c", p=128), allow_slow_non_contiguous=True)


def rmsnorm_T(P, xn, aT, g_sb, ones_bf, sq, ssq_ps, rstd, KC, n, dmodel):
    for kc in range(KC):
        P.act(sq[:, kc, 0:n], aT[:, kc, 0:n], AF.Square)
    for kc in range(KC):
        P.mm(ssq_ps[:, 0:n], ones_bf[:], sq[:, kc, 0:n], start=(kc == 0), stop=(kc == KC - 1))
    P.ts(rstd[:, 0:n], ssq_ps[:, 0:n], 1.0 / dmodel, EPS, ALU.mult, ALU.add)
    P.act(rstd[:, 0:n], rstd[:, 0:n], AF.Sqrt)
    P.recip(rstd[:, 0:n], rstd[:, 0:n])
    for kc in range(KC):
        P.stt(xn[:, kc, 0:n], aT[:, kc, 0:n], g_sb[:, kc:kc + 1], rstd[:, 0:n], ALU.mult, ALU.mult)


def build_linear(T, K, N, norm, mode, K2=0):
    nc = new_nc()
    aT = dram_in(nc, "aT", [K, T])
    W = dram_in(nc, "W", [K, N])
    g = dram_in(nc, "g", [K]) if norm else None
    resT = dram_in(nc, "resT", [N, T]) if mode == 'res' else None
    if mode == 'ple':
        W2 = dram_in(nc, "W2", [K2, N])
        a2T = dram_in(nc, "a2T", [K2, T])
    oT = dram_out(nc, "oT", [N, T])
    KC, NCH, n = K // 128, N // 128, 512
    with ExitStack() as st:
        P = Prog(nc, st)
        w_sb = P.sb("w_sb", [128, KC, N], BF16)
        stg = Rot([P.sb("wstg%d" % i, [128, 2048]) for i in range(2)])
        load_weight_bf16(P, w_sb, W, K, N, stg)
        if mode == 'ple':
            w2_sb = P.sb("w2_sb", [128, K2 // 128, N], BF16)
            load_weight_bf16(P, w2_sb, W2, K2, N, stg)
        ones_bf = P.sb("ones_bf", [128, 128], BF16)
        P.memset(ones_bf[:], 1.0)
        if norm:
            g_sb = P.sb("g_sb", [128, KC])
            load_cols(P, g_sb, g, KC)
        a_t = Rot([P.sb("a_t%d" % i, [128, KC, n]) for i in range(2)])
        xn_t = Rot([P.sb("xn_t%d" % i, [128, KC, n], BF16) for i in range(2)])
        sq = P.sb("sq", [128, KC, n], BF16)
        rstd = P.sb("rstd", [128, n])
        ssq_ps = P.ps("ssq_ps", [128, n])
        pss = Rot([P.ps("ps%d" % i, [128, n]) for i in range(4)])
        outs = Rot([P.sb("o%d" % i, [128, n]) for i in range(4)])
        if mode == 'res':
            res_t = Rot([P.sb("res%d" % i, [128, n]) for i in range(3)])
        if mode == 'ple':
            a2_t = Rot([P.sb("a2_t%d" % i, [128, K2 // 128, n]) for i in range(2)])
            a2b_t = Rot([P.sb("a2b_t%d" % i, [128, K2 // 128, n], BF16) for i in range(2)])
            ps2s = Rot([P.ps("ps2_%d" % i, [128, n]) for i in range(2)])
            sg_t = Rot([P.sb("sg%d" % i, [128, n]) for i in range(2)])
        aTv = aT.rearrange("(kc p) t -> p kc t", p=128)
        for t0 in range(0, T, n):
            a = a_t.get()
            P.dma(a[:], aTv[:, :, t0:t0 + n])
            xn = xn_t.get()
            if norm:
                rmsnorm_T(P, xn, a, g_sb, ones_bf, sq, ssq_ps, rstd, KC, n, K)
            else:
                for kc in range(KC):
                    P.copy(xn[:, kc, :], a[:, kc, :], eng=('dve' if kc % 2 == 0 else 'pool'))
            if mode == 'ple':
                a2 = a2_t.get()
                P.dma(a2[:], a2T.rearrange("(kc p) t -> p kc t", p=128)[:, :, t0:t0 + n])
                a2b = a2b_t.get()
                P.copy(a2b[:], a2[:], eng='pool')
            for c in range(NCH):
                ps = pss.get()
                for kc in range(KC):
                    P.mm(ps[:], w_sb[:, kc, c * 128:(c + 1) * 128], xn[:, kc, :], start=(kc == 0), stop=(kc == KC - 1))
                o = outs.get()
                if mode == 'plain':
                    if c % 2 == 0:
                        P.copy(o[:], ps[:], eng='dve')
                    else:
                        P.copy(o[:], ps[:], eng='act')
                elif mode == 'res':
                    r = res_t.get()
                    P.dma(r[:], resT[c * 128:(c + 1) * 128, t0:t0 + n])
                    P.tt(o[:], ps[:], r[:], ALU.add)
                else:
                    ps2 = ps2s.get()
                    for kc in range(K2 // 128):
                        P.mm(ps2[:], w2_sb[:, kc, c * 128:(c + 1) * 128], a2b[:, kc, :], start=(kc == 0),
                             stop=(kc == K2 // 128 - 1))
                    sg = sg_t.get()
                    P.act(sg[:], ps[:], AF.Sigmoid)
                    P.tt(sg[:], sg[:], ps2[:], ALU.mult)
                    P.tt(o[:], sg[:], a[:, c, :], ALU.add, eng='pool')
                P.dma(oT[c * 128:(c + 1) * 128, t0:t0 + n], o[:], q='act')
        P.finish([oT])
    return nc


DFF = 2816


def build_ffn(T):
    nc = new_nc()
    hTp = dram_in(nc, "hTp", [D, T + 2])
    g = dram_in(nc, "g", [D])
    Wu = dram_in(nc, "Wu", [D, 2 * DFF])
    cw = dram_in(nc, "cw", [3, 2 * DFF])
    cb = dram_in(nc, "cb", [2 * DFF])
    Wd = dram_in(nc, "Wd", [DFF, D])
    oT = dram_out(nc, "oT", [D, T])
    KC, n, FC = D // 128, 256, DFF // 128
    with ExitStack() as st:
        P = Prog(nc, st)
        wu_sb = P.sb("wu_sb", [128, KC, 2 * DFF], BF16)
        wd_sb = P.sb("wd_sb", [128, FC, D], BF16)
        stg = Rot([P.sb("wstg%d" % i, [128, 2048]) for i in range(2)])
        load_weight_bf16(P, wu_sb, Wu, D, 2 * DFF, stg)
        load_weight_bf16(P, wd_sb, Wd, DFF, D, stg)
        ones_bf = P.sb("ones_bf", [128, 128], BF16)
        P.memset(ones_bf[:], 1.0)
        g_sb = P.sb("g_sb", [128, KC])
        load_cols(P, g_sb, g, KC)
        cw_sb = P.sb("cw_sb", [128, 3, 2 * FC])
        for j in range(3):
            load_cols(P, cw_sb[:, j, :], cw[j], 2 * FC)
        cb_sb = P.sb("cb_sb", [128, 2 * FC])
        load_cols(P, cb_sb, cb, 2 * FC)
        a_t = Rot([P.sb("a_t%d" % i, [128, KC, n + 2]) for i in range(2)])
        xn = P.sb("xn", [128, KC, n + 2], BF16)
        sq = P.sb("sq", [128, KC, n + 2], BF16)
        rstd = P.sb("rstd", [128, n + 2])
        ssq_ps = P.ps("ssq_ps", [128, n + 2])
        pss = Rot([P.ps("ps%d" % i, [128, n + 2]) for i in range(4)])
        psd = Rot([P.ps("psd%d" % i, [128, n]) for i in range(2)])
        gT = P.sb("gT", [128, FC, n], BF16)
        ca_t = Rot([P.sb("ca%d" % i, [128, n]) for i in range(2)])
        cb_t = Rot([P.sb("cbv%d" % i, [128, n]) for i in range(2)])
        t1_t = Rot([P.sb("t1_%d" % i, [128, n]) for i in range(2)])
        t2_t = Rot([P.sb("t2_%d" % i, [128, n]) for i in range(2)])
        outs = Rot([P.sb("o%d" % i, [128, n]) for i in range(3)])
        hv = hTp.rearrange("(kc p) t -> p kc t", p=128)

        def conv(dst, ps, col):
            P.act(dst[:], ps[:, 1:n + 1], AF.Identity, bias=cb_sb[:, col:col + 1], scale=cw_sb[:, 1, col:col + 1])
            P.stt(dst[:], ps[:, 0:n], cw_sb[:, 0, col:col + 1], dst[:], ALU.mult, ALU.add)
            P.stt(dst[:], ps[:, 2:n + 2], cw_sb[:, 2, col:col + 1], dst[:], ALU.mult, ALU.add)

        for t0 in range(0, T, n):
            a = a_t.get()
            P.dma(a[:], hv[:, :, t0:t0 + n + 2])
            rmsnorm_T(P, xn, a, g_sb, ones_bf, sq, ssq_ps, rstd, KC, n + 2, D)
            for fc in range(FC):
                psa, psb = pss.get(), pss.get()
                for kc in range(KC):
                    P.mm(psa[:], wu_sb[:, kc, fc * 128:(fc + 1) * 128], xn[:, kc, :], start=(kc == 0),
                         stop=(kc == KC - 1))
                for kc in range(KC):
                    P.mm(psb[:], wu_sb[:, kc, DFF + fc * 128:DFF + (fc + 1) * 128], xn[:, kc, :], start=(kc == 0),
                         stop=(kc == KC - 1))
                ca, cbv, t1, t2 = ca_t.get(), cb_t.get(), t1_t.get(), t2_t.get()
                conv(ca, psa, fc)
                conv(cbv, psb, FC + fc)
                P.act(t2[:], ca[:], AF.Gelu_apprx_tanh)
                P.tt(gT[:, fc, :], t2[:], cbv[:], ALU.mult, eng='pool')
            for c in range(KC):
                ps = psd.get()
                for fc in range(FC):
                    P.mm(ps[:], wd_sb[:, fc, c * 128:(c + 1) * 128], gT[:, fc, :], start=(fc == 0), stop=(fc == FC - 1))
                o = outs.get()
                P.tt(o[:], ps[:], a[:, c, 1:n + 1], ALU.add)
                P.dma(oT[c * 128:(c + 1) * 128, t0:t0 + n], o[:], q='sp')
        P.finish([oT])
    return nc


def run(nc, in_maps, trace=False):
    res = run_bass_kernel_spmd(nc, in_maps, core_ids=list(range(NCORES)), trace=trace)
    if trace:
        print("exec_time_ns", res.exec_time_ns, flush=True)
    return res.results


GW = 64
NROWS = L // GW


def build_na(dbg_rows=NROWS, dbg_pv=True):
    nc = new_nc()
    qT = dram_in(nc, "qT", [512, L])
    kT = dram_in(nc, "kT", [512, L])
    v = dram_in(nc, "v", [L, 512])
    bias = dram_in(nc, "bias", [19 * 64, 512])
    qg = dram_in(nc, "qg", [64])
    kg = dram_in(nc, "kg", [64])
    o = dram_out(nc, "o", [L, 512])
    n = 512
    NT = L // n
    with ExitStack() as st:
        P = Prog(nc, st)
        bd = P.sb("bd", [128, 128], BF16)
        P.memset(bd[:], 0.0)
        P.memset(bd[0:64, 0:64], 1.0)
        P.memset(bd[64:128, 64:128], 1.0)
        g2 = P.sb("g2", [128, 2])
        for hh in range(2):
            P.dma(g2[hh * 64:(hh + 1) * 64, 0:1], qg.rearrange("(p o) -> p o", o=1))
            P.dma(g2[hh * 64:(hh + 1) * 64, 1:2], kg.rearrange("(p o) -> p o", o=1))
        P.ts(g2[:, 0:1], g2[:, 0:1], 0.125, None, ALU.mult)
        bias_sb = P.sb("bias_sb", [128, 16, 512])
        for dr0 in range(14):
            P.dma(bias_sb[:, dr0, :], bias[dr0 * 64:dr0 * 64 + 128, :])
        for x in range(2):
            P.dma(bias_sb[:, 14 + x, :], bias[(15 + 2 * x) * 64:(15 + 2 * x) * 64 + 128, :])
        kTn = P.sb("kTn", [128, 4, L], BF16)
        V1 = P.sb("V1", [128, L // 128, 8, 65], BF16)
        P.memset(V1[:, :, :, 64:65], 1.0, eng='pool')
        raw = Rot([P.sb("raw%d" % i, [128, 4, n]) for i in range(1)])
        sq = P.sb("sq", [128, 4, n], BF16)
        rstd = P.sb("rstd", [128, 4, n])
        ssq = Rot([P.ps("ssq%d" % i, [128, n]) for i in range(2)])
        qTn_t = Rot([P.sb("qz%d" % i, [128, 8, n], BF16) for i in range(2)])
        for t_ in qTn_t.t:
            P.memset(t_[:], 0.0, eng='pool')
        vst = raw

        def headnorm(dst, src_dram, t0, gcol, sep=False):
            r = raw.get()
            P.dma(r[:], src_dram.rearrange("(c p) t -> p c t", p=128)[:, :, t0:t0 + n])
            for c in range(4):
                P.act(sq[:, c, :], r[:, c, :], AF.Square)
            for c in range(4):
                ps = ssq.get()
                P.mm(ps[:], bd[:], sq[:, c, :])
                P.ts(rstd[:, c, :], ps[:], 1.0 / 64, EPS, ALU.mult, ALU.add)
            P.act(rstd[:], rstd[:], AF.Sqrt)
            P.recip(rstd[:], rstd[:])
            for c in range(4):
                if sep:
                    for hh in range(2):
                        pr = slice(hh * 64, hh * 64 + 64)
                        P.stt(dst[pr, 2 * c + hh, :], r[pr, c, :], g2[pr, gcol:gcol + 1], rstd[pr, c, :], ALU.mult,
                              ALU.mult)
                else:
                    P.stt(dst[:, c, :], r[:, c, :], g2[:, gcol:gcol + 1], rstd[:, c, :], ALU.mult, ALU.mult)

        for j in range(NT):
            headnorm(kTn[:, :, j * n:(j + 1) * n], kT, j * n, 1)
            s = vst.get()
            P.dma(s[:], v[j * n:(j + 1) * n, :].rearrange("(a p) c -> p a c", p=128))
            P.copy(V1[:, j * 4:(j + 1) * 4, :, 0:64], s[:].rearrange("p a (h d) -> p a h d", d=64), eng='pool')

        st_ps = Rot([P.ps("st%d" % i, [128, 512]) for i in range(2)])
        o_ps = Rot([P.ps("ops%d" % i, [64, 512])[:, 0:260].rearrange("p (h d) -> p h d", d=65) for i in range(4)])
        sb_t = Rot([P.sb("sbt%d" % i, [128, 512]) for i in range(2)])
        e_t = Rot([P.sb("et%d" % i, [128, 512], BF16) for i in range(2)])
        rec_t = Rot([P.sb("rec%d" % i, [64, 8]) for i in range(2)])
        ob_t = Rot([P.sb("ob%d" % i, [64, 512]) for i in range(2)])
        for j in range(NT):
            qTn = qTn_t.get()
            headnorm(qTn, qT, j * n, 0, sep=True)
            units = []
            for ii in range(8):
                i = j * 8 + ii
                if i >= dbg_rows:
                    break
                rs = min(max(i - 4, 0), NROWS - 8)
                dl = i - rs
                odd = rs % 2
                nkc = 5 if odd else 4
                for kc in range(nkc):
                    k0 = (rs - odd + 2 * kc) * GW
                    if not odd:
                        bt = 2 * kc - dl + 7
                    else:
                        assert dl == 4
                        bt = 14 if kc == 0 else (15 if kc == 4 else 2 * kc - 1 - dl + 7)
                    units.append((i, ii, kc, nkc, k0, bt))
            state = {}

            def emit_qk(u):
                i, ii, kc, nkc, k0, bt = u
                sp = st_ps.get()
                for c in range(4):
                    P.mm(sp[:, c * 128:(c + 1) * 128].rearrange("p (a q) -> p a q", a=2), kTn[:, c, k0:k0 + 128],
                         qTn[:, 2 * c:2 * c + 2, ii * 64:(ii + 1) * 64])
                sb_ = sb_t.get()
                P.tt(sb_[:], sp[:], bias_sb[:, bt, :], ALU.add)
                e = e_t.get()
                P.act(e[:], sb_[:], AF.Exp)
                state[u] = e

            def emit_pv(u):
                i, ii, kc, nkc, k0, bt = u
                if kc == 0:
                    state["o"] = (o_ps.get(), o_ps.get())
                oa, ob = state["o"]
                e = state.pop(u)
                for h in range(8):
                    op_ = oa if h < 4 else ob
                    P.mm(op_[:, h % 4, :], e[:, h * 64:(h + 1) * 64], V1[:, k0 // 128, h, :],
                         start=(kc == 0 and h % 4 == 0), stop=(kc == nkc - 1 and h % 4 == 3))
                if kc == nkc - 1:
                    rec = rec_t.get()
                    P.recip(rec[:, 0:4], oa[:, :, 64])
                    P.recip(rec[:, 4:8], ob[:, :, 64])
                    obuf = ob_t.get()
                    for h in range(8):
                        op_ = oa if h < 4 else ob
                        P.act(obuf[:, h * 64:(h + 1) * 64], op_[:, h % 4, 0:64], AF.Copy, scale=rec[:, h:h + 1])
                    P.dma(o[i * 64:(i + 1) * 64, :], obuf[:], q='act')

            if units:
                emit_qk(units[0])
            for ui, u in enumerate(units):
                if ui + 1 < len(units):
                    emit_qk(units[ui + 1])
                emit_pv(u)
        P.finish([o])
    return nc


def na_bias_table(rpb, hg):
    col = np.arange(GW)
    cstart = np.clip(col - 8, 0, GW - 16)
    cp = np.arange(GW)[:, None]
    cq = np.arange(GW)[None, :]
    valid = (cp >= cstart[None, :]) & (cp < cstart[None, :] + 16)
    dc = np.clip(cp - cq + 15, 0, 30)
    t = rpb[hg * 8:(hg + 1) * 8][:, :, dc]
    t = np.where(valid[None, None], t, np.float32(-30000.0)).astype(np.float32)
    t = t.transpose(1, 2, 0, 3).reshape(15, 64, 8 * 64)
    m = np.full((1, 64, 512), -30000.0, np.float32)
    return np.ascontiguousarray(np.concatenate([t, m, t[3:4], t[10:11], m], axis=0).reshape(19 * 64, 512))


HG = 64
NFFT = 2 * L


def hyena_consts():
    f64 = np.float64
    i64 = np.arange(64, dtype=f64)[:, None]
    i128 = np.arange(128, dtype=f64)
    a = 2 * np.pi * i64 * i128[None, :] / 128.0
    c = {}
    c["F1"] = np.concatenate([np.cos(a), -np.sin(a)], axis=1)
    t = 2 * np.pi * i128[:, None] * i128[None, :] / NFFT
    c["TW1"] = np.stack([np.cos(t), -np.sin(t)], axis=1)
    c["TW2"] = np.stack([np.cos(t), np.sin(t)], axis=1)
    b = 2 * np.pi * i128[:, None] * i128[None, :] / 128.0
    c["F3"] = np.stack([np.cos(b), -np.sin(b), np.sin(b)], axis=1)
    c["G1"] = np.concatenate([np.cos(b), np.sin(b)], axis=1)
    c["G2"] = np.concatenate([-np.sin(b), np.cos(b)], axis=1)
    a2 = 2 * np.pi * i128[:, None] * i64.T / 128.0
    c["FI"] = np.stack([np.cos(a2) / NFFT, -np.sin(a2) / NFFT], axis=1)
    f32 = np.float32
    tt_ = np.linspace(0.0, 1.0, L, dtype=f32)[:, None]
    bands = 8
    ang = (f32(2.0 * math.pi / L) * np.arange(L, dtype=f32)[:, None]) * np.linspace(1e-4, bands - 1, bands, dtype=f32)[None]
    feats = np.concatenate([tt_, np.cos(ang), -np.sin(ang)], axis=-1)
    c["featsT"] = np.ascontiguousarray(feats.T)
    c["tpos"] = tt_[:, 0]
    return {k: np.ascontiguousarray(v, dtype=np.float32) for k, v in c.items()}


def hyena_window(core):
    deltas = np.abs(np.linspace(DECAY_MIN_, DECAY_MAX_, 512, dtype=np.float32))[core * HG:(core + 1) * HG]
    t = np.linspace(0.0, 1.0, L, dtype=np.float32)
    w = np.exp(-t[:, None] * deltas[None, :])
    return np.ascontiguousarray(w.reshape(64, 128, HG).transpose(0, 2, 1))


DECAY_MIN_ = math.log(1e-2) / 1.5
DECAY_MAX_ = math.log(1e-2) / 0.3


def build_hyena():
    nc = new_nc()
    hz = dram_in(nc, "hz", [B, 3, 64, HG, 130])
    cw = dram_in(nc, "cw", [3 * 3 * HG])
    cbv = dram_in(nc, "cbv", [3 * HG])
    hb = dram_in(nc, "hb", [HG])
    w1 = dram_in(nc, "w1", [17, 64]); b1 = dram_in(nc, "b1", [64])
    w2 = dram_in(nc, "w2", [64, 64]); b2 = dram_in(nc, "b2", [64])
    w3 = dram_in(nc, "w3", [64, 64]); b3 = dram_in(nc, "b3", [64])
    w4 = dram_in(nc, "w4", [64, 2 * HG])
    fr = dram_in(nc, "fr", [64])
    win = dram_in(nc, "win", [64, HG, 128])
    cF1 = dram_in(nc, "F1", [64, 256]); cTW1 = dram_in(nc, "TW1", [128, 2, 128]); cTW2 = dram_in(nc, "TW2", [128, 2, 128])
    cF3 = dram_in(nc, "F3", [128, 3, 128]); cG1 = dram_in(nc, "G1", [128, 256]); cG2 = dram_in(nc, "G2", [128, 256])
    cFI = dram_in(nc, "FI", [128, 2, 64]); featsT = dram_in(nc, "featsT", [17, L])
    o = dram_out(nc, "o", [B, 64, HG, 128])
    TWO_PI = 2.0 * math.pi
    F32R = mybir.dt.float32r
    RR = lambda ap: ap.bitcast(F32R)
    with ExitStack() as st:
        P = Prog(nc, st)
        F1 = P.sb("F1s", [64, 256]); TW1 = P.sb("TW1s", [128, 2, 128]); TW2 = P.sb("TW2s", [128, 2, 128])
        F3 = P.sb("F3s", [128, 3, 128]); G1 = P.sb("G1s", [128, 256]); G2 = P.sb("G2s", [128, 256])
        FI = P.sb("FIs", [128, 2, 64])
        for t_, d_ in ((TW1, cTW1), (TW2, cTW2)):
            P.dma(t_[:], d_)
        cstg = P.sb("cstg", [128, 3, 128])
        for t_, d_ in ((F1, cF1), (F3, cF3), (G1, cG1), (G2, cG2), (FI, cFI)):
            sh = list(t_[:].shape)
            n_el = int(np.prod(sh[1:]))
            sv = cstg[0:sh[0], :, :].rearrange("p a b -> p (a b)")[:, 0:n_el]
            dv = d_ if len(sh) == 2 else d_.rearrange("p a b -> p (a b)")
            tv = t_[:] if len(sh) == 2 else t_[:].rearrange("p a b -> p (a b)")
            P.dma(sv, dv)
            P.copy(RR(tv), sv)
        HK = P.sb("HK", [128, 2, HG, 128])
        At = Rot([P.sb("At%d" % i, [128, 2, 4, 128]) for i in range(1)])
        Yt = Rot([P.sb("Yt%d" % i, [128, 2, 4, 128]) for i in range(1)])
        Bt = Rot([P.sb("Bt%d" % i, [128, 2, 4, 128]) for i in range(1)])
        tm = Rot([P.sb("tm%d" % i, [128, 4, 128]) for i in range(4)])
        ps1 = Rot([P.ps("ps1_%d" % i, [128, 2, 256]) for i in range(2)])
        psX = [P.ps("psXr", [128, 4, 128]), P.ps("psXi", [128, 4, 128])]
        psB = Rot([P.ps("psB%d" % i, [128, 2, 256]) for i in range(2)])
        psY = P.ps("psY", [64, 4, 128])
        pi_c = P.sb("pi_c", [128, 1])
        hs4 = P.sb("hs4", [64, 4, 128])
        P.memset(pi_c[:], -math.pi)

        def cmul(out_r, out_i, ar, ai, br, bi, ns):
            t1, t2, t3, t4 = [tm.get()[:, 0:ns, :] for _ in range(4)]
            P.tt(t1, ar, br, ALU.mult)
            P.tt(t2, ai, bi, ALU.mult)
            P.tt(RR(out_r), t1, t2, ALU.subtract, eng='pool')
            P.tt(t3, ar, bi, ALU.mult)
            P.tt(t4, ai, br, ALU.mult)
            P.tt(RR(out_i), t3, t4, ALU.add, eng='pool')

        def bc(tw, k, ns):
            return tw[:, k, :].unsqueeze(1).to_broadcast([128, ns, 128])

        def fwd4(sig):
            a = At.get()
            for pr in range(2):
                p1 = ps1.get()
                for i in range(2):
                    P.mm(p1[:, i, :], RR(sig[2 * pr + i]), RR(F1[:]))
                cmul(a[:, 0, 2 * pr:2 * pr + 2, :], a[:, 1, 2 * pr:2 * pr + 2, :], p1[:, :, 0:128], p1[:, :, 128:256],
                     bc(TW1, 0, 2), bc(TW1, 1, 2), 2)
            ar = a[:, 0, :, :].rearrange("p s k -> p (s k)")
            ai = a[:, 1, :, :].rearrange("p s k -> p (s k)")
            xr = psX[0][:].rearrange("p s k -> p (s k)")
            xi = psX[1][:].rearrange("p s k -> p (s k)")
            P.mm(xr, RR(F3[:, 0, :]), RR(ar), start=True, stop=False)
            P.mm(xr, RR(F3[:, 2, :]), RR(ai), start=False, stop=True)
            P.mm(xi, RR(F3[:, 0, :]), RR(ai), start=True, stop=False)
            P.mm(xi, RR(F3[:, 1, :]), RR(ar), start=False, stop=True)

        with ExitStack() as st2:
            def sb2(name, shape):
                return st2.enter_context(nc.sbuf_tensor(name, list(shape), F32))
            Hk = sb2("Hk", [64, 2 * HG, 128])
            h3 = sb2("h3", [64, L])
            w1s = sb2("w1s", [17, 64]); w2s = sb2("w2s", [64, 64]); w3s = sb2("w3s", [64, 64]); w4s = sb2("w4s", [64, 2 * HG])
            P.dma(w1s[:], w1); P.dma(w2s[:], w2); P.dma(w3s[:], w3); P.dma(w4s[:], w4)
            bs = sb2("bs", [64, 4])
            for i, v_ in enumerate((b1, b2, b3, fr)):
                P.dma(bs[:, i:i + 1], v_.rearrange("(p o) -> p o", o=1))
            targ = Rot([sb2("targ%d" % i, [64, 512]) for i in range(2)])
            hdt = Rot([sb2("hdt%d" % i, [64, 512]) for i in range(2)])
            fTt = Rot([sb2("fTt%d" % i, [17, 512]) for i in range(1)])
            winc = Rot([sb2("winc%d" % i, [64, 8, 128]) for i in range(1)])
            psm = [psB.t[0], psB.t[1]]
            for j in range(L // 512):
                src = fTt.get()
                P.dma(src[:], featsT[:, j * 512:(j + 1) * 512])
                for layer, wl in enumerate((w1s, w2s, w3s)):
                    pm = psm[layer % 2][0:64, :, :].rearrange("p a b -> p (a b)")
                    P.mm(pm, wl[:], src[:])
                    ta = targ.get()
                    P.ts(ta[:], pm, bs[:, layer:layer + 1], bs[:, 3:4], ALU.add, ALU.mult)
                    tk = targ.get()
                    P.ts(tk[:], ta[:], 1.0 / TWO_PI, 12582912.0, ALU.mult, ALU.add)
                    P.ts(tk[:], tk[:], 12582912.0, -TWO_PI, ALU.subtract, ALU.mult)
                    P.tt(ta[:], ta[:], tk[:], ALU.add)
                    P.ts(ta[:], ta[:], -3.1415925, 3.1415925, ALU.max, ALU.min)
                    dst = h3[:, j * 512:(j + 1) * 512] if layer == 2 else hdt.get()[:]
                    P.act(dst, ta[:], AF.Sin)
                    src = dst if layer == 2 else hdt.t[(hdt.i - 1) % 2]
            for q4 in range(32):
                p4 = psm[q4 % 2][0:64, :, :].rearrange("p a (b c) -> p (a b) c", c=128)
                for a_ in range(4):
                    n2 = q4 * 4 + a_
                    P.mm(p4[:, a_, :], h3[:, n2:L:128], w4s[:])
                P.copy(Hk[:, :, q4 * 4:q4 * 4 + 4], p4.rearrange("p a c -> p c a"), eng=('dve' if q4 % 2 == 0 else 'act'))
            for q in range(8):
                wc = winc.get()
                P.dma(wc[:], win[:, q * 8:(q + 1) * 8, :])
                for d_ in range(2):
                    hv = Hk[:, d_ * HG + q * 8:d_ * HG + (q + 1) * 8, :]
                    P.tt(hv, hv, wc[:], ALU.mult, eng=('dve' if d_ == 0 else 'pool'))
            r1 = sb2("r1", [64, 2 * HG]); r2 = sb2("r2", [64, HG]); sc = sb2("sc", [64, HG])
            ones64 = sb2("ones64", [64, 64])
            P.memset(ones64[:], 1.0)
            for q in range(32):
                sv = tm.t[q % 4][0:64, :, :]
                P.tt(sv, Hk[:, q * 4:(q + 1) * 4, :], Hk[:, q * 4:(q + 1) * 4, :], ALU.mult, eng='pool')
                P.reduce(r1[:, q * 4:(q + 1) * 4], sv, ALU.add)
            P.tt(r2[:], r1[:, 0:HG], r1[:, HG:2 * HG], ALU.add)
            pn = psY[:, 0, 0:HG]
            P.mm(pn, ones64[:], r2[:])
            P.ts(sc[:], pn, EPS, None, ALU.add)
            P.act(sc[:], sc[:], AF.Sqrt)
            P.recip(sc[:], sc[:])
            for d_ in range(2):
                for q in range(4):
                    hv = Hk[:, d_ * HG + q * 16:d_ * HG + (q + 1) * 16, :]
                    P.tt(hv, hv, sc[:, q * 16:(q + 1) * 16].unsqueeze(2).to_broadcast([64, 16, 128]), ALU.mult,
                         eng=('dve' if q % 2 == 0 else 'pool'))
            P.memset(Hk[0:1, HG:2 * HG, 0:1], 0.0)
            for d_ in range(2):
                for g in range(HG // 4):
                    P.copy(RR(hs4[:]), Hk[:, d_ * HG + g * 4:d_ * HG + g * 4 + 4, :], eng='act')
                    fwd4([hs4[:, i, :] for i in range(4)])
                    hr, hi = HK[:, 0, g * 4:g * 4 + 4, :], HK[:, 1, g * 4:g * 4 + 4, :]
                    if d_ == 0:
                        P.copy(hr, psX[0][:])
                        P.copy(hi, psX[1][:], eng='act')
                    else:
                        P.tt(hr, hr, psX[0][:], ALU.add)
                        P.tt(hi, hi, psX[1][:], ALU.subtract)
            P.barrier()
        zin = [P.sb("zin%d" % i, [64, 16, 130]) for i in range(3)]
        ut = [P.sb("ut%d" % i, [64, 16, 128]) for i in range(3)]
        tcv = [P.sb("tcv%d" % i, [64, 16, 128]) for i in range(2)]
        og = Rot([P.sb("og%d" % i, [64, 16, 128]) for i in range(2)])
        sgl = P.sb("sgl", [64, 16, 128])
        cws = P.sb("cws", [64, 9 * HG]); cbs = P.sb("cbs", [64, 3 * HG]); hbs = P.sb("hbs", [64, HG])
        P.dma(cws[:], cw.partition_broadcast(64))
        P.dma(cbs[:], cbv.partition_broadcast(64))
        P.dma(hbs[:], hb.partition_broadcast(64))

        def chb(t_, off, c0):
            return t_[:, off + c0:off + c0 + 16].unsqueeze(2).to_broadcast([64, 16, 128])

        for b_ in range(B):
            for cg in range(HG // 16):
                c0 = cg * 16
                for wh in range(3):
                    P.dma(zin[wh][:], hz[b_, wh, :, c0:c0 + 16, :])
                    eng = 'dve' if wh != 1 else 'pool'
                    u, t2 = ut[wh], tcv[0 if wh != 1 else 1]
                    P.tt(u[:], zin[wh][:, :, 0:128], chb(cws, (wh * 3 + 0) * HG, c0), ALU.mult, eng=eng)
                    P.tt(t2[:], zin[wh][:, :, 1:129], chb(cws, (wh * 3 + 1) * HG, c0), ALU.mult, eng=eng)
                    P.tt(u[:], u[:], t2[:], ALU.add, eng=eng)
                    P.tt(t2[:], zin[wh][:, :, 2:130], chb(cws, (wh * 3 + 2) * HG, c0), ALU.mult, eng=eng)
                    P.tt(u[:], u[:], t2[:], ALU.add, eng=eng)
                    P.tt(u[:], u[:], chb(cbs, wh * HG, c0), ALU.add, eng=eng)
                x0, s_, sb_ = ut[0], ut[2], ut[1]
                P.tt(RR(sgl[:]), ut[2][:], ut[1][:], ALU.mult)
                P.tt(s_[:], ut[2][:], ut[1][:], ALU.mult)
                P.tt(sb_[:], s_[:], chb(hbs, 0, c0), ALU.mult, eng='pool')
                ogt = og.get()
                for sg in range(4):
                    ch0 = c0 + sg * 4
                    fwd4([sgl[:, sg * 4 + i, :] for i in range(4)])
                    y = Yt.get()
                    cmul(y[:, 0, :, :], y[:, 1, :, :], psX[0][:], psX[1][:], HK[:, 0, ch0:ch0 + 4, :], HK[:, 1, ch0:ch0 + 4, :], 4)
                    bt = Bt.get()
                    for pr in range(2):
                        pb = psB.get()
                        for i in range(2):
                            P.mm(pb[:, i, :], RR(y[:, 0, 2 * pr + i, :]), RR(G1[:]), start=True, stop=False)
                            P.mm(pb[:, i, :], RR(y[:, 1, 2 * pr + i, :]), RR(G2[:]), start=False, stop=True)
                        cmul(bt[:, 0, 2 * pr:2 * pr + 2, :], bt[:, 1, 2 * pr:2 * pr + 2, :], pb[:, :, 0:128], pb[:, :, 128:256],
                             bc(TW2, 0, 2), bc(TW2, 1, 2), 2)
                    py = psY[:].rearrange("p s k -> p (s k)")
                    P.mm(py, RR(FI[:, 0, :]), RR(bt[:, 0, :, :].rearrange("p s k -> p (s k)")), start=True, stop=False)
                    P.mm(py, RR(FI[:, 1, :]), RR(bt[:, 1, :, :].rearrange("p s k -> p (s k)")), start=False, stop=True)
                    ov = ogt[:, sg * 4:sg * 4 + 4, :]
                    P.tt(ov, psY[:], sb_[:, sg * 4:sg * 4 + 4, :], ALU.add)
                    P.tt(ov, ov, x0[:, sg * 4:sg * 4 + 4, :], ALU.mult, eng='pool')
                P.dma(o[b_, :, c0:c0 + 16, :], ogt[:], q='act')
        P.finish([o])
    return nc


RS = 1664
DEC = math.exp(-0.5)


def rwkv_consts():
    s = np.arange(128)[:, None]
    t = np.arange(128)[None, :]
    mU, mSU, mSL, idn = (s <= t), (s < t), (s > t), (s == t)
    f = lambda m: m.astype(np.float32)
    return np.ascontiguousarray(np.stack([f(mU), f(mSU), f(mSL), f(idn), -f(mSL), f(mSL), -f(mSU), -f(mU)], axis=1))


def build_rwkv(Lc=L, dbg=9, use_r=True):
    nc = new_nc()
    zp = dram_in(nc, "zp", [Lc + 1, RS])
    mu = dram_in(nc, "mu", [RS])
    w0 = dram_in(nc, "w0", [512]); wup = dram_in(nc, "wup", [64, 512])
    a0 = dram_in(nc, "a0", [512]); aup = dram_in(nc, "aup", [64, 512])
    kkv = dram_in(nc, "kk", [512]); kav = dram_in(nc, "ka", [512]); rkv = dram_in(nc, "rk", [512])
    tri = dram_in(nc, "tri", [128, 8, 128])
    ys = dram_out(nc, "ys", [Lc, 512])
    bon = dram_out(nc, "bon", [Lc, 512])
    NCH = Lc // 128
    NS = 4
    F32R = mybir.dt.float32r
    RR = (lambda ap: ap.bitcast(F32R)) if use_r else (lambda ap: ap)
    with ExitStack() as st:
        P = Prog(nc, st)
        tr_ = P.sb("tri_s", [128, 8, 128])
        P.dma(tr_[:], tri)
        mU, mSU, mSL, ident = tr_[:, 0, :], tr_[:, 1, :], tr_[:, 2, :], tr_[:, 3, :]
        mask4 = tr_[:, 4:8, :]
        mu_s = P.sb("mu_s", [128, RS]); P.dma(mu_s[:], mu.partition_broadcast(128))
        kk_s = P.sb("kk_s", [128, 512]); P.dma(kk_s[:], kkv.partition_broadcast(128))
        ka_s = P.sb("ka_s", [128, 512]); P.dma(ka_s[:], kav.partition_broadcast(128))
        rk_s = P.sb("rk_s", [128, 512]); P.dma(rk_s[:], rkv.partition_broadcast(128))
        wup_s = P.sb("wup_s", [64, 512]); P.dma(wup_s[:], wup)
        aup_s = P.sb("aup_s", [64, 512]); P.dma(aup_s[:], aup)
        rows = P.sb("rows", [1, 2, 512])
        P.dma(rows[:, 0, :], w0.rearrange("(o n) -> o n", o=1))
        P.dma(rows[:, 1, :], a0.rearrange("(o n) -> o n", o=1))
        ones = P.sb("ones", [128, 128]); P.memset(ones[:], 1.0)
        ST = [P.sb("ST%d" % h, [64, 64]) for h in range(8)]
        zer = P.sb("zer", [64, 64]); P.memset(zer[:], 0.0)
        for h in range(8):
            P.copy(RR(ST[h][:]), zer[:], eng='pool')
        prev_t = Rot([P.sb("prev%d" % i, [128, RS]) for i in range(2)])
        cur_t = Rot([P.sb("cur%d" % i, [128, RS]) for i in range(2)])
        zd_t = Rot([P.sb("zd%d" % i, [128, RS]) for i in range(2)])
        Q4_t = Rot([P.sb("Q4_%d" % i, [128, 4, 512]) for i in range(2)])
        QT_t = Rot([P.sb("QT_%d" % i, [64, 4, 8, 128]) for i in range(2)])
        KH_t = Rot([P.sb("KH_%d" % i, [128, 2, 512]) for i in range(2)])
        gC_t = Rot([P.sb("gC_%d" % i, [64, 8]) for i in range(2)])
        KT_t = Rot([P.sb("KT_%d" % i, [128, 512]) for i in range(2)])
        wT = P.sb("wT", [64, 2, 128])
        zdt = P.sb("zdt", [128, RS])
        sg = P.sb("sg", [128, 512]); av = P.sb("av", [128, 512])
        Gt = P.sb("Gt", [128, 512]); Gp = P.sb("Gp", [128, 512]); Gi = P.sb("Gi", [128, 512]); Gr = P.sb("Gr", [128, 512])
        kap = P.sb("kap", [128, 512]); kmod = P.sb("kmod", [128, 512]); beta = P.sb("beta", [128, 512])
        T1 = P.sb("T1", [128, 512]); T2 = P.sb("T2", [128, 512])
        s8 = P.sb("s8", [128, 4, 8])
        ys_t = Rot([P.sb("ys_t%d" % i, [128, 512]) for i in range(2)])
        bn_t = Rot([P.sb("bn_t%d" % i, [128, 512]) for i in range(2)])
        SL = []
        for s_ in range(NS):
            d_ = {"SCr": P.sb("SCr%d" % s_, [128, 4, 128]), "SC": P.sb("SC%d" % s_, [128, 4, 128]),
                  "Nk": P.sb("Nk%d" % s_, [128, 128]), "W2": P.sb("W2_%d" % s_, [128, 128]),
                  "W1": P.sb("W1_%d" % s_, [64, 128]), "U": P.sb("U_%d" % s_, [128, 64])}
            for nm in ("B", "BT", "IBT", "Pm"):
                d_[nm] = Rot([P.sb("%s%d_%d" % (nm, s_, i), [128, 128]) for i in range(2)])
            SL.append(d_)
        bkP = Rot([P.ps("bkP%d" % i, [128, 512]) for i in range(2)])
        bkA = [P.ps("bkA%d" % i, [128, 512]) for i in range(NS)]
        bkD = [P.ps("bkD%d" % i, [128, 512]) for i in range(NS // 2)]

        def v8(t_):
            return t_.rearrange("p (h j) -> p h j", j=64)

        def bc8(t_):
            return t_.unsqueeze(2).to_broadcast([128, 8, 64])

        def head_gen(h, s_, zd, Q4, QT, KH, gC, yst):
            hs = slice(h * 64, (h + 1) * 64)
            T_ = SL[s_]

            def mm(o_, l_, r_, **kw):
                return P.mm(o_, RR(l_), RR(r_), **kw)

            A_, D_ = bkA[s_], bkD[s_ // 2]
            c0 = (s_ % 2) * 256
            SCr, SC, Nk, W1, W2, U = T_["SCr"], T_["SC"], T_["Nk"], T_["W1"], T_["W2"], T_["U"]
            mm(A_[:, 0:256].rearrange("p (a t) -> p a t", a=2), QT[:, 0, h, :], QT[:, 2:4, h, :])
            mm(A_[:, 256:512].rearrange("p (a t) -> p a t", a=2), QT[:, 2, h, :], QT[:, 0:2, h, :])
            mm(D_[:, c0 + 128:c0 + 256], QT[:, 3, h, :], QT[:, 1, h, :])
            P.copy(SCr[:].rearrange("p a t -> p (a t)"), A_[:], eng='act')
            P.tt(RR(SC[:]), SCr[:], mask4, ALU.mult, eng='pool')
            P.tt(RR(Nk[:]), D_[:, c0 + 128:c0 + 256], mU, ALU.mult)
            AT, MkT, A, nNb = SC[:, 0, :], SC[:, 1, :], SC[:, 2, :], SC[:, 3, :]
            Pm = T_["Pm"].get()
            P.tt(RR(Pm[:]), A, ident, ALU.add, eng='pool')
            yield
            Bm, BTm = A, AT
            for kk_ in range(1, 7):
                last = (kk_ == 6)
                if not last:
                    mm(A_[:, 0:128], BTm, Bm)
                mm(D_[:, c0:c0 + 128], Bm, BTm)
                IBT = T_["IBT"].get()
                P.tt(RR(IBT[:]), D_[:, c0:c0 + 128], ident, ALU.add)
                if not last:
                    Bn, BTn = T_["B"].get(), T_["BT"].get()
                    P.copy(RR(BTn[:]), D_[:, c0:c0 + 128])
                    P.copy(RR(Bn[:]), A_[:, 0:128], eng='act')
                yield
                mm(A_[:, 128:256], IBT[:], Pm[:])
                Pn = T_["Pm"].get()
                P.copy(RR(Pn[:]), A_[:, 128:256], eng='act')
                Pm = Pn
                if not last:
                    Bm, BTm = Bn[:], BTn[:]
                yield
            mm(A_[0:64, 256:384], Q4[:, hs], Pm[:])
            mm(A_[:, 384:512], MkT, Pm[:])
            P.copy(RR(W1[:]), A_[0:64, 256:384], eng='act')
            P.copy(RR(W2[:]), A_[:, 384:512], eng='act')
            yield
            vh = zd[:, 1024 + h * 64:1024 + (h + 1) * 64]
            mm(D_[:, c0:c0 + 64], W2[:], vh, start=True, stop=False)
            mm(D_[:, c0:c0 + 64], W1[:], ST[h][:], start=False, stop=True)
            P.copy(RR(U[:]), D_[:, c0:c0 + 64])
            yield
            mm(D_[:, c0 + 64:c0 + 128], QT[:, 1, h, :], ST[h][:], start=True, stop=False)
            mm(D_[:, c0 + 64:c0 + 128], Nk[:], vh, start=False, stop=False)
            mm(D_[:, c0 + 64:c0 + 128], nNb, U[:], start=False, stop=True)
            mm(D_[0:64, c0 + 128:c0 + 192], KH[:, 0, hs], vh, start=True, stop=False)
            mm(D_[0:64, c0 + 128:c0 + 192], KH[:, 1, hs], U[:], start=False, stop=True)
            P.copy(yst[:, hs], D_[:, c0 + 64:c0 + 128])
            P.stt(RR(ST[h][:]), ST[h][:], gC[:, h:h + 1], D_[0:64, c0 + 128:c0 + 192], ALU.mult, ALU.add)

        ctxs = {}

        def prep_gen(ci):
            prev, cur, zd = prev_t.get(), cur_t.get(), zd_t.get()
            P.dma(prev[:], zp[ci * 128:ci * 128 + 128, :])
            P.dma(cur[:], zp[ci * 128 + 1:ci * 128 + 129, :])
            P.tt(zdt[:], prev[:], cur[:], ALU.subtract, eng='pool')
            P.tt(zdt[:], zdt[:], mu_s[:], ALU.mult, eng='pool')
            P.tt(RR(zd[:]), zdt[:], cur[:], ALU.add, eng='pool')
            yield
            r, k, v = zd[:, 0:512], zd[:, 512:1024], zd[:, 1024:1536]
            pb = bkP.get()
            pT = pb[0:64, 0:256].rearrange("p (a t) -> p a t", a=2)
            P.tr(pT[:, 0, :], zd[:, 1536:1600], ident)
            P.tr(pT[:, 1, :], zd[:, 1600:1664], ident)
            P.act(wT[:, 0, :], pT[:, 0, :], AF.Tanh)
            yield
            P.copy(wT[:, 1, :], pT[:, 1, :], eng='act')
            pb = bkP.get()
            P.mm(pb[:], wT[:, 0, :], wup_s[:], start=True, stop=False)
            P.mm(pb[:], ones[0:1, :], rows[:, 0, :], start=False, stop=True)
            P.act(sg[:], pb[:], AF.Sigmoid)
            yield
            pb = bkP.get()
            P.mm(pb[:], wT[:, 1, :], aup_s[:], start=True, stop=False)
            P.mm(pb[:], ones[0:1, :], rows[:, 1, :], start=False, stop=True)
            P.act(av[:], pb[:], AF.Sigmoid)
            yield
            pb = bkP.get()
            P.mm(pb[:], mU, sg[:])
            P.act(Gt[:], pb[:], AF.Exp, scale=-DEC)
            yield
            P.act(Gi[:], pb[:], AF.Exp, scale=DEC)
            yield
            pb = bkP.get()
            P.mm(pb[:], mSU, sg[:])
            P.act(Gp[:], pb[:], AF.Exp, scale=-DEC)
            yield
            pb = bkP.get()
            P.mm(pb[:], mSL, sg[:])
            P.act(Gr[:], pb[:], AF.Exp, scale=-DEC)
            yield
            pb = bkP.get()
            for h in range(8):
                P.mm(pb[0:64, h:h + 1], sg[:, h * 64:(h + 1) * 64], ones[:, 0:1])
            gC = gC_t.get()
            P.act(gC[:], pb[0:64, 0:8], AF.Exp, scale=-DEC)
            yield
            P.tt(T1[:], k, kk_s[:], ALU.mult)
            P.tt(T2[:], T1[:], T1[:], ALU.mult, eng='pool')
            P.reduce(s8[:, 0, :], v8(T2[:]), ALU.add)
            P.act(s8[:, 1, :], s8[:, 0, :], AF.Sqrt)
            yield
            P.ts(s8[:, 1, :], s8[:, 1, :], 1e-12, None, ALU.max)
            P.recip(s8[:, 1, :], s8[:, 1, :])
            P.tt(v8(kap[:]), v8(T1[:]), bc8(s8[:, 1, :]), ALU.mult)
            P.stt(T2[:], av[:], -1.0, ka_s[:], ALU.add, ALU.mult)
            P.stt(kmod[:], T2[:], 1.0, k, ALU.add, ALU.mult)
            P.tt(beta[:], kap[:], av[:], ALU.mult, eng='pool')
            P.tt(T1[:], r, kmod[:], ALU.mult, eng='pool')
            P.tt(T1[:], T1[:], rk_s[:], ALU.mult, eng='pool')
            P.reduce(s8[:, 2, :], v8(T1[:]), ALU.add)
            bn = bn_t.get()
            P.tt(v8(bn[:]), v8(v), bc8(s8[:, 2, :]), ALU.mult, eng='pool')
            P.dma(bon[ci * 128:(ci + 1) * 128, :], bn[:], q='act')
            yield
            Q4, KH = Q4_t.get(), KH_t.get()
            P.tt(Q4[:, 0, :], kap[:], Gp[:], ALU.mult)
            KT = KT_t.get()
            P.tt(RR(KT[:]), kap[:], Gp[:], ALU.mult)
            P.tt(Q4[:, 1, :], r, Gt[:], ALU.mult, eng='pool')
            P.tt(Q4[:, 2, :], beta[:], Gi[:], ALU.mult)
            P.tt(Q4[:, 3, :], kmod[:], Gi[:], ALU.mult, eng='pool')
            P.tt(RR(KH[:, 0, :]), kmod[:], Gr[:], ALU.mult, eng='pool')
            P.stt(RR(KH[:, 1, :]), beta[:], -1.0, Gr[:], ALU.mult, ALU.mult)
            yield
            QT = QT_t.get()
            for h in range(8):
                pb = bkP.get()
                pq = pb[0:64, :].rearrange("p (q t) -> p q t", q=4)
                for q in range(4):
                    P.tr(pq[:, q, :], Q4[:, q, h * 64:(h + 1) * 64], ident)
                P.copy(RR(QT[:, :, h, :]), pq, eng=('act' if h % 2 == 0 else 'dve'))
                yield
            ctxs[ci] = (zd, KT, QT, KH, gC)

        for _ in prep_gen(0):
            pass
        for ci in range(NCH):
            zd, Q4, QT, KH, gC = ctxs.pop(ci)
            yst = ys_t.get()
            todo = list(range(8))
            active = {}
            pg = prep_gen(ci + 1) if ci + 1 < NCH else None
            while todo or active or pg is not None:
                for s_ in range(NS):
                    if s_ not in active and todo:
                        active[s_] = head_gen(todo.pop(0), s_, zd, Q4, QT, KH, gC, yst)
                    if s_ in active:
                        try:
                            next(active[s_])
                        except StopIteration:
                            del active[s_]
                if pg is not None:
                    try:
                        next(pg)
                    except StopIteration:
                        pg = None
            P.dma(ys[ci * 128:(ci + 1) * 128, :], yst[:], q='act')
        P.finish([ys, bon])
    return nc


def build_rwkv_post(T):
    nc = new_nc()
    ysf = dram_in(nc, "ysf", [T, 512]); ysb = dram_in(nc, "ysb", [T, 512])
    bf = dram_in(nc, "bf", [T, 512]); bb = dram_in(nc, "bb", [T, 512])
    gdT = dram_in(nc, "gdT", [128, T])
    gup = dram_in(nc, "gup", [128, 512])
    lnw = dram_in(nc, "lnw", [512]); lnb = dram_in(nc, "lnb", [512])
    ya = dram_out(nc, "ya", [T, 512])
    with ExitStack() as st:
        P = Prog(nc, st)
        gup_s = P.sb("gup_s", [128, 512]); P.dma(gup_s[:], gup)
        lnw_s = P.sb("lnw_s", [128, 512]); P.dma(lnw_s[:], lnw.partition_broadcast(128))
        lnb_s = P.sb("lnb_s", [128, 512]); P.dma(lnb_s[:], lnb.partition_broadcast(128))
        gd_s = P.sb("gd_s", [128, T]); P.dma(gd_s[:], gdT)
        P.act(gd_s[:], gd_s[:], AF.Sigmoid)
        ins_t = [Rot([P.sb("in%d_%d" % (q, i), [128, 512]) for i in range(2)]) for q in range(4)]
        y_t = Rot([P.sb("y%d" % i, [128, 512]) for i in range(2)])
        sq = P.sb("sq", [128, 512])
        s8 = P.sb("s8", [128, 3, 8])
        o_t = Rot([P.sb("o%d" % i, [128, 512]) for i in range(2)])
        psg = Rot([P.ps("psg%d" % i, [128, 512]) for i in range(2)])

        def v8(t_):
            return t_.rearrange("p (h j) -> p h j", j=64)

        def bc8(t_):
            return t_.unsqueeze(2).to_broadcast([128, 8, 64])

        for t0 in range(0, T, 128):
            tl = [r_.get() for r_ in ins_t]
            for q, src in enumerate((ysf, ysb, bf, bb)):
                P.dma(tl[q][:], src[t0:t0 + 128, :])
            y = y_t.get()
            P.tt(y[:], tl[0][:], tl[1][:], ALU.add)
            P.reduce(s8[:, 0, :], v8(y[:]), ALU.add)
            P.ts(s8[:, 0, :], s8[:, 0, :], -1.0 / 64, None, ALU.mult)
            P.tt(v8(y[:]), v8(y[:]), bc8(s8[:, 0, :]), ALU.add)
            P.tt(sq[:], y[:], y[:], ALU.mult, eng='pool')
            P.reduce(s8[:, 1, :], v8(sq[:]), ALU.add)
            P.ts(s8[:, 1, :], s8[:, 1, :], 1.0 / 64, 64e-5, ALU.mult, ALU.add)
            P.act(s8[:, 1, :], s8[:, 1, :], AF.Sqrt)
            P.recip(s8[:, 1, :], s8[:, 1, :])
            P.tt(v8(y[:]), v8(y[:]), bc8(s8[:, 1, :]), ALU.mult)
            P.tt(y[:], y[:], lnw_s[:], ALU.mult, eng='pool')
            P.tt(y[:], y[:], lnb_s[:], ALU.add, eng='pool')
            P.tt(tl[2][:], tl[2][:], tl[3][:], ALU.add, eng='pool')
            P.tt(y[:], y[:], tl[2][:], ALU.add, eng='pool')
            pg = psg.get()
            P.mm(pg[:], gd_s[:, t0:t0 + 128], gup_s[:])
            o = o_t.get()
            P.tt(o[:], pg[:], y[:], ALU.mult)
            P.dma(ya[t0:t0 + 128, :], o[:], q='act')
        P.finish([ya])
    return nc


TSH = L // 2


def _shardT(a):
    return [np.ascontiguousarray(a[c // 2, (c % 2) * TSH:(c % 2 + 1) * TSH].T) for c in range(NCORES)]


def _unshardT(lst):
    out = np.empty((B, L, lst[0].shape[0]), np.float32)
    for c in range(NCORES):
        out[c // 2, (c % 2) * TSH:(c % 2 + 1) * TSH] = lst[c].T
    return out


def _shard_tok(a):
    return [np.ascontiguousarray(a[c // 2, (c % 2) * TSH:(c % 2 + 1) * TSH]) for c in range(NCORES)]


def _unshard_tok(lst):
    out = np.empty((B, L, lst[0].shape[1]), np.float32)
    for c in range(NCORES):
        out[c // 2, (c % 2) * TSH:(c % 2 + 1) * TSH] = lst[c]
    return out


_NC_CACHE = {}


def _nc(key, fn):
    if key not in _NC_CACHE:
        _NC_CACHE[key] = fn()
    return _NC_CACHE[key]


def linear_launch(aT_sh, W, g=None, mode='plain', resT_sh=None, W2=None, a2T_sh=None):
    K_, N_ = W.shape
    K2 = 0 if W2 is None else W2.shape[0]
    nc = _nc(("lin", K_, N_, g is not None, mode, K2), lambda: build_linear(TSH, K_, N_, g is not None, mode, K2))
    ins = []
    for c in range(NCORES):
        m = {"aT": aT_sh[c], "W": np.ascontiguousarray(W)}
        if g is not None:
            m["g"] = np.ascontiguousarray(g)
        if mode == 'res':
            m["resT"] = resT_sh[c]
        if mode == 'ple':
            m["W2"] = np.ascontiguousarray(W2)
            m["a2T"] = a2T_sh[c]
        ins.append(m)
    return [r["oT"] for r in run(nc, ins)]


def ffn_launch(hT_sh, g, Wu, cw, cb, Wd):
    nc = _nc(("ffn",), lambda: build_ffn(TSH))
    ins = []
    for c in range(NCORES):
        left = hT_sh[c - 1][:, -1:] if c % 2 == 1 else np.zeros((D, 1), np.float32)
        right = hT_sh[c + 1][:, :1] if c % 2 == 0 else np.zeros((D, 1), np.float32)
        ins.append({"hTp": np.ascontiguousarray(np.concatenate([left, hT_sh[c], right], axis=1)), "g": np.ascontiguousarray(g),
                    "Wu": np.ascontiguousarray(Wu), "cw": np.ascontiguousarray(cw), "cb": np.ascontiguousarray(cb),
                    "Wd": np.ascontiguousarray(Wd)})
    return [r["oT"] for r in run(nc, ins)]


def rwkv_launch(z, p):
    tri = rwkv_consts()
    ins = []
    for c in range(NCORES):
        b, dr = c // 2, c % 2
        zs = z[b, :, :RS]
        if dr == 1:
            zs = zs[::-1]
        zp = np.concatenate([np.zeros((1, RS), np.float32), zs], 0)
        ins.append({"zp": np.ascontiguousarray(zp), "mu": np.ascontiguousarray(p["rwkv_mu"][0, dr]),
                    "w0": np.ascontiguousarray(p["rwkv_w0"][0, dr]), "wup": np.ascontiguousarray(p["rwkv_w_up"][0, dr]),
                    "a0": np.ascontiguousarray(p["rwkv_a0"][0, dr]), "aup": np.ascontiguousarray(p["rwkv_a_up"][0, dr]),
                    "kk": np.ascontiguousarray(p["rwkv_k_k"][0]), "ka": np.ascontiguousarray(p["rwkv_k_a"][0]),
                    "rk": np.ascontiguousarray(p["rwkv_r_k"][0].reshape(-1)), "tri": tri})
    res = run(_nc(("rwkv",), lambda: build_rwkv(L)), ins)
    ysf = np.stack([res[2 * b]["ys"] for b in range(B)])
    ysb = np.stack([res[2 * b + 1]["ys"][::-1] for b in range(B)])
    bf = np.stack([res[2 * b]["bon"] for b in range(B)])
    bb = np.stack([res[2 * b + 1]["bon"][::-1] for b in range(B)])
    gdT = _shardT(z[:, :, RS:RS + 128])
    sh = [_shard_tok(a) for a in (ysf, ysb, bf, bb)]
    ins = [{"ysf": sh[0][c], "ysb": sh[1][c], "bf": sh[2][c], "bb": sh[3][c], "gdT": gdT[c],
            "gup": np.ascontiguousarray(p["rwkv_g_up"][0]), "lnw": np.ascontiguousarray(p["rwkv_ln_w"][0]),
            "lnb": np.ascontiguousarray(p["rwkv_ln_b"][0])} for c in range(NCORES)]
    res = run(_nc(("rwkvpost",), lambda: build_rwkv_post(TSH)), ins)
    return _unshard_tok([r["ya"] for r in res])


def hyena_launch(z, p):
    C = hyena_consts()
    zz = z[:, :, 1792:]
    zpad = np.pad(zz, ((0, 0), (1, 1), (0, 0)))
    idx = (np.arange(64)[:, None] * 128 + np.arange(130)[None, :])
    sw, sbias, w4 = p["hy_short_w"][0], p["hy_short_b"][0], p["hy_f_w4"][0]
    ins = []
    for c in range(NCORES):
        hz = np.empty((B, 3, 64, HG, 130), np.float32)
        for wh in range(3):
            cols = wh * 512 + c * HG + np.arange(HG)
            hz[:, wh] = zpad[:, :, cols][:, idx, :].transpose(0, 1, 3, 2)
        cw = np.stack([sw[:, wh * 512 + c * HG: wh * 512 + (c + 1) * HG] for wh in range(3)])
        cb = np.stack([sbias[wh * 512 + c * HG: wh * 512 + (c + 1) * HG] for wh in range(3)])
        w4c = np.concatenate([w4[:, dd * 512 + c * HG: dd * 512 + (c + 1) * HG] for dd in range(2)], axis=1)
        m = {"hz": hz, "cw": np.ascontiguousarray(cw.reshape(-1)), "cbv": np.ascontiguousarray(cb.reshape(-1)),
             "hb": np.ascontiguousarray(p["hy_bias"][0][c * HG:(c + 1) * HG]),
             "w1": np.ascontiguousarray(p["hy_f_w1"][0]), "b1": np.ascontiguousarray(p["hy_f_b1"][0]),
             "w2": np.ascontiguousarray(p["hy_f_w2"][0]), "b2": np.ascontiguousarray(p["hy_f_b2"][0]),
             "w3": np.ascontiguousarray(p["hy_f_w3"][0]), "b3": np.ascontiguousarray(p["hy_f_b3"][0]),
             "w4": np.ascontiguousarray(w4c), "fr": np.ascontiguousarray(p["hy_f_freq"][0]), "win": hyena_window(c)}
        for k_ in ("F1", "TW1", "TW2", "F3", "G1", "G2", "FI", "featsT"):
            m[k_] = C[k_]
        ins.append(m)
    res = run(_nc(("hyena",), build_hyena), ins)
    yb = np.empty((B, L, 512), np.float32)
    for c in range(NCORES):
        yb[:, :, c * HG:(c + 1) * HG] = res[c]["o"].transpose(0, 1, 3, 2).reshape(B, L, HG)
    return yb


def na_launch(z, p):
    ins = []
    for c in range(NCORES):
        b, hg = c // 2, c % 2
        ins.append({"qT": np.ascontiguousarray(z[b, :, hg * 512:(hg + 1) * 512].T),
                    "kT": np.ascontiguousarray(z[b, :, D + hg * 512:D + (hg + 1) * 512].T),
                    "v": np.ascontiguousarray(z[b, :, 2 * D + hg * 512:2 * D + (hg + 1) * 512]),
                    "bias": na_bias_table(p["na_rpb"][0], hg), "qg": np.ascontiguousarray(p["na_q_g"][0]),
                    "kg": np.ascontiguousarray(p["na_k_g"][0])})
    res = run(_nc(("na",), build_na), ins)
    att = np.empty((B, L, D), np.float32)
    for c in range(NCORES):
        att[c // 2, :, (c % 2) * 512:(c % 2 + 1) * 512] = res[c]["o"]
    return att


def kernel(**p):
    p = {k: np.asarray(v, dtype=np.float32) for k, v in p.items()}
    hT = _shardT(p["x"])
    zT = linear_launch(hT, p["mix_w_in"][0], g=p["mix_norm"][0])
    z = _unshardT(zT)
    ya = rwkv_launch(z, p)
    yb = hyena_launch(z, p)
    yT = _shardT(np.concatenate([ya, yb], axis=-1))
    hT = linear_launch(yT, p["mix_w_out"][0], mode='res', resT_sh=hT)
    hT = ffn_launch(hT, p["ffn_norm"][0], p["ffn_w_up"][0], p["ffn_conv_w"][0], p["ffn_conv_b"][0], p["ffn_w_down"][0])
    hT = linear_launch(hT, p["ple_w_gate"][0], g=p["ple_norm"][0], mode='ple', W2=p["ple_w_proj"][0],
                       a2T_sh=_shardT(p["p"][0]))
    zT = linear_launch(hT, p["na_w_qkv"][0], g=p["na_norm"][0])
    att = na_launch(_unshardT(zT), p)
    hT = linear_launch(_shardT(att), p["na_w_out"][0], mode='res', resT_sh=hT)
    hT = ffn_launch(hT, p["ffn_norm"][1], p["ffn_w_up"][1], p["ffn_conv_w"][1], p["ffn_conv_b"][1], p["ffn_w_down"][1])
    hT = linear_launch(hT, p["ple_w_gate"][1], g=p["ple_norm"][1], mode='ple', W2=p["ple_w_proj"][1],
                       a2T_sh=_shardT(p["p"][1]))
    return _unshardT(hT)
```

```python
from contextlib import ExitStack
import math
import numpy as np
import concourse.bass as bass
import concourse.mybir as mybir
from concourse.bass_utils import run_bass_kernel_spmd

F32 = mybir.dt.float32
BF16 = mybir.dt.bfloat16
AF = mybir.ActivationFunctionType
ALU = mybir.AluOpType
AX = mybir.AxisListType

NCORES = 8
D = 1024
B = 4
L = 8192
EPS = 1e-6


class Prog:
    NDMA = 24
    SEM_LIMIT = 6000
    EMBED = True

    def __init__(self, nc, stack):
        self.nc = nc
        self.st = stack
        self.E = {'pe': nc.tensor, 'dve': nc.vector, 'act': nc.scalar, 'pool': nc.gpsimd, 'sp': nc.sync}
        self.sem = {e: stack.enter_context(nc.semaphore("s_" + e)) for e in ('pe', 'dve', 'act', 'pool')}
        self.dsem = [stack.enter_context(nc.semaphore("d%d" % i)) for i in range(self.NDMA)]
        self.dcnt = [0] * self.NDMA
        self.dnext = 0
        self.cnt = {e: 0 for e in self.sem}
        self.gen = {e: 0 for e in self.sem}
        self.allsem = {(e, 0): self.sem[e] for e in self.sem}
        self.seen = {e: {} for e in self.E}
        self.lastw = {}
        self.readers = {}
        self.nuniq = 0
        self.psum_names = set()

    def sb(self, name, shape, dt=F32):
        return self.st.enter_context(self.nc.sbuf_tensor(name, list(shape), dt))

    def ps(self, name, shape, dt=F32):
        self.psum_names.add(name)
        return self.st.enter_context(self.nc.psum_tensor(name, list(shape), dt))

    @staticmethod
    def key(ap):
        if isinstance(ap, str):
            return ap
        return ap.tensor.name

    def _need(self, eng, ev, waits):
        if ev is None:
            return
        src, n = ev
        if src[0] == eng and eng == 'pe':
            return
        if self.seen[eng].get(src, 0) >= n:
            return
        waits[src] = max(waits.get(src, 0), n)

    def _deps(self, eng, reads, writes, embed=False):
        waits = {}
        for k in reads:
            self._need(eng, self.lastw.get(k), waits)
            if k in self.psum_names:
                for ev in self.readers.get(k, {}).items():
                    if ev[0][0] != eng:
                        self._need(eng, ev, waits)
        for k in writes:
            self._need(eng, self.lastw.get(k), waits)
            for ev in self.readers.get(k, {}).items():
                self._need(eng, ev, waits)
        items = list(waits.items())
        emb = None
        if embed and items:
            emb = items.pop()
        for src, n in items:
            s = self.dsem[src[1]] if src[0] == 'd' else self.allsem[src]
            self.E[eng].wait_ge(s, n)
            self.seen[eng][src] = n
        if emb is not None:
            src, n = emb
            self.seen[eng][src] = n
            return (self.dsem[src[1]] if src[0] == 'd' else self.allsem[src], n)
        return None

    def _commit(self, ev, reads, writes):
        for k in writes:
            self.lastw[k] = ev
            self.readers[k] = {}
        for k in reads:
            r = self.readers.setdefault(k, {})
            r[ev[0]] = max(r.get(ev[0], 0), ev[1])

    def op(self, eng, fn, reads, writes):
        reads = [self.key(a) for a in reads if a is not None and not isinstance(a, (int, float))]
        writes = [self.key(a) for a in writes if a is not None]
        emb = self._deps(eng, reads, writes, embed=self.EMBED)
        if self.cnt[eng] >= self.SEM_LIMIT:
            self.gen[eng] += 1
            self.cnt[eng] = 0
            self.sem[eng] = self.st.enter_context(self.nc.semaphore("s_%s_%d" % (eng, self.gen[eng])))
            self.allsem[(eng, self.gen[eng])] = self.sem[eng]
        ins = fn(self.E[eng])
        if emb is not None:
            ins._wait_ge(emb[0], emb[1])
        self.cnt[eng] += 1
        ins.then_inc(self.sem[eng], 1)
        self._commit(((eng, self.gen[eng]), self.cnt[eng]), reads, writes)
        return ins

    def dma(self, out, in_, q='sp', **kw):
        reads = [self.key(in_)]
        writes = [self.key(out)]
        i = self.dnext
        self.dnext = (self.dnext + 1) % self.NDMA
        if self.dcnt[i] > 0:
            w = {}
            self._need(q, (('d', i), self.dcnt[i]), w)
            for src, n in w.items():
                self.E[q].wait_ge(self.dsem[i], n)
                self.seen[q][src] = n
        self._deps(q, reads, writes)
        ins = self.E[q].dma_start(out=out, in_=in_, **kw)
        self.dcnt[i] += 16
        ins.then_inc(self.dsem[i], 16)
        self._commit((('d', i), self.dcnt[i]), reads, writes)
        return ins

    def barrier(self):
        for eng in self.E:
            for src in self.sem:
                key = (src, self.gen[src])
                if src != eng and self.cnt[src] > self.seen[eng].get(key, 0):
                    self.E[eng].wait_ge(self.sem[src], self.cnt[src])
                    self.seen[eng][key] = self.cnt[src]
            for i in range(self.NDMA):
                if self.dcnt[i] > self.seen[eng].get(('d', i), 0):
                    self.E[eng].wait_ge(self.dsem[i], self.dcnt[i])
                    self.seen[eng][('d', i)] = self.dcnt[i]

    def finish(self, keys):
        for eng in ('sp', 'pool'):
            self._deps(eng, [self.key(k) for k in keys], [])

    def mm(self, out, lhsT, rhs, start=True, stop=True):
        return self.op('pe', lambda e: e.matmul(out, lhsT, rhs, start=start, stop=stop), [lhsT, rhs], [out])

    def tr(self, out, in_, ident):
        return self.op('pe', lambda e: e.transpose(out, in_, ident), [in_, ident], [out])

    def act(self, out, in_, func, bias=0.0, scale=1.0, accum_out=None):
        kw = {}
        if accum_out is not None:
            kw['accum_out'] = accum_out
        return self.op('act', lambda e: e.activation(out, in_, func, bias=bias, scale=scale, **kw),
                       [in_, bias, scale], [out, accum_out])

    def tt(self, out, a, b, op, eng='dve'):
        return self.op(eng, lambda e: e.tensor_tensor(out, a, b, op), [a, b], [out])

    def ts(self, out, a, s1, s2, op0, op1=None, eng='dve', accum_out=None):
        kw = {}
        if accum_out is not None:
            kw['accum_out'] = accum_out
        if op1 is None:
            return self.op(eng, lambda e: e.tensor_scalar(out, a, s1, None, op0, **kw), [a, s1], [out, accum_out])
        return self.op(eng, lambda e: e.tensor_scalar(out, a, s1, s2, op0, op1, **kw), [a, s1, s2],
                       [out, accum_out])

    def stt(self, out, a, s, b, op0, op1, eng='dve'):
        return self.op(eng, lambda e: e.scalar_tensor_tensor(out, a, s, b, op0, op1), [a, s, b], [out])

    def copy(self, out, a, eng='dve'):
        if eng == 'act':
            return self.op('act', lambda e: e.copy(out, a), [a], [out])
        return self.op(eng, lambda e: e.tensor_copy(out, a), [a], [out])

    def recip(self, out, a):
        return self.op('dve', lambda e: e.reciprocal(out, a), [a], [out])

    def memset(self, out, v, eng='dve'):
        return self.op(eng, lambda e: e.memset(out, v), [], [out])

    def reduce(self, out, a, op, axis=AX.X, eng='dve'):
        return self.op(eng, lambda e: e.tensor_reduce(out, a, axis, op), [a], [out])


class Rot:
    def __init__(self, tiles):
        self.t = tiles
        self.i = 0

    def get(self):
        t = self.t[self.i % len(self.t)]
        self.i += 1
        return t


def new_nc():
    return bass.Bass("TRN2", target_bir_lowering=False)


def dram_in(nc, name, shape, dt=F32):
    return nc.dram_tensor(name, list(shape), dt, kind="ExternalInput").ap()


def dram_out(nc, name, shape, dt=F32):
    return nc.dram_tensor(name, list(shape), dt, kind="ExternalOutput").ap()


def load_weight_bf16(P, w_sb, w_dram, K, N, stg, engs=('dve', 'pool')):
    wv = w_dram.rearrange("(kc p) n -> p kc n", p=128)
    i = 0
    for kc in range(K // 128):
        for n0 in range(0, N, 2048):
            n1 = min(N, n0 + 2048)
            s = stg.get()
            P.dma(s[:, 0:n1 - n0], wv[:, kc, n0:n1])
            P.copy(w_sb[:, kc, n0:n1], s[:, 0:n1 - n0], eng=engs[i % len(engs)])
            i += 1


def load_cols(P, dst, vec_dram, n):
    P.dma(dst[:, 0:n], vec_dram.rearrange("(c p) -> p c", p=128), allow_slow_non_contiguous=True)


def rmsnorm_T(P, xn, aT, g_sb, ones_bf, sq, ssq_ps, rstd, KC, n, dmodel):
    for kc in range(KC):
        P.act(sq[:, kc, 0:n], aT[:, kc, 0:n], AF.Square)
    for kc in range(KC):
        P.mm(ssq_ps[:, 0:n], ones_bf[:], sq[:, kc, 0:n], start=(kc == 0), stop=(kc == KC - 1))
    P.ts(rstd[:, 0:n], ssq_ps[:, 0:n], 1.0 / dmodel, EPS, ALU.mult, ALU.add)
    P.act(rstd[:, 0:n], rstd[:, 0:n], AF.Sqrt)
    P.recip(rstd[:, 0:n], rstd[:, 0:n])
    for kc in range(KC):
        P.stt(xn[:, kc, 0:n], aT[:, kc, 0:n], g_sb[:, kc:kc + 1], rstd[:, 0:n], ALU.mult, ALU.mult)


def build_linear(T, K, N, norm, mode, K2=0):
    nc = new_nc()
    aT = dram_in(nc, "aT", [K, T])
    W = dram_in(nc, "W", [K, N])
    g = dram_in(nc, "g", [K]) if norm else None
    resT = dram_in(nc, "resT", [N, T]) if mode == 'res' else None
    if mode == 'ple':
        W2 = dram_in(nc, "W2", [K2, N])
        a2T = dram_in(nc, "a2T", [K2, T])
    oT = dram_out(nc, "oT", [N, T])
    KC, NCH, n = K // 128, N // 128, 512
    with ExitStack() as st:
        P = Prog(nc, st)
        w_sb = P.sb("w_sb", [128, KC, N], BF16)
        stg = Rot([P.sb("wstg%d" % i, [128, 2048]) for i in range(2)])
        load_weight_bf16(P, w_sb, W, K, N, stg)
        if mode == 'ple':
            w2_sb = P.sb("w2_sb", [128, K2 // 128, N], BF16)
            load_weight_bf16(P, w2_sb, W2, K2, N, stg)
        ones_bf = P.sb("ones_bf", [128, 128], BF16)
        P.memset(ones_bf[:], 1.0)
        if norm:
            g_sb = P.sb("g_sb", [128, KC])
            load_cols(P, g_sb, g, KC)
        a_t = Rot([P.sb("a_t%d" % i, [128, KC, n]) for i in range(2)])
        xn_t = Rot([P.sb("xn_t%d" % i, [128, KC, n], BF16) for i in range(2)])
        sq = P.sb("sq", [128, KC, n], BF16)
        rstd = P.sb("rstd", [128, n])
        ssq_ps = P.ps("ssq_ps", [128, n])
        pss = Rot([P.ps("ps%d" % i, [128, n]) for i in range(4)])
        outs = Rot([P.sb("o%d" % i, [128, n]) for i in range(4)])
        if mode == 'res':
            res_t = Rot([P.sb("res%d" % i, [128, n]) for i in range(3)])
        if mode == 'ple':
            a2_t = Rot([P.sb("a2_t%d" % i, [128, K2 // 128, n]) for i in range(2)])
            a2b_t = Rot([P.sb("a2b_t%d" % i, [128, K2 // 128, n], BF16) for i in range(2)])
            ps2s = Rot([P.ps("ps2_%d" % i, [128, n]) for i in range(2)])
            sg_t = Rot([P.sb("sg%d" % i, [128, n]) for i in range(2)])
        aTv = aT.rearrange("(kc p) t -> p kc t", p=128)
        for t0 in range(0, T, n):
            a = a_t.get()
            P.dma(a[:], aTv[:, :, t0:t0 + n])
            xn = xn_t.get()
            if norm:
                rmsnorm_T(P, xn, a, g_sb, ones_bf, sq, ssq_ps, rstd, KC, n, K)
            else:
                for kc in range(KC):
                    P.copy(xn[:, kc, :], a[:, kc, :], eng=('dve' if kc % 2 == 0 else 'pool'))
            if mode == 'ple':
                a2 = a2_t.get()
                P.dma(a2[:], a2T.rearrange("(kc p) t -> p kc t", p=128)[:, :, t0:t0 + n])
                a2b = a2b_t.get()
                P.copy(a2b[:], a2[:], eng='pool')
            for c in range(NCH):
                ps = pss.get()
                for kc in range(KC):
                    P.mm(ps[:], w_sb[:, kc, c * 128:(c + 1) * 128], xn[:, kc, :], start=(kc == 0), stop=(kc == KC - 1))
                o = outs.get()
                if mode == 'plain':
                    if c % 2 == 0:
                        P.copy(o[:], ps[:], eng='dve')
                    else:
                        P.copy(o[:], ps[:], eng='act')
                elif mode == 'res':
                    r = res_t.get()
                    P.dma(r[:], resT[c * 128:(c + 1) * 128, t0:t0 + n])
                    P.tt(o[:], ps[:], r[:], ALU.add)
                else:
                    ps2 = ps2s.get()
                    for kc in range(K2 // 128):
                        P.mm(ps2[:], w2_sb[:, kc, c * 128:(c + 1) * 128], a2b[:, kc, :], start=(kc == 0),
                             stop=(kc == K2 // 128 - 1))
                    sg = sg_t.get()
                    P.act(sg[:], ps[:], AF.Sigmoid)
                    P.tt(sg[:], sg[:], ps2[:], ALU.mult)
                    P.tt(o[:], sg[:], a[:, c, :], ALU.add, eng='pool')
                P.dma(oT[c * 128:(c + 1) * 128, t0:t0 + n], o[:], q='act')
        P.finish([oT])
    return nc


DFF = 2816


def build_ffn(T):
    nc = new_nc()
    hTp = dram_in(nc, "hTp", [D, T + 2])
    g = dram_in(nc, "g", [D])
    Wu = dram_in(nc, "Wu", [D, 2 * DFF])
    cw = dram_in(nc, "cw", [3, 2 * DFF])
    cb = dram_in(nc, "cb", [2 * DFF])
    Wd = dram_in(nc, "Wd", [DFF, D])
    oT = dram_out(nc, "oT", [D, T])
    KC, n, FC = D // 128, 256, DFF // 128
    with ExitStack() as st:
        P = Prog(nc, st)
        wu_sb = P.sb("wu_sb", [128, KC, 2 * DFF], BF16)
        wd_sb = P.sb("wd_sb", [128, FC, D], BF16)
        stg = Rot([P.sb("wstg%d" % i, [128, 2048]) for i in range(2)])
        load_weight_bf16(P, wu_sb, Wu, D, 2 * DFF, stg)
        load_weight_bf16(P, wd_sb, Wd, DFF, D, stg)
        ones_bf = P.sb("ones_bf", [128, 128], BF16)
        P.memset(ones_bf[:], 1.0)
        g_sb = P.sb("g_sb", [128, KC])
        load_cols(P, g_sb, g, KC)
        cw_sb = P.sb("cw_sb", [128, 3, 2 * FC])
        for j in range(3):
            load_cols(P, cw_sb[:, j, :], cw[j], 2 * FC)
        cb_sb = P.sb("cb_sb", [128, 2 * FC])
        load_cols(P, cb_sb, cb, 2 * FC)
        a_t = Rot([P.sb("a_t%d" % i, [128, KC, n + 2]) for i in range(2)])
        xn = P.sb("xn", [128, KC, n + 2], BF16)
        sq = P.sb("sq", [128, KC, n + 2], BF16)
        rstd = P.sb("rstd", [128, n + 2])
        ssq_ps = P.ps("ssq_ps", [128, n + 2])
        pss = Rot([P.ps("ps%d" % i, [128, n + 2]) for i in range(4)])
        psd = Rot([P.ps("psd%d" % i, [128, n]) for i in range(2)])
        gT = P.sb("gT", [128, FC, n], BF16)
        ca_t = Rot([P.sb("ca%d" % i, [128, n]) for i in range(2)])
        cb_t = Rot([P.sb("cbv%d" % i, [128, n]) for i in range(2)])
        t1_t = Rot([P.sb("t1_%d" % i, [128, n]) for i in range(2)])
        t2_t = Rot([P.sb("t2_%d" % i, [128, n]) for i in range(2)])
        outs = Rot([P.sb("o%d" % i, [128, n]) for i in range(3)])
        hv = hTp.rearrange("(kc p) t -> p kc t", p=128)

        def conv(dst, ps, col):
            P.act(dst[:], ps[:, 1:n + 1], AF.Identity, bias=cb_sb[:, col:col + 1], scale=cw_sb[:, 1, col:col + 1])
            P.stt(dst[:], ps[:, 0:n], cw_sb[:, 0, col:col + 1], dst[:], ALU.mult, ALU.add)
            P.stt(dst[:], ps[:, 2:n + 2], cw_sb[:, 2, col:col + 1], dst[:], ALU.mult, ALU.add)

        for t0 in range(0, T, n):
            a = a_t.get()
            P.dma(a[:], hv[:, :, t0:t0 + n + 2])
            rmsnorm_T(P, xn, a, g_sb, ones_bf, sq, ssq_ps, rstd, KC, n + 2, D)
            for fc in range(FC):
                psa, psb = pss.get(), pss.get()
                for kc in range(KC):
                    P.mm(psa[:], wu_sb[:, kc, fc * 128:(fc + 1) * 128], xn[:, kc, :], start=(kc == 0),
                         stop=(kc == KC - 1))
                for kc in range(KC):
                    P.mm(psb[:], wu_sb[:, kc, DFF + fc * 128:DFF + (fc + 1) * 128], xn[:, kc, :], start=(kc == 0),
                         stop=(kc == KC - 1))
                ca, cbv, t1, t2 = ca_t.get(), cb_t.get(), t1_t.get(), t2_t.get()
                conv(ca, psa, fc)
                conv(cbv, psb, FC + fc)
                P.act(t2[:], ca[:], AF.Gelu_apprx_tanh)
                P.tt(gT[:, fc, :], t2[:], cbv[:], ALU.mult, eng='pool')
            for c in range(KC):
                ps = psd.get()
                for fc in range(FC):
                    P.mm(ps[:], wd_sb[:, fc, c * 128:(c + 1) * 128], gT[:, fc, :], start=(fc == 0), stop=(fc == FC - 1))
                o = outs.get()
                P.tt(o[:], ps[:], a[:, c, 1:n + 1], ALU.add)
                P.dma(oT[c * 128:(c + 1) * 128, t0:t0 + n], o[:], q='sp')
        P.finish([oT])
    return nc


def run(nc, in_maps, trace=False):
    res = run_bass_kernel_spmd(nc, in_maps, core_ids=list(range(NCORES)), trace=trace)
    if trace:
        print("exec_time_ns", res.exec_time_ns, flush=True)
    return res.results


GW = 64
NROWS = L // GW


def build_na(dbg_rows=NROWS, dbg_pv=True):
    nc = new_nc()
    qT = dram_in(nc, "qT", [512, L])
    kT = dram_in(nc, "kT", [512, L])
    v = dram_in(nc, "v", [L, 512])
    bias = dram_in(nc, "bias", [19 * 64, 512])
    qg = dram_in(nc, "qg", [64])
    kg = dram_in(nc, "kg", [64])
    o = dram_out(nc, "o", [L, 512])
    n = 512
    NT = L // n
    with ExitStack() as st:
        P = Prog(nc, st)
        bd = P.sb("bd", [128, 128], BF16)
        P.memset(bd[:], 0.0)
        P.memset(bd[0:64, 0:64], 1.0)
        P.memset(bd[64:128, 64:128], 1.0)
        g2 = P.sb("g2", [128, 2])
        for hh in range(2):
            P.dma(g2[hh * 64:(hh + 1) * 64, 0:1], qg.rearrange("(p o) -> p o", o=1))
            P.dma(g2[hh * 64:(hh + 1) * 64, 1:2], kg.rearrange("(p o) -> p o", o=1))
        P.ts(g2[:, 0:1], g2[:, 0:1], 0.125, None, ALU.mult)
        bias_sb = P.sb("bias_sb", [128, 16, 512])
        for dr0 in range(14):
            P.dma(bias_sb[:, dr0, :], bias[dr0 * 64:dr0 * 64 + 128, :])
        for x in range(2):
            P.dma(bias_sb[:, 14 + x, :], bias[(15 + 2 * x) * 64:(15 + 2 * x) * 64 + 128, :])
        kTn = P.sb("kTn", [128, 4, L], BF16)
        V1 = P.sb("V1", [128, L // 128, 8, 65], BF16)
        P.memset(V1[:, :, :, 64:65], 1.0, eng='pool')
        raw = Rot([P.sb("raw%d" % i, [128, 4, n]) for i in range(1)])
        sq = P.sb("sq", [128, 4, n], BF16)
        rstd = P.sb("rstd", [128, 4, n])
        ssq = Rot([P.ps("ssq%d" % i, [128, n]) for i in range(2)])
        qTn_t = Rot([P.sb("qz%d" % i, [128, 8, n], BF16) for i in range(2)])
        for t_ in qTn_t.t:
            P.memset(t_[:], 0.0, eng='pool')
        vst = raw

        def headnorm(dst, src_dram, t0, gcol, sep=False):
            r = raw.get()
            P.dma(r[:], src_dram.rearrange("(c p) t -> p c t", p=128)[:, :, t0:t0 + n])
            for c in range(4):
                P.act(sq[:, c, :], r[:, c, :], AF.Square)
            for c in range(4):
                ps = ssq.get()
                P.mm(ps[:], bd[:], sq[:, c, :])
                P.ts(rstd[:, c, :], ps[:], 1.0 / 64, EPS, ALU.mult, ALU.add)
            P.act(rstd[:], rstd[:], AF.Sqrt)
            P.recip(rstd[:], rstd[:])
            for c in range(4):
                if sep:
                    for hh in range(2):
                        pr = slice(hh * 64, hh * 64 + 64)
                        P.stt(dst[pr, 2 * c + hh, :], r[pr, c, :], g2[pr, gcol:gcol + 1], rstd[pr, c, :], ALU.mult,
                              ALU.mult)
                else:
                    P.stt(dst[:, c, :], r[:, c, :], g2[:, gcol:gcol + 1], rstd[:, c, :], ALU.mult, ALU.mult)

        def kvprep(j):
            headnorm(kTn[:, :, j * n:(j + 1) * n], kT, j * n, 1)
            s = vst.get()
            P.dma(s[:], v[j * n:(j + 1) * n, :].rearrange("(a p) c -> p a c", p=128))
            P.copy(V1[:, j * 4:(j + 1) * 4, :, 0:64], s[:].rearrange("p a (h d) -> p a h d", d=64), eng='pool')

        for j in range(min(2, NT)):
            kvprep(j)

        st_ps = Rot([P.ps("st%d" % i, [128, 512]) for i in range(2)])
        o_ps = Rot([P.ps("ops%d" % i, [64, 512])[:, 0:260].rearrange("p (h d) -> p h d", d=65) for i in range(4)])
        sb_t = Rot([P.sb("sbt%d" % i, [128, 512]) for i in range(2)])
        e_t = Rot([P.sb("et%d" % i, [128, 512], BF16) for i in range(2)])
        rec_t = Rot([P.sb("rec%d" % i, [64, 8]) for i in range(2)])
        ob_t = Rot([P.sb("ob%d" % i, [64, 512]) for i in range(2)])
        for j in range(NT):
            if j + 2 < NT:
                kvprep(j + 2)
            qTn = qTn_t.get()
            headnorm(qTn, qT, j * n, 0, sep=True)
            units = []
            for ii in range(8):
                i = j * 8 + ii
                if i >= dbg_rows:
                    break
                rs = min(max(i - 4, 0), NROWS - 8)
                dl = i - rs
                odd = rs % 2
                nkc = 5 if odd else 4
                for kc in range(nkc):
                    k0 = (rs - odd + 2 * kc) * GW
                    if not odd:
                        bt = 2 * kc - dl + 7
                    else:
                        assert dl == 4
                        bt = 14 if kc == 0 else (15 if kc == 4 else 2 * kc - 1 - dl + 7)
                    units.append((i, ii, kc, nkc, k0, bt))
            state = {}

            def emit_qk(u):
                i, ii, kc, nkc, k0, bt = u
                sp = st_ps.get()
                for c in range(4):
                    P.mm(sp[:, c * 128:(c + 1) * 128].rearrange("p (a q) -> p a q", a=2), kTn[:, c, k0:k0 + 128],
                         qTn[:, 2 * c:2 * c + 2, ii * 64:(ii + 1) * 64])
                sb_ = sb_t.get()
                P.tt(sb_[:], sp[:], bias_sb[:, bt, :], ALU.add)
                e = e_t.get()
                P.act(e[:], sb_[:], AF.Exp)
                state[u] = e

            def emit_pv(u):
                i, ii, kc, nkc, k0, bt = u
                if kc == 0:
                    state["o"] = (o_ps.get(), o_ps.get())
                oa, ob = state["o"]
                e = state.pop(u)
                for h in range(8):
                    op_ = oa if h < 4 else ob
                    P.mm(op_[:, h % 4, :], e[:, h * 64:(h + 1) * 64], V1[:, k0 // 128, h, :],
                         start=(kc == 0 and h % 4 == 0), stop=(kc == nkc - 1 and h % 4 == 3))
                if kc == nkc - 1:
                    rec = rec_t.get()
                    P.recip(rec[:, 0:4], oa[:, :, 64])
                    P.recip(rec[:, 4:8], ob[:, :, 64])
                    obuf = ob_t.get()
                    for h in range(8):
                        op_ = oa if h < 4 else ob
                        P.act(obuf[:, h * 64:(h + 1) * 64], op_[:, h % 4, 0:64], AF.Copy, scale=rec[:, h:h + 1])
                    P.dma(o[i * 64:(i + 1) * 64, :], obuf[:], q='act')

            if units:
                emit_qk(units[0])
            for ui, u in enumerate(units):
                if ui + 1 < len(units):
                    emit_qk(units[ui + 1])
                emit_pv(u)
        P.finish([o])
    return nc


def na_bias_table(rpb, hg):
    col = np.arange(GW)
    cstart = np.clip(col - 8, 0, GW - 16)
    cp = np.arange(GW)[:, None]
    cq = np.arange(GW)[None, :]
    valid = (cp >= cstart[None, :]) & (cp < cstart[None, :] + 16)
    dc = np.clip(cp - cq + 15, 0, 30)
    t = rpb[hg * 8:(hg + 1) * 8][:, :, dc]
    t = np.where(valid[None, None], t, np.float32(-30000.0)).astype(np.float32)
    t = t.transpose(1, 2, 0, 3).reshape(15, 64, 8 * 64)
    m = np.full((1, 64, 512), -30000.0, np.float32)
    return np.ascontiguousarray(np.concatenate([t, m, t[3:4], t[10:11], m], axis=0).reshape(19 * 64, 512))


HG = 64
NFFT = 2 * L


def hyena_consts():
    f64 = np.float64
    i64 = np.arange(64, dtype=f64)[:, None]
    i128 = np.arange(128, dtype=f64)
    a = 2 * np.pi * i64 * i128[None, :] / 128.0
    c = {}
    c["F1"] = np.concatenate([np.cos(a), -np.sin(a)], axis=1)
    t = 2 * np.pi * i128[:, None] * i128[None, :] / NFFT
    c["TW1"] = np.stack([np.cos(t), -np.sin(t)], axis=1)
    c["TW2"] = np.stack([np.cos(t), np.sin(t)], axis=1)
    b = 2 * np.pi * i128[:, None] * i128[None, :] / 128.0
    c["F3"] = np.stack([np.cos(b), -np.sin(b), np.sin(b)], axis=1)
    c["G1"] = np.concatenate([np.cos(b), np.sin(b)], axis=1)
    c["G2"] = np.concatenate([-np.sin(b), np.cos(b)], axis=1)
    a2 = 2 * np.pi * i128[:, None] * i64.T / 128.0
    c["FI"] = np.stack([np.cos(a2) / NFFT, -np.sin(a2) / NFFT], axis=1)
    f32 = np.float32
    tt_ = np.linspace(0.0, 1.0, L, dtype=f32)[:, None]
    bands = 8
    ang = (f32(2.0 * math.pi / L) * np.arange(L, dtype=f32)[:, None]) * np.linspace(1e-4, bands - 1, bands, dtype=f32)[None]
    feats = np.concatenate([tt_, np.cos(ang), -np.sin(ang)], axis=-1)
    c["featsT"] = np.ascontiguousarray(feats.T)
    c["tpos"] = tt_[:, 0]
    return {k: np.ascontiguousarray(v, dtype=np.float32) for k, v in c.items()}


def hyena_window(core):
    deltas = np.abs(np.linspace(DECAY_MIN_, DECAY_MAX_, 512, dtype=np.float32))[core * HG:(core + 1) * HG]
    t = np.linspace(0.0, 1.0, L, dtype=np.float32)
    w = np.exp(-t[:, None] * deltas[None, :])
    return np.ascontiguousarray(w.reshape(64, 128, HG).transpose(0, 2, 1))


DECAY_MIN_ = math.log(1e-2) / 1.5
DECAY_MAX_ = math.log(1e-2) / 0.3


def build_hyena():
    nc = new_nc()
    hz = dram_in(nc, "hz", [B, 3, 64, HG, 130])
    cw = dram_in(nc, "cw", [3 * 3 * HG])
    cbv = dram_in(nc, "cbv", [3 * HG])
    hb = dram_in(nc, "hb", [HG])
    w1 = dram_in(nc, "w1", [17, 64]); b1 = dram_in(nc, "b1", [64])
    w2 = dram_in(nc, "w2", [64, 64]); b2 = dram_in(nc, "b2", [64])
    w3 = dram_in(nc, "w3", [64, 64]); b3 = dram_in(nc, "b3", [64])
    w4 = dram_in(nc, "w4", [64, 2 * HG])
    fr = dram_in(nc, "fr", [64])
    win = dram_in(nc, "win", [64, HG, 128])
    cF1 = dram_in(nc, "F1", [64, 256]); cTW1 = dram_in(nc, "TW1", [128, 2, 128]); cTW2 = dram_in(nc, "TW2", [128, 2, 128])
    cF3 = dram_in(nc, "F3", [128, 3, 128]); cG1 = dram_in(nc, "G1", [128, 256]); cG2 = dram_in(nc, "G2", [128, 256])
    cFI = dram_in(nc, "FI", [128, 2, 64]); featsT = dram_in(nc, "featsT", [17, L])
    o = dram_out(nc, "o", [B, 64, HG, 128])
    TWO_PI = 2.0 * math.pi
    F32R = mybir.dt.float32r
    RR = lambda ap: ap.bitcast(F32R)
    with ExitStack() as st:
        P = Prog(nc, st)
        F1 = P.sb("F1s", [64, 256]); TW1 = P.sb("TW1s", [128, 2, 128]); TW2 = P.sb("TW2s", [128, 2, 128])
        F3 = P.sb("F3s", [128, 3, 128]); G1 = P.sb("G1s", [128, 256]); G2 = P.sb("G2s", [128, 256])
        FI = P.sb("FIs", [128, 2, 64])
        for t_, d_ in ((TW1, cTW1), (TW2, cTW2)):
            P.dma(t_[:], d_)
        cstg = P.sb("cstg", [128, 3, 128])
        for t_, d_ in ((F1, cF1), (F3, cF3), (G1, cG1), (G2, cG2), (FI, cFI)):
            sh = list(t_[:].shape)
            n_el = int(np.prod(sh[1:]))
            sv = cstg[0:sh[0], :, :].rearrange("p a b -> p (a b)")[:, 0:n_el]
            dv = d_ if len(sh) == 2 else d_.rearrange("p a b -> p (a b)")
            tv = t_[:] if len(sh) == 2 else t_[:].rearrange("p a b -> p (a b)")
            P.dma(sv, dv)
            P.copy(RR(tv), sv)
        HK = P.sb("HK", [128, 2, HG, 128])
        At = Rot([P.sb("At%d" % i, [128, 2, 4, 128]) for i in range(1)])
        Yt = Rot([P.sb("Yt%d" % i, [128, 2, 4, 128]) for i in range(1)])
        Bt = Rot([P.sb("Bt%d" % i, [128, 2, 4, 128]) for i in range(1)])
        tm = Rot([P.sb("tm%d" % i, [128, 4, 128]) for i in range(4)])
        ps1 = Rot([P.ps("ps1_%d" % i, [128, 2, 256]) for i in range(2)])
        psX = [P.ps("psXr", [128, 4, 128]), P.ps("psXi", [128, 4, 128])]
        psB = Rot([P.ps("psB%d" % i, [128, 2, 256]) for i in range(2)])
        psY = P.ps("psY", [64, 4, 128])
        pi_c = P.sb("pi_c", [128, 1])
        hs4 = P.sb("hs4", [64, 4, 128])
        P.memset(pi_c[:], -math.pi)

        def cmul(out_r, out_i, ar, ai, br, bi, ns):
            t1, t2, t3, t4 = [tm.get()[:, 0:ns, :] for _ in range(4)]
            P.tt(t1, ar, br, ALU.mult)
            P.tt(t2, ai, bi, ALU.mult)
            P.tt(RR(out_r), t1, t2, ALU.subtract, eng='pool')
            P.tt(t3, ar, bi, ALU.mult)
            P.tt(t4, ai, br, ALU.mult)
            P.tt(RR(out_i), t3, t4, ALU.add, eng='pool')

        def bc(tw, k, ns):
            return tw[:, k, :].unsqueeze(1).to_broadcast([128, ns, 128])

        def fwd4(sig):
            a = At.get()
            for pr in range(2):
                p1 = ps1.get()
                for i in range(2):
                    P.mm(p1[:, i, :], RR(sig[2 * pr + i]), RR(F1[:]))
                cmul(a[:, 0, 2 * pr:2 * pr + 2, :], a[:, 1, 2 * pr:2 * pr + 2, :], p1[:, :, 0:128], p1[:, :, 128:256],
                     bc(TW1, 0, 2), bc(TW1, 1, 2), 2)
            ar = a[:, 0, :, :].rearrange("p s k -> p (s k)")
            ai = a[:, 1, :, :].rearrange("p s k -> p (s k)")
            xr = psX[0][:].rearrange("p s k -> p (s k)")
            xi = psX[1][:].rearrange("p s k -> p (s k)")
            P.mm(xr, RR(F3[:, 0, :]), RR(ar), start=True, stop=False)
            P.mm(xr, RR(F3[:, 2, :]), RR(ai), start=False, stop=True)
            P.mm(xi, RR(F3[:, 0, :]), RR(ai), start=True, stop=False)
            P.mm(xi, RR(F3[:, 1, :]), RR(ar), start=False, stop=True)

        with ExitStack() as st2:
            def sb2(name, shape):
                return st2.enter_context(nc.sbuf_tensor(name, list(shape), F32))
            Hk = sb2("Hk", [64, 2 * HG, 128])
            h3 = sb2("h3", [64, L])
            w1s = sb2("w1s", [17, 64]); w2s = sb2("w2s", [64, 64]); w3s = sb2("w3s", [64, 64]); w4s = sb2("w4s", [64, 2 * HG])
            P.dma(w1s[:], w1); P.dma(w2s[:], w2); P.dma(w3s[:], w3); P.dma(w4s[:], w4)
            bs = sb2("bs", [64, 4])
            for i, v_ in enumerate((b1, b2, b3, fr)):
                P.dma(bs[:, i:i + 1], v_.rearrange("(p o) -> p o", o=1))
            targ = Rot([sb2("targ%d" % i, [64, 512]) for i in range(2)])
            hdt = Rot([sb2("hdt%d" % i, [64, 512]) for i in range(2)])
            fTt = Rot([sb2("fTt%d" % i, [17, 512]) for i in range(1)])
            winc = Rot([sb2("winc%d" % i, [64, 8, 128]) for i in range(1)])
            psm = [psB.t[0], psB.t[1]]
            for j in range(L // 512):
                src = fTt.get()
                P.dma(src[:], featsT[:, j * 512:(j + 1) * 512])
                for layer, wl in enumerate((w1s, w2s, w3s)):
                    pm = psm[layer % 2][0:64, :, :].rearrange("p a b -> p (a b)")
                    P.mm(pm, wl[:], src[:])
                    ta = targ.get()
                    P.ts(ta[:], pm, bs[:, layer:layer + 1], bs[:, 3:4], ALU.add, ALU.mult)
                    tk = targ.get()
                    P.ts(tk[:], ta[:], 1.0 / TWO_PI, 12582912.0, ALU.mult, ALU.add)
                    P.ts(tk[:], tk[:], 12582912.0, -TWO_PI, ALU.subtract, ALU.mult)
                    P.tt(ta[:], ta[:], tk[:], ALU.add)
                    P.ts(ta[:], ta[:], -3.1415925, 3.1415925, ALU.max, ALU.min)
                    dst = h3[:, j * 512:(j + 1) * 512] if layer == 2 else hdt.get()[:]
                    P.act(dst, ta[:], AF.Sin)
                    src = dst if layer == 2 else hdt.t[(hdt.i - 1) % 2]
            for q4 in range(32):
                p4 = psm[q4 % 2][0:64, :, :].rearrange("p a (b c) -> p (a b) c", c=128)
                for a_ in range(4):
                    n2 = q4 * 4 + a_
                    P.mm(p4[:, a_, :], h3[:, n2:L:128], w4s[:])
                P.copy(Hk[:, :, q4 * 4:q4 * 4 + 4], p4.rearrange("p a c -> p c a"), eng=('dve' if q4 % 2 == 0 else 'act'))
            for q in range(8):
                wc = winc.get()
                P.dma(wc[:], win[:, q * 8:(q + 1) * 8, :])
                for d_ in range(2):
                    hv = Hk[:, d_ * HG + q * 8:d_ * HG + (q + 1) * 8, :]
                    P.tt(hv, hv, wc[:], ALU.mult, eng=('dve' if d_ == 0 else 'pool'))
            r1 = sb2("r1", [64, 2 * HG]); r2 = sb2("r2", [64, HG]); sc = sb2("sc", [64, HG])
            ones64 = sb2("ones64", [64, 64])
            P.memset(ones64[:], 1.0)
            for q in range(32):
                sv = tm.t[q % 4][0:64, :, :]
                P.tt(sv, Hk[:, q * 4:(q + 1) * 4, :], Hk[:, q * 4:(q + 1) * 4, :], ALU.mult, eng='pool')
                P.reduce(r1[:, q * 4:(q + 1) * 4], sv, ALU.add)
            P.tt(r2[:], r1[:, 0:HG], r1[:, HG:2 * HG], ALU.add)
            pn = psY[:, 0, 0:HG]
            P.mm(pn, ones64[:], r2[:])
            P.ts(sc[:], pn, EPS, None, ALU.add)
            P.act(sc[:], sc[:], AF.Sqrt)
            P.recip(sc[:], sc[:])
            for d_ in range(2):
                for q in range(4):
                    hv = Hk[:, d_ * HG + q * 16:d_ * HG + (q + 1) * 16, :]
                    P.tt(hv, hv, sc[:, q * 16:(q + 1) * 16].unsqueeze(2).to_broadcast([64, 16, 128]), ALU.mult,
                         eng=('dve' if q % 2 == 0 else 'pool'))
            P.memset(Hk[0:1, HG:2 * HG, 0:1], 0.0)
            for d_ in range(2):
                for g in range(HG // 4):
                    P.copy(RR(hs4[:]), Hk[:, d_ * HG + g * 4:d_ * HG + g * 4 + 4, :], eng='act')
                    fwd4([hs4[:, i, :] for i in range(4)])
                    hr, hi = HK[:, 0, g * 4:g * 4 + 4, :], HK[:, 1, g * 4:g * 4 + 4, :]
                    if d_ == 0:
                        P.copy(hr, psX[0][:])
                        P.copy(hi, psX[1][:], eng='act')
                    else:
                        P.tt(hr, hr, psX[0][:], ALU.add)
                        P.tt(hi, hi, psX[1][:], ALU.subtract)
            P.barrier()
        zin = [P.sb("zin%d" % i, [64, 16, 130]) for i in range(3)]
        ut_sets = Rot([[P.sb("ut%d_%d" % (i, j), [64, 16, 128]) for i in range(3)] for j in range(2)])
        tcv = [P.sb("tcv%d" % i, [64, 16, 128]) for i in range(2)]
        og = Rot([P.sb("og%d" % i, [64, 16, 128]) for i in range(1)])
        sgl_t = Rot([P.sb("sgl%d" % j, [64, 16, 128]) for j in range(1)])
        cws = P.sb("cws", [64, 9 * HG]); cbs = P.sb("cbs", [64, 3 * HG]); hbs = P.sb("hbs", [64, HG])
        P.dma(cws[:], cw.partition_broadcast(64))
        P.dma(cbs[:], cbv.partition_broadcast(64))
        P.dma(hbs[:], hb.partition_broadcast(64))

        def chb(t_, off, c0):
            return t_[:, off + c0:off + c0 + 16].unsqueeze(2).to_broadcast([64, 16, 128])

        for b_ in range(B):
            for cg in range(HG // 16):
                c0 = cg * 16
                ut = ut_sets.get()
                sgl = sgl_t.get()
                for wh in range(3):
                    P.dma(zin[wh][:], hz[b_, wh, :, c0:c0 + 16, :])
                    eng = 'dve' if wh != 1 else 'pool'
                    u, t2 = ut[wh], tcv[0 if wh != 1 else 1]
                    P.tt(u[:], zin[wh][:, :, 0:128], chb(cws, (wh * 3 + 0) * HG, c0), ALU.mult, eng=eng)
                    P.tt(t2[:], zin[wh][:, :, 1:129], chb(cws, (wh * 3 + 1) * HG, c0), ALU.mult, eng=eng)
                    P.tt(u[:], u[:], t2[:], ALU.add, eng=eng)
                    P.tt(t2[:], zin[wh][:, :, 2:130], chb(cws, (wh * 3 + 2) * HG, c0), ALU.mult, eng=eng)
                    P.tt(u[:], u[:], t2[:], ALU.add, eng=eng)
                    P.tt(u[:], u[:], chb(cbs, wh * HG, c0), ALU.add, eng=eng)
                x0, s_, sb_ = ut[0], ut[2], ut[1]
                P.tt(RR(sgl[:]), ut[2][:], ut[1][:], ALU.mult)
                P.tt(s_[:], ut[2][:], ut[1][:], ALU.mult)
                P.tt(sb_[:], s_[:], chb(hbs, 0, c0), ALU.mult, eng='pool')
                ogt = og.get()
                for sg in range(4):
                    ch0 = c0 + sg * 4
                    fwd4([sgl[:, sg * 4 + i, :] for i in range(4)])
                    y = Yt.get()
                    cmul(y[:, 0, :, :], y[:, 1, :, :], psX[0][:], psX[1][:], HK[:, 0, ch0:ch0 + 4, :], HK[:, 1, ch0:ch0 + 4, :], 4)
                    bt = Bt.get()
                    for pr in range(2):
                        pb = psB.get()
                        for i in range(2):
                            P.mm(pb[:, i, :], RR(y[:, 0, 2 * pr + i, :]), RR(G1[:]), start=True, stop=False)
                            P.mm(pb[:, i, :], RR(y[:, 1, 2 * pr + i, :]), RR(G2[:]), start=False, stop=True)
                        cmul(bt[:, 0, 2 * pr:2 * pr + 2, :], bt[:, 1, 2 * pr:2 * pr + 2, :], pb[:, :, 0:128], pb[:, :, 128:256],
                             bc(TW2, 0, 2), bc(TW2, 1, 2), 2)
                    py = psY[:].rearrange("p s k -> p (s k)")
                    P.mm(py, RR(FI[:, 0, :]), RR(bt[:, 0, :, :].rearrange("p s k -> p (s k)")), start=True, stop=False)
                    P.mm(py, RR(FI[:, 1, :]), RR(bt[:, 1, :, :].rearrange("p s k -> p (s k)")), start=False, stop=True)
                    ov = ogt[:, sg * 4:sg * 4 + 4, :]
                    P.tt(ov, psY[:], sb_[:, sg * 4:sg * 4 + 4, :], ALU.add)
                    P.tt(ov, ov, x0[:, sg * 4:sg * 4 + 4, :], ALU.mult, eng='pool')
                P.dma(o[b_, :, c0:c0 + 16, :], ogt[:], q='act')
        P.finish([o])
    return nc


RS = 1664
DEC = math.exp(-0.5)


def rwkv_consts():
    s = np.arange(128)[:, None]
    t = np.arange(128)[None, :]
    mU, mSU, mSL, idn = (s <= t), (s < t), (s > t), (s == t)
    f = lambda m: m.astype(np.float32)
    return np.ascontiguousarray(np.stack([f(mU), f(mSU), f(mSL), f(idn), -f(mSL), f(mSL), -f(mSU), -f(mU)], axis=1))


def build_rwkv(Lc=L, dbg=9, use_r=True):
    nc = new_nc()
    zp = dram_in(nc, "zp", [Lc + 1, RS])
    mu = dram_in(nc, "mu", [RS])
    w0 = dram_in(nc, "w0", [512]); wup = dram_in(nc, "wup", [64, 512])
    a0 = dram_in(nc, "a0", [512]); aup = dram_in(nc, "aup", [64, 512])
    kkv = dram_in(nc, "kk", [512]); kav = dram_in(nc, "ka", [512]); rkv = dram_in(nc, "rk", [512])
    tri = dram_in(nc, "tri", [128, 8, 128])
    ys = dram_out(nc, "ys", [Lc, 512])
    bon = dram_out(nc, "bon", [Lc, 512])
    NCH = Lc // 128
    NS = 4
    F32R = mybir.dt.float32r
    RR = (lambda ap: ap.bitcast(F32R)) if use_r else (lambda ap: ap)
    with ExitStack() as st:
        P = Prog(nc, st)
        tr_ = P.sb("tri_s", [128, 8, 128])
        P.dma(tr_[:], tri)
        mU, mSU, mSL, ident = tr_[:, 0, :], tr_[:, 1, :], tr_[:, 2, :], tr_[:, 3, :]
        mask4 = tr_[:, 4:8, :]
        mu_s = P.sb("mu_s", [128, RS]); P.dma(mu_s[:], mu.partition_broadcast(128))
        kk_s = P.sb("kk_s", [128, 512]); P.dma(kk_s[:], kkv.partition_broadcast(128))
        ka_s = P.sb("ka_s", [128, 512]); P.dma(ka_s[:], kav.partition_broadcast(128))
        rk_s = P.sb("rk_s", [128, 512]); P.dma(rk_s[:], rkv.partition_broadcast(128))
        wup_s = P.sb("wup_s", [64, 512]); P.dma(wup_s[:], wup)
        aup_s = P.sb("aup_s", [64, 512]); P.dma(aup_s[:], aup)
        rows = P.sb("rows", [1, 2, 512])
        P.dma(rows[:, 0, :], w0.rearrange("(o n) -> o n", o=1))
        P.dma(rows[:, 1, :], a0.rearrange("(o n) -> o n", o=1))
        ones = P.sb("ones", [128, 128]); P.memset(ones[:], 1.0)
        ST = [P.sb("ST%d" % h, [64, 64]) for h in range(8)]
        zer = P.sb("zer", [64, 64]); P.memset(zer[:], 0.0)
        for h in range(8):
            P.copy(RR(ST[h][:]), zer[:], eng='pool')
        prev_t = Rot([P.sb("prev%d" % i, [128, RS]) for i in range(2)])
        cur_t = Rot([P.sb("cur%d" % i, [128, RS]) for i in range(2)])
        zd_t = Rot([P.sb("zd%d" % i, [128, RS]) for i in range(2)])
        Q4_t = Rot([P.sb("Q4_%d" % i, [128, 4, 512]) for i in range(2)])
        QT_t = Rot([P.sb("QT_%d" % i, [64, 4, 8, 128]) for i in range(2)])
        KH_t = Rot([P.sb("KH_%d" % i, [128, 2, 512]) for i in range(2)])
        gC_t = Rot([P.sb("gC_%d" % i, [64, 8]) for i in range(2)])
        KT_t = Rot([P.sb("KT_%d" % i, [128, 512]) for i in range(2)])
        wT = P.sb("wT", [64, 2, 128])
        zdt = P.sb("zdt", [128, RS])
        sg = P.sb("sg", [128, 512]); av = P.sb("av", [128, 512])
        Gt = P.sb("Gt", [128, 512]); Gp = P.sb("Gp", [128, 512]); Gi = P.sb("Gi", [128, 512]); Gr = P.sb("Gr", [128, 512])
        kap = P.sb("kap", [128, 512]); kmod = P.sb("kmod", [128, 512]); beta = P.sb("beta", [128, 512])
        T1 = P.sb("T1", [128, 512]); T2 = P.sb("T2", [128, 512])
        s8 = P.sb("s8", [128, 4, 8])
        ys_t = Rot([P.sb("ys_t%d" % i, [128, 512]) for i in range(2)])
        bn_t = Rot([P.sb("bn_t%d" % i, [128, 512]) for i in range(2)])
        SL = []
        for s_ in range(NS):
            d_ = {"SCr": P.sb("SCr%d" % s_, [128, 4, 128]), "SC": P.sb("SC%d" % s_, [128, 4, 128]),
                  "Nk": P.sb("Nk%d" % s_, [128, 128]), "W2": P.sb("W2_%d" % s_, [128, 128]),
                  "W1": P.sb("W1_%d" % s_, [64, 128]), "U": P.sb("U_%d" % s_, [128, 64])}
            d_["BT"] = Rot([P.sb("BT%d_%d" % (s_, i), [128, 128]) for i in range(2)])
            d_["BP"] = Rot([P.sb("BP%d_%d" % (s_, i), [128, 2, 128]) for i in range(2)])
            d_["raw"] = P.sb("raw%d" % s_, [128, 128])
            SL.append(d_)
        bkP = Rot([P.ps("bkP%d" % i, [128, 512]) for i in range(2)])
        bkA = [P.ps("bkA%d" % i, [128, 512]) for i in range(NS)]
        bkD = [P.ps("bkD%d" % i, [128, 512]) for i in range(NS // 2)]

        def v8(t_):
            return t_.rearrange("p (h j) -> p h j", j=64)

        def bc8(t_):
            return t_.unsqueeze(2).to_broadcast([128, 8, 64])

        def head_gen(h, s_, zd, Q4, QT, KH, gC, yst):
            hs = slice(h * 64, (h + 1) * 64)
            T_ = SL[s_]

            def mm(o_, l_, r_, **kw):
                return P.mm(o_, RR(l_), RR(r_), **kw)

            A_, D_ = bkA[s_], bkD[s_ // 2]
            c0 = (s_ % 2) * 256
            SCr, SC, Nk, W1, W2, U = T_["SCr"], T_["SC"], T_["Nk"], T_["W1"], T_["W2"], T_["U"]
            mm(A_[:, 0:256].rearrange("p (a t) -> p a t", a=2), QT[:, 0, h, :], QT[:, 2:4, h, :])
            mm(A_[:, 256:512].rearrange("p (a t) -> p a t", a=2), QT[:, 2, h, :], QT[:, 0:2, h, :])
            mm(D_[:, c0 + 128:c0 + 256], QT[:, 3, h, :], QT[:, 1, h, :])
            P.copy(SCr[:].rearrange("p a t -> p (a t)"), A_[:], eng='act')
            P.tt(RR(SC[:]), SCr[:], mask4, ALU.mult, eng='pool')
            P.tt(RR(Nk[:]), D_[:, c0 + 128:c0 + 256], mU, ALU.mult)
            AT, MkT, A, nNb = SC[:, 0, :], SC[:, 1, :], SC[:, 2, :], SC[:, 3, :]
            yield
            BP = T_["BP"].get()
            P.tt(RR(BP[:, 1, :]), A, ident, ALU.add, eng='pool')
            mm(A_[:, 0:128], AT, A)
            mm(D_[:, c0:c0 + 128], A, AT)
            BTm = T_["BT"].get()
            P.copy(RR(BP[:, 0, :]), A_[:, 0:128], eng='act')
            P.copy(RR(BTm[:]), D_[:, c0:c0 + 128])
            yield
            for kk_ in range(1, 7):
                last = (kk_ == 6)
                BPn = T_["BP"].get()
                raw = T_["raw"]
                if not last:
                    mm(A_[:, 0:256].rearrange("p (a t) -> p a t", a=2), BTm[:], BP[:])
                    mm(D_[:, c0:c0 + 128], BP[:, 0, :], BTm[:])
                    BTn = T_["BT"].get()
                    P.copy(RR(BPn[:, 0, :]), A_[:, 0:128], eng='act')
                    P.copy(raw[:], A_[:, 128:256], eng='act')
                    P.copy(RR(BTn[:]), D_[:, c0:c0 + 128])
                else:
                    mm(A_[:, 128:256], BTm[:], BP[:, 1, :])
                    P.copy(raw[:], A_[:, 128:256], eng='act')
                P.tt(RR(BPn[:, 1, :]), BP[:, 1, :], raw[:], ALU.add, eng='pool')
                BP = BPn
                if not last:
                    BTm = BTn
                yield
            Pm = BP[:, 1, :]
            mm(A_[0:64, 256:384], Q4[:, hs], Pm)
            mm(A_[:, 384:512], MkT, Pm)
            P.copy(RR(W1[:]), A_[0:64, 256:384], eng='act')
            P.copy(RR(W2[:]), A_[:, 384:512], eng='act')
            yield
            vh = zd[:, 1024 + h * 64:1024 + (h + 1) * 64]
            mm(D_[:, c0:c0 + 64], W2[:], vh, start=True, stop=False)
            mm(D_[:, c0:c0 + 64], W1[:], ST[h][:], start=False, stop=True)
            P.copy(RR(U[:]), D_[:, c0:c0 + 64])
            yield
            mm(D_[:, c0 + 64:c0 + 128], QT[:, 1, h, :], ST[h][:], start=True, stop=False)
            mm(D_[:, c0 + 64:c0 + 128], Nk[:], vh, start=False, stop=False)
            mm(D_[:, c0 + 64:c0 + 128], nNb, U[:], start=False, stop=True)
            mm(D_[0:64, c0 + 128:c0 + 192], KH[:, 0, hs], vh, start=True, stop=False)
            mm(D_[0:64, c0 + 128:c0 + 192], KH[:, 1, hs], U[:], start=False, stop=True)
            P.copy(yst[:, hs], D_[:, c0 + 64:c0 + 128])
            P.stt(RR(ST[h][:]), ST[h][:], gC[:, h:h + 1], D_[0:64, c0 + 128:c0 + 192], ALU.mult, ALU.add)

        ctxs = {}

        def prep_gen(ci):
            prev, cur, zd = prev_t.get(), cur_t.get(), zd_t.get()
            P.dma(prev[:], zp[ci * 128:ci * 128 + 128, :])
            P.dma(cur[:], zp[ci * 128 + 1:ci * 128 + 129, :])
            P.tt(zdt[:], prev[:], cur[:], ALU.subtract, eng='pool')
            P.tt(zdt[:], zdt[:], mu_s[:], ALU.mult, eng='pool')
            P.tt(RR(zd[:]), zdt[:], cur[:], ALU.add, eng='pool')
            yield
            r, k, v = zd[:, 0:512], zd[:, 512:1024], zd[:, 1024:1536]
            pb = bkP.get()
            pT = pb[0:64, 0:256].rearrange("p (a t) -> p a t", a=2)
            P.tr(pT[:, 0, :], zd[:, 1536:1600], ident)
            P.tr(pT[:, 1, :], zd[:, 1600:1664], ident)
            P.act(wT[:, 0, :], pT[:, 0, :], AF.Tanh)
            yield
            P.copy(wT[:, 1, :], pT[:, 1, :], eng='act')
            pb = bkP.get()
            P.mm(pb[:], wT[:, 0, :], wup_s[:], start=True, stop=False)
            P.mm(pb[:], ones[0:1, :], rows[:, 0, :], start=False, stop=True)
            P.act(sg[:], pb[:], AF.Sigmoid)
            yield
            pb = bkP.get()
            P.mm(pb[:], wT[:, 1, :], aup_s[:], start=True, stop=False)
            P.mm(pb[:], ones[0:1, :], rows[:, 1, :], start=False, stop=True)
            P.act(av[:], pb[:], AF.Sigmoid)
            yield
            pb = bkP.get()
            P.mm(pb[:], mU, sg[:])
            P.act(Gt[:], pb[:], AF.Exp, scale=-DEC)
            yield
            P.act(Gi[:], pb[:], AF.Exp, scale=DEC)
            yield
            pb = bkP.get()
            P.mm(pb[:], mSU, sg[:])
            P.act(Gp[:], pb[:], AF.Exp, scale=-DEC)
            yield
            pb = bkP.get()
            P.mm(pb[:], mSL, sg[:])
            P.act(Gr[:], pb[:], AF.Exp, scale=-DEC)
            yield
            pb = bkP.get()
            for h in range(8):
                P.mm(pb[0:64, h:h + 1], sg[:, h * 64:(h + 1) * 64], ones[:, 0:1])
            gC = gC_t.get()
            P.act(gC[:], pb[0:64, 0:8], AF.Exp, scale=-DEC)
            yield
            P.tt(T1[:], k, kk_s[:], ALU.mult)
            P.tt(T2[:], T1[:], T1[:], ALU.mult, eng='pool')
            P.reduce(s8[:, 0, :], v8(T2[:]), ALU.add)
            P.act(s8[:, 1, :], s8[:, 0, :], AF.Sqrt)
            yield
            P.ts(s8[:, 1, :], s8[:, 1, :], 1e-12, None, ALU.max)
            P.recip(s8[:, 1, :], s8[:, 1, :])
            P.tt(v8(kap[:]), v8(T1[:]), bc8(s8[:, 1, :]), ALU.mult)
            P.stt(T2[:], av[:], -1.0, ka_s[:], ALU.add, ALU.mult)
            P.stt(kmod[:], T2[:], 1.0, k, ALU.add, ALU.mult)
            P.tt(beta[:], kap[:], av[:], ALU.mult, eng='pool')
            P.tt(T1[:], r, kmod[:], ALU.mult, eng='pool')
            P.tt(T1[:], T1[:], rk_s[:], ALU.mult, eng='pool')
            P.reduce(s8[:, 2, :], v8(T1[:]), ALU.add)
            bn = bn_t.get()
            P.tt(v8(bn[:]), v8(v), bc8(s8[:, 2, :]), ALU.mult, eng='pool')
            P.dma(bon[ci * 128:(ci + 1) * 128, :], bn[:], q='act')
            yield
            Q4, KH = Q4_t.get(), KH_t.get()
            P.tt(Q4[:, 0, :], kap[:], Gp[:], ALU.mult)
            KT = KT_t.get()
            P.tt(RR(KT[:]), kap[:], Gp[:], ALU.mult)
            P.tt(Q4[:, 1, :], r, Gt[:], ALU.mult, eng='pool')
            P.tt(Q4[:, 2, :], beta[:], Gi[:], ALU.mult)
            P.tt(Q4[:, 3, :], kmod[:], Gi[:], ALU.mult, eng='pool')
            P.tt(RR(KH[:, 0, :]), kmod[:], Gr[:], ALU.mult, eng='pool')
            P.stt(RR(KH[:, 1, :]), beta[:], -1.0, Gr[:], ALU.mult, ALU.mult)
            yield
            QT = QT_t.get()
            for h in range(8):
                pb = bkP.get()
                pq = pb[0:64, :].rearrange("p (q t) -> p q t", q=4)
                for q in range(4):
                    P.tr(pq[:, q, :], Q4[:, q, h * 64:(h + 1) * 64], ident)
                P.copy(RR(QT[:, :, h, :]), pq, eng=('act' if h % 2 == 0 else 'dve'))
                yield
            ctxs[ci] = (zd, KT, QT, KH, gC)

        for _ in prep_gen(0):
            pass
        for ci in range(NCH):
            zd, Q4, QT, KH, gC = ctxs.pop(ci)
            yst = ys_t.get()
            todo = list(range(8))
            active = {}
            pg = prep_gen(ci + 1) if ci + 1 < NCH else None
            while todo or active or pg is not None:
                for s_ in range(NS):
                    if s_ not in active and todo:
                        active[s_] = head_gen(todo.pop(0), s_, zd, Q4, QT, KH, gC, yst)
                    if s_ in active:
                        try:
                            next(active[s_])
                        except StopIteration:
                            del active[s_]
                if pg is not None:
                    try:
                        next(pg)
                    except StopIteration:
                        pg = None
            P.dma(ys[ci * 128:(ci + 1) * 128, :], yst[:], q='act')
        P.finish([ys, bon])
    return nc


def build_rwkv_post(T):
    nc = new_nc()
    ysf = dram_in(nc, "ysf", [T, 512]); ysb = dram_in(nc, "ysb", [T, 512])
    bf = dram_in(nc, "bf", [T, 512]); bb = dram_in(nc, "bb", [T, 512])
    gdT = dram_in(nc, "gdT", [128, T])
    gup = dram_in(nc, "gup", [128, 512])
    lnw = dram_in(nc, "lnw", [512]); lnb = dram_in(nc, "lnb", [512])
    ya = dram_out(nc, "ya", [T, 512])
    with ExitStack() as st:
        P = Prog(nc, st)
        gup_s = P.sb("gup_s", [128, 512]); P.dma(gup_s[:], gup)
        lnw_s = P.sb("lnw_s", [128, 512]); P.dma(lnw_s[:], lnw.partition_broadcast(128))
        lnb_s = P.sb("lnb_s", [128, 512]); P.dma(lnb_s[:], lnb.partition_broadcast(128))
        gd_s = P.sb("gd_s", [128, T]); P.dma(gd_s[:], gdT)
        P.act(gd_s[:], gd_s[:], AF.Sigmoid)
        ins_t = [Rot([P.sb("in%d_%d" % (q, i), [128, 512]) for i in range(2)]) for q in range(4)]
        y_t = Rot([P.sb("y%d" % i, [128, 512]) for i in range(2)])
        sq = P.sb("sq", [128, 512])
        s8 = P.sb("s8", [128, 3, 8])
        o_t = Rot([P.sb("o%d" % i, [128, 512]) for i in range(2)])
        psg = Rot([P.ps("psg%d" % i, [128, 512]) for i in range(2)])

        def v8(t_):
            return t_.rearrange("p (h j) -> p h j", j=64)

        def bc8(t_):
            return t_.unsqueeze(2).to_broadcast([128, 8, 64])

        for t0 in range(0, T, 128):
            tl = [r_.get() for r_ in ins_t]
            for q, src in enumerate((ysf, ysb, bf, bb)):
                P.dma(tl[q][:], src[t0:t0 + 128, :])
            y = y_t.get()
            P.tt(y[:], tl[0][:], tl[1][:], ALU.add)
            P.reduce(s8[:, 0, :], v8(y[:]), ALU.add)
            P.ts(s8[:, 0, :], s8[:, 0, :], -1.0 / 64, None, ALU.mult)
            P.tt(v8(y[:]), v8(y[:]), bc8(s8[:, 0, :]), ALU.add)
            P.tt(sq[:], y[:], y[:], ALU.mult, eng='pool')
            P.reduce(s8[:, 1, :], v8(sq[:]), ALU.add)
            P.ts(s8[:, 1, :], s8[:, 1, :], 1.0 / 64, 64e-5, ALU.mult, ALU.add)
            P.act(s8[:, 1, :], s8[:, 1, :], AF.Sqrt)
            P.recip(s8[:, 1, :], s8[:, 1, :])
            P.tt(v8(y[:]), v8(y[:]), bc8(s8[:, 1, :]), ALU.mult)
            P.tt(y[:], y[:], lnw_s[:], ALU.mult, eng='pool')
            P.tt(y[:], y[:], lnb_s[:], ALU.add, eng='pool')
            P.tt(tl[2][:], tl[2][:], tl[3][:], ALU.add, eng='pool')
            P.tt(y[:], y[:], tl[2][:], ALU.add, eng='pool')
            pg = psg.get()
            P.mm(pg[:], gd_s[:, t0:t0 + 128], gup_s[:])
            o = o_t.get()
            P.tt(o[:], pg[:], y[:], ALU.mult)
            P.dma(ya[t0:t0 + 128, :], o[:], q='act')
        P.finish([ya])
    return nc


TSH = L // 2


def _shardT(a):
    return [np.ascontiguousarray(a[c // 2, (c % 2) * TSH:(c % 2 + 1) * TSH].T) for c in range(NCORES)]


def _unshardT(lst):
    out = np.empty((B, L, lst[0].shape[0]), np.float32)
    for c in range(NCORES):
        out[c // 2, (c % 2) * TSH:(c % 2 + 1) * TSH] = lst[c].T
    return out


def _shard_tok(a):
    return [np.ascontiguousarray(a[c // 2, (c % 2) * TSH:(c % 2 + 1) * TSH]) for c in range(NCORES)]


def _unshard_tok(lst):
    out = np.empty((B, L, lst[0].shape[1]), np.float32)
    for c in range(NCORES):
        out[c // 2, (c % 2) * TSH:(c % 2 + 1) * TSH] = lst[c]
    return out


_NC_CACHE = {}


def _nc(key, fn):
    if key not in _NC_CACHE:
        _NC_CACHE[key] = fn()
    return _NC_CACHE[key]


def linear_launch(aT_sh, W, g=None, mode='plain', resT_sh=None, W2=None, a2T_sh=None):
    K_, N_ = W.shape
    K2 = 0 if W2 is None else W2.shape[0]
    nc = _nc(("lin", K_, N_, g is not None, mode, K2), lambda: build_linear(TSH, K_, N_, g is not None, mode, K2))
    ins = []
    for c in range(NCORES):
        m = {"aT": aT_sh[c], "W": np.ascontiguousarray(W)}
        if g is not None:
            m["g"] = np.ascontiguousarray(g)
        if mode == 'res':
            m["resT"] = resT_sh[c]
        if mode == 'ple':
            m["W2"] = np.ascontiguousarray(W2)
            m["a2T"] = a2T_sh[c]
        ins.append(m)
    return [r["oT"] for r in run(nc, ins)]


def ffn_launch(hT_sh, g, Wu, cw, cb, Wd):
    nc = _nc(("ffn",), lambda: build_ffn(TSH))
    ins = []
    for c in range(NCORES):
        left = hT_sh[c - 1][:, -1:] if c % 2 == 1 else np.zeros((D, 1), np.float32)
        right = hT_sh[c + 1][:, :1] if c % 2 == 0 else np.zeros((D, 1), np.float32)
        ins.append({"hTp": np.ascontiguousarray(np.concatenate([left, hT_sh[c], right], axis=1)), "g": np.ascontiguousarray(g),
                    "Wu": np.ascontiguousarray(Wu), "cw": np.ascontiguousarray(cw), "cb": np.ascontiguousarray(cb),
                    "Wd": np.ascontiguousarray(Wd)})
    return [r["oT"] for r in run(nc, ins)]


def rwkv_launch(z, p):
    tri = rwkv_consts()
    ins = []
    for c in range(NCORES):
        b, dr = c // 2, c % 2
        zs = z[b, :, :RS]
        if dr == 1:
            zs = zs[::-1]
        zp = np.concatenate([np.zeros((1, RS), np.float32), zs], 0)
        ins.append({"zp": np.ascontiguousarray(zp), "mu": np.ascontiguousarray(p["rwkv_mu"][0, dr]),
                    "w0": np.ascontiguousarray(p["rwkv_w0"][0, dr]), "wup": np.ascontiguousarray(p["rwkv_w_up"][0, dr]),
                    "a0": np.ascontiguousarray(p["rwkv_a0"][0, dr]), "aup": np.ascontiguousarray(p["rwkv_a_up"][0, dr]),
                    "kk": np.ascontiguousarray(p["rwkv_k_k"][0]), "ka": np.ascontiguousarray(p["rwkv_k_a"][0]),
                    "rk": np.ascontiguousarray(p["rwkv_r_k"][0].reshape(-1)), "tri": tri})
    res = run(_nc(("rwkv",), lambda: build_rwkv(L)), ins)
    ysf = np.stack([res[2 * b]["ys"] for b in range(B)])
    ysb = np.stack([res[2 * b + 1]["ys"][::-1] for b in range(B)])
    bf = np.stack([res[2 * b]["bon"] for b in range(B)])
    bb = np.stack([res[2 * b + 1]["bon"][::-1] for b in range(B)])
    gdT = _shardT(z[:, :, RS:RS + 128])
    sh = [_shard_tok(a) for a in (ysf, ysb, bf, bb)]
    ins = [{"ysf": sh[0][c], "ysb": sh[1][c], "bf": sh[2][c], "bb": sh[3][c], "gdT": gdT[c],
            "gup": np.ascontiguousarray(p["rwkv_g_up"][0]), "lnw": np.ascontiguousarray(p["rwkv_ln_w"][0]),
            "lnb": np.ascontiguousarray(p["rwkv_ln_b"][0])} for c in range(NCORES)]
    res = run(_nc(("rwkvpost",), lambda: build_rwkv_post(TSH)), ins)
    return _unshard_tok([r["ya"] for r in res])


def hyena_launch(z, p):
    C = hyena_consts()
    zz = z[:, :, 1792:]
    zpad = np.pad(zz, ((0, 0), (1, 1), (0, 0)))
    idx = (np.arange(64)[:, None] * 128 + np.arange(130)[None, :])
    sw, sbias, w4 = p["hy_short_w"][0], p["hy_short_b"][0], p["hy_f_w4"][0]
    ins = []
    for c in range(NCORES):
        hz = np.empty((B, 3, 64, HG, 130), np.float32)
        for wh in range(3):
            cols = wh * 512 + c * HG + np.arange(HG)
            hz[:, wh] = zpad[:, :, cols][:, idx, :].transpose(0, 1, 3, 2)
        cw = np.stack([sw[:, wh * 512 + c * HG: wh * 512 + (c + 1) * HG] for wh in range(3)])
        cb = np.stack([sbias[wh * 512 + c * HG: wh * 512 + (c + 1) * HG] for wh in range(3)])
        w4c = np.concatenate([w4[:, dd * 512 + c * HG: dd * 512 + (c + 1) * HG] for dd in range(2)], axis=1)
        m = {"hz": hz, "cw": np.ascontiguousarray(cw.reshape(-1)), "cbv": np.ascontiguousarray(cb.reshape(-1)),
             "hb": np.ascontiguousarray(p["hy_bias"][0][c * HG:(c + 1) * HG]),
             "w1": np.ascontiguousarray(p["hy_f_w1"][0]), "b1": np.ascontiguousarray(p["hy_f_b1"][0]),
             "w2": np.ascontiguousarray(p["hy_f_w2"][0]), "b2": np.ascontiguousarray(p["hy_f_b2"][0]),
             "w3": np.ascontiguousarray(p["hy_f_w3"][0]), "b3": np.ascontiguousarray(p["hy_f_b3"][0]),
             "w4": np.ascontiguousarray(w4c), "fr": np.ascontiguousarray(p["hy_f_freq"][0]), "win": hyena_window(c)}
        for k_ in ("F1", "TW1", "TW2", "F3", "G1", "G2", "FI", "featsT"):
            m[k_] = C[k_]
        ins.append(m)
    res = run(_nc(("hyena",), build_hyena), ins)
    yb = np.empty((B, L, 512), np.float32)
    for c in range(NCORES):
        yb[:, :, c * HG:(c + 1) * HG] = res[c]["o"].transpose(0, 1, 3, 2).reshape(B, L, HG)
    return yb


def na_launch(z, p):
    ins = []
    for c in range(NCORES):
        b, hg = c // 2, c % 2
        ins.append({"qT": np.ascontiguousarray(z[b, :, hg * 512:(hg + 1) * 512].T),
                    "kT": np.ascontiguousarray(z[b, :, D + hg * 512:D + (hg + 1) * 512].T),
                    "v": np.ascontiguousarray(z[b, :, 2 * D + hg * 512:2 * D + (hg + 1) * 512]),
                    "bias": na_bias_table(p["na_rpb"][0], hg), "qg": np.ascontiguousarray(p["na_q_g"][0]),
                    "kg": np.ascontiguousarray(p["na_k_g"][0])})
    res = run(_nc(("na",), build_na), ins)
    att = np.empty((B, L, D), np.float32)
    for c in range(NCORES):
        att[c // 2, :, (c % 2) * 512:(c % 2 + 1) * 512] = res[c]["o"]
    return att


def kernel(**p):
    p = {k: np.asarray(v, dtype=np.float32) for k, v in p.items()}
    hT = _shardT(p["x"])
    zT = linear_launch(hT, p["mix_w_in"][0], g=p["mix_norm"][0])
    z = _unshardT(zT)
    ya = rwkv_launch(z, p)
    yb = hyena_launch(z, p)
    yT = _shardT(np.concatenate([ya, yb], axis=-1))
    hT = linear_launch(yT, p["mix_w_out"][0], mode='res', resT_sh=hT)
    hT = ffn_launch(hT, p["ffn_norm"][0], p["ffn_w_up"][0], p["ffn_conv_w"][0], p["ffn_conv_b"][0], p["ffn_w_down"][0])
    hT = linear_launch(hT, p["ple_w_gate"][0], g=p["ple_norm"][0], mode='ple', W2=p["ple_w_proj"][0],
                       a2T_sh=_shardT(p["p"][0]))
    zT = linear_launch(hT, p["na_w_qkv"][0], g=p["na_norm"][0])
    att = na_launch(_unshardT(zT), p)
    hT = linear_launch(_shardT(att), p["na_w_out"][0], mode='res', resT_sh=hT)
    hT = ffn_launch(hT, p["ffn_norm"][1], p["ffn_w_up"][1], p["ffn_conv_w"][1], p["ffn_conv_b"][1], p["ffn_w_down"][1])
    hT = linear_launch(hT, p["ple_w_gate"][1], g=p["ple_norm"][1], mode='ple', W2=p["ple_w_proj"][1],
                       a2T_sh=_shardT(p["p"][1]))
    return _unshardT(hT)
```
